# Optimizing a Trainium2 kernel written in Bass

```python
import jax, jax.numpy as jnp
from jax import lax
import numpy as np

D_MODEL = 2048
BATCH = 4
SEQ = 4096
DEPTH = 2

N_MEM = 256
EPS = 1e-6
N_EVEN = (DEPTH + 1) // 2
N_ODD = DEPTH // 2
MIX_A = D_MODEL // 2
POOL_WINDOWS = (2, 4, 8, 16)
N_POOL_GROUPS = len(POOL_WINDOWS)
POOL_GROUP = MIX_A // N_POOL_GROUPS
MIX_B = D_MODEL - MIX_A
HG_HEAD = 128
HG_HEADS = MIX_B // HG_HEAD
HG_CHUNK = 64
IN_EVEN = MIX_A + 4 * MIX_B
FOX_HEAD = 128
FOX_HEADS = D_MODEL // FOX_HEAD
FOX_BLOCK = 128
IN_ODD = 3 * D_MODEL + FOX_HEADS
XA_HEADS = 4
XA_HEAD = D_MODEL // XA_HEADS
D_FF = -(-8 * D_MODEL // (3 * 256)) * 256

kernel_name = "hybrid_pool_hgrn2_fox_trunk"


def rmsnorm(x, g):
    xf = x.astype(jnp.float32)
    y = xf * lax.rsqrt(jnp.mean(xf * xf, axis=-1, keepdims=True) + EPS)
    return (y * g.astype(jnp.float32)).astype(x.dtype)


def pool_mixer(u, w_pool, pool_scale):
    B, T, _ = u.shape
    uf = u.astype(jnp.float32)
    c = jnp.pad(jnp.cumsum(uf, axis=1), ((0, 0), (1, 0), (0, 0)))
    t = jnp.arange(T)
    outs = []
    for gi, w in enumerate(POOL_WINDOWS):
        cg = c[:, :, gi * POOL_GROUP:(gi + 1) * POOL_GROUP]
        c_lag = jnp.pad(cg, ((0, 0), (w - 1, 0), (0, 0)))[:, :T]
        cnt = jnp.minimum(t + 1, w).astype(jnp.float32)[None, :, None]
        mean = (cg[:, 1:] - c_lag) / cnt
        outs.append(mean - uf[:, :, gi * POOL_GROUP:(gi + 1) * POOL_GROUP])
    p = jnp.stack(outs, axis=2)
    y = jnp.einsum('btgc,gcd->btgd', p, w_pool.astype(jnp.float32)).reshape(B, T, MIX_A)
    return (y * pool_scale.astype(jnp.float32)).astype(u.dtype)


def hgrn2_mixer(q, fl, i, g, lb, norm_g):
    B, T, _ = q.shape
    H, Dh, C = HG_HEADS, HG_HEAD, HG_CHUNK
    N = T // C
    f = lb + (1.0 - lb) * jax.nn.sigmoid(fl.astype(jnp.float32))
    logf = jnp.log(f)
    k = 1.0 - f
    qf = jax.nn.silu(q.astype(jnp.float32)) * (Dh ** -0.5)

    def to_chunks(a):
        return a.reshape(B, N, C, H, Dh).transpose(1, 0, 3, 2, 4)

    qc, kc, vc = to_chunks(qf), to_chunks(k), to_chunks(i.astype(jnp.float32))
    bc = jnp.cumsum(to_chunks(logf), axis=3)
    causal = jnp.tril(jnp.ones((C, C), dtype=bool))[:, :, None]

    def step(S, inp):
        qh, kh, vh, bh = inp
        diff = bh[:, :, :, None, :] - bh[:, :, None, :, :]
        decay = jnp.exp(jnp.where(causal, diff, -jnp.inf))
        A = jnp.einsum('bhtk,bhsk,bhtsk->bhts', qh, kh, decay)
        o = (jnp.einsum('bhts,bhsv->bhtv', A, vh)
             + jnp.einsum('bhtk,bhkv->bhtv', qh * jnp.exp(bh), S))
        b_last = bh[:, :, -1:, :]
        S = (jnp.exp(b_last[:, :, 0, :])[..., None] * S
             + jnp.einsum('bhsk,bhsv->bhkv', kh * jnp.exp(b_last - bh), vh))
        return S, o

    S0 = jnp.zeros((B, H, Dh, Dh), jnp.float32)
    _, o = lax.scan(step, S0, (qc, kc, vc, bc))
    o = o.transpose(1, 0, 3, 2, 4).reshape(B, T, H, Dh)
    o = rmsnorm(o, norm_g).reshape(B, T, MIX_B)
    return (o * jax.nn.silu(g.astype(jnp.float32))).astype(q.dtype)


def fox_attention(q, k, v, fl):
    B, T, H, Dh = q.shape
    Fc = jnp.cumsum(jax.nn.log_sigmoid(fl.astype(jnp.float32)), axis=1).transpose(0, 2, 1)
    scale = Dh ** -0.5
    outs = []
    for blk in range(T // FOX_BLOCK):
        q0, q1 = blk * FOX_BLOCK, (blk + 1) * FOX_BLOCK
        s = jnp.einsum('bqhd,bkhd->bhqk', q[:, q0:q1], k[:, :q1]).astype(jnp.float32) * scale
        s = s + (Fc[:, :, q0:q1, None] - Fc[:, :, None, :q1])
        mask = (q0 + jnp.arange(FOX_BLOCK))[:, None] >= jnp.arange(q1)[None, :]
        p = jax.nn.softmax(jnp.where(mask, s, -jnp.inf), axis=-1)
        outs.append(jnp.einsum('bhqk,bkhd->bqhd', p.astype(v.dtype), v[:, :q1]))
    return jnp.concatenate(outs, axis=1).reshape(B, T, H * Dh)


def cross_attention(h, mem_n, wq, wkv, wo):
    B, T, _ = h.shape
    M = mem_n.shape[1]
    q = (h @ wq).reshape(B, T, XA_HEADS, XA_HEAD)
    kv = mem_n @ wkv
    k = kv[..., :D_MODEL].reshape(B, M, XA_HEADS, XA_HEAD)
    v = kv[..., D_MODEL:].reshape(B, M, XA_HEADS, XA_HEAD)
    s = jnp.einsum('bthd,bmhd->bhtm', q, k).astype(jnp.float32) * (XA_HEAD ** -0.5)
    p = jax.nn.softmax(s, axis=-1)
    o = jnp.einsum('bhtm,bmhd->bthd', p.astype(v.dtype), v).reshape(B, T, D_MODEL)
    return o @ wo


def setup_inputs(seed: int = 0) -> dict:
    key = jax.random.key(seed)
    ks = jax.random.split(key, 24)
    D = D_MODEL

    def nrm(k, shape, s):
        return jax.random.normal(k, shape, jnp.float32) * s

    def gain(k, shape):
        return 1.0 + 0.05 * jax.random.normal(k, shape, jnp.float32)

    return {
        "x": nrm(ks[0], (BATCH, SEQ, D), 1.0),
        "mem": nrm(ks[1], (BATCH, N_MEM, D), 1.0),
        "lb_table": nrm(ks[2], (DEPTH + 1, MIX_B), 0.5),
        "ev_norm": gain(ks[3], (N_EVEN, D)),
        "ev_w_in": nrm(ks[4], (N_EVEN, D, IN_EVEN), D ** -0.5),
        "ev_w_pool": nrm(ks[5], (N_EVEN, N_POOL_GROUPS, POOL_GROUP, POOL_GROUP), POOL_GROUP ** -0.5),
        "ev_pool_scale": gain(ks[6], (N_EVEN, MIX_A)),
        "ev_hg_norm": gain(ks[7], (N_EVEN, HG_HEAD)),
        "ev_w_out": nrm(ks[8], (N_EVEN, D, D), D ** -0.5),
        "od_norm": gain(ks[9], (N_ODD, D)),
        "od_w_in": nrm(ks[10], (N_ODD, D, IN_ODD), D ** -0.5),
        "od_b_f": 2.0 + nrm(ks[11], (N_ODD, FOX_HEADS), 0.1),
        "od_w_out": nrm(ks[12], (N_ODD, D, D), D ** -0.5),
        "xa_norm": gain(ks[13], (DEPTH, D)),
        "xa_mem_norm": gain(ks[14], (DEPTH, D)),
        "xa_wq": nrm(ks[15], (DEPTH, D, D), D ** -0.5),
        "xa_wkv": nrm(ks[16], (DEPTH, D, 2 * D), D ** -0.5),
        "xa_wo": nrm(ks[17], (DEPTH, D, D), D ** -0.5),
        "ffn_norm": gain(ks[18], (DEPTH, D)),
        "ffn_w_gate": nrm(ks[19], (DEPTH, D, D_FF), D ** -0.5),
        "ffn_w_up": nrm(ks[20], (DEPTH, D, D_FF), D ** -0.5),
        "ffn_w_down": nrm(ks[21], (DEPTH, D_FF, D), D_FF ** -0.5),
        "final_norm": gain(ks[22], (D,)),
    }


def reference(x, mem, lb_table, ev_norm, ev_w_in, ev_w_pool, ev_pool_scale, ev_hg_norm, ev_w_out,
              od_norm, od_w_in, od_b_f, od_w_out, xa_norm, xa_mem_norm, xa_wq, xa_wkv, xa_wo,
              ffn_norm, ffn_w_gate, ffn_w_up, ffn_w_down, final_norm):
    B, T, D = x.shape
    lb_cum = jnp.cumsum(jax.nn.softmax(lb_table.astype(jnp.float32), axis=0), axis=0)
    for l in range(DEPTH):
        if l % 2 == 0:
            e = l // 2
            h = rmsnorm(x, ev_norm[e])
            z = h @ ev_w_in[e]
            u = z[..., :MIX_A]
            q, fl, i, g = jnp.split(z[..., MIX_A:], 4, axis=-1)
            ya = pool_mixer(u, ev_w_pool[e], ev_pool_scale[e])
            yb = hgrn2_mixer(q, fl, i, g, lb_cum[l + 1] - lb_cum[0], ev_hg_norm[e])
            x = x + jnp.concatenate([ya, yb], axis=-1) @ ev_w_out[e]
        else:
            o = l // 2
            h = rmsnorm(x, od_norm[o])
            z = h @ od_w_in[o]
            q = z[..., :D].reshape(B, T, FOX_HEADS, FOX_HEAD)
            k = z[..., D:2 * D].reshape(B, T, FOX_HEADS, FOX_HEAD)
            v = z[..., 2 * D:3 * D].reshape(B, T, FOX_HEADS, FOX_HEAD)
            fl = z[..., 3 * D:] + od_b_f[o]
            x = x + fox_attention(q, k, v, fl) @ od_w_out[o]
        h = rmsnorm(x, xa_norm[l])
        x = x + cross_attention(h, rmsnorm(mem, xa_mem_norm[l]), xa_wq[l], xa_wkv[l], xa_wo[l])
        h = rmsnorm(x, ffn_norm[l])
        x = x + (jax.nn.silu(h @ ffn_w_gate[l]) * (h @ ffn_w_up[l])) @ ffn_w_down[l]
    return rmsnorm(x, final_norm)
```

```python
import numpy as np
from contextlib import ExitStack
import concourse.bass as bass
import concourse.mybir as mybir
from concourse.bass_utils import run_bass_kernel_spmd

F32 = mybir.dt.float32
BF16 = mybir.dt.bfloat16
AF = mybir.ActivationFunctionType
ALU = mybir.AluOpType

D = 2048
KC = 16
TOK = 2048
NB = 4
TB = 512
DFF = 5632
FC = DFF // 128
NMEM = 256
EPS = 1e-6
NCORES = 8

ENG = ("pe", "act", "dve", "pool", "sp")


class Res:
    __slots__ = ("w", "r")

    def __init__(self):
        self.w = {}
        self.r = {}


def RL(n):
    return [Res() for _ in range(n)]


class Prog:
    def __init__(self, nc):
        self.nc = nc
        self.streams = {e: [] for e in ENG}
        self.cnt = {e: 0 for e in ENG}
        self.seen = {e: {} for e in ENG}
        self.nslots = {"sp": 8, "pool": 8}
        self.slotnext = {"sp": 0, "pool": 0}
        self.slotval = {}
        for q, n in self.nslots.items():
            for i in range(n):
                self.slotval[f"{q}{i}"] = 0
        self.scope = "init"
        self.ccval = 0
        self.ccnames = [f"cc{i}" for i in range(20)]
        self.semnames = list(ENG) + list(self.slotval) + self.ccnames
        self.sems = {}

    def _wait(self, eng, ev):
        s, v = ev
        if v <= 0 or self.seen[eng].get(s, 0) >= v or (eng == "pe" and s == "pe"):
            return
        self.seen[eng][s] = v
        self.streams[eng].append(("w", s, v, self.scope))

    def _deps(self, eng, reads, writes, nowaw):
        for r in reads:
            for ev in r.w.items():
                self._wait(eng, ev)
        for w in writes:
            if not nowaw:
                for ev in w.w.items():
                    self._wait(eng, ev)
            for ev in w.r.items():
                self._wait(eng, ev)

    @staticmethod
    def _commit(ev, reads, writes):
        for r in reads:
            r.r[ev[0]] = ev[1]
        for w in writes:
            w.w[ev[0]] = ev[1]

    def op(self, eng, fn, reads=(), writes=(), nowaw=False):
        self._deps(eng, reads, writes, nowaw)
        self.cnt[eng] += 1
        ev = (eng, self.cnt[eng])
        self.streams[eng].append(("i", fn, eng, 1, self.scope))
        self._commit(ev, reads, writes)

    def dma(self, q, out, in_, reads=(), writes=(), nowaw=False, **kw):
        i = self.slotnext[q]
        self.slotnext[q] = (i + 1) % self.nslots[q]
        s = f"{q}{i}"
        self._wait(q, (s, self.slotval[s]))
        self._deps(q, reads, writes, nowaw)
        self.slotval[s] += 16
        ev = (s, self.slotval[s])
        self.streams[q].append(("i", lambda E: E.dma_start(out=out, in_=in_, **kw), s, 16, self.scope))
        self._commit(ev, reads, writes)

    def coll(self, ins, outs, groups, reads=(), writes=()):
        q = "pool"
        self._deps(q, reads, writes, False)
        name = self.ccnames[self.ccval]
        self.ccval += 1
        ev = (name, 1)
        self.streams[q].append(("c", lambda E: E.collective_compute(
            "AllGather", ALU.bypass, replica_groups=groups, ins=[ins], outs=[outs]), name, None, self.scope))
        self._commit(ev, reads, writes)

    def barrier_compute(self):
        cur = [(e, self.cnt[e]) for e in ENG] + list(self.slotval.items())
        for e in ENG:
            for ev in cur:
                self._wait(e, ev)

    def barrier(self):
        cur = [(e, self.cnt[e]) for e in ENG] + list(self.slotval.items()) + [(n, 1) for n in self.ccnames[:self.ccval]]
        for e in ENG:
            for ev in cur:
                self._wait(e, ev)

    def emit(self, block):
        self.barrier()
        sems = self.sems

        def run(E, name):
            items = self.streams[name]
            i = 0
            while i < len(items):
                lab = items[i][-1]
                j = i
                while j < len(items) and items[j][-1] == lab:
                    j += 1
                with self.nc.named_scope(lab):
                    for it in items[i:j]:
                        if it[0] == "w":
                            E.wait_ge(sems[it[1]], it[2])
                        elif it[0] == "c":
                            it[1](E).then_inc(sems[it[2]])
                        else:
                            it[1](E).then_inc(sems[it[2]], it[3])
                i = j

        @block.tensor
        def _(E):
            run(E, "pe")

        @block.scalar
        def _(E):
            run(E, "act")

        @block.vector
        def _(E):
            run(E, "dve")

        @block.gpsimd
        def _(E):
            run(E, "pool")

        @block.sync
        def _(E):
            run(E, "sp")


def panelize(W, pc):
    K, N = W.shape
    kc = K // 128
    npan = N // pc
    return np.ascontiguousarray(
        W.reshape(kc, 128, npan, pc).transpose(2, 1, 0, 3).reshape(npan * 128, kc * pc))


def colvec(v):
    return np.ascontiguousarray(v.reshape(-1, 128).T)


PCOL = {}
_off = 0
for _nm, _n in (("ev_norm", 16), ("xa_norm0", 16), ("xa_norm1", 16), ("ffn_norm0", 16), ("ffn_norm1", 16),
                ("od_norm", 16), ("final_norm", 16), ("mem_norm0", 16), ("mem_norm1", 16),
                ("pool_scale", 8), ("hg_norm", 1), ("lb0", 8), ("lb1", 8), ("lb2", 8),
                ("b_f", 16), ("tok0", 1), ("prevmask", 1), ("iota16", 16)):
    PCOL[_nm] = (_off, _n)
    _off += _n
NPAR = _off


class Builder:
    def __init__(self, stage_lo, stage_hi, dbg=False):
        self.stage_lo = stage_lo
        self.stage_hi = stage_hi
        self.dbg = dbg
        self.nc = bass.Bass("TRN2", target_bir_lowering=False)
        self.P = Prog(self.nc)
        self.dram = {}

    def din(self, name, shape, dt=F32):
        t = self.nc.dram_tensor(name, list(shape), dt, kind="ExternalInput").ap()
        self.dram[name] = t
        return t

    def dout(self, name, shape, dt=F32):
        t = self.nc.dram_tensor(name, list(shape), dt, kind="ExternalOutput").ap()
        self.dram[name] = t
        return t

    def dint(self, name, shape, dt=F32):
        t = self.nc.dram_tensor(name, list(shape), dt, kind="Internal").ap()
        self.dram[name] = t
        return t

    def sb(self, es, name, shape, dt):
        self._uid = getattr(self, "_uid", 0) + 1
        return es.enter_context(self.nc.sbuf_tensor(f"sb{self._uid}_{name}", list(shape), dt))

    def par(self, name, j=0, n=1):
        o, _ = PCOL[name]
        return self.params[:, o + j:o + j + n]

    def load_panel(self, wbuf_ap, wres, src, pi, width):
        self.P.dma("pool", wbuf_ap, src[pi * 128:(pi + 1) * 128, 0:width], reads=(), writes=(wres,),
                   max_dma_last_dim=4096)

    def norm_block(self, xs, xs_res, gname, h_out, h_res, n, bank, bank_res):
        P = self.P
        for kc in range(KC):
            sq = self.sq[kc % 4]
            sqr = self.sq_res[kc % 4]
            P.op("act", lambda E, kc=kc, sq=sq: E.activation(out=sq[:, 0:n], in_=xs(kc), func=AF.Square),
                 reads=(xs_res,), writes=(sqr,))
            P.op("pe", lambda E, kc=kc, sq=sq: E.matmul(bank[:, 0:n], lhsT=self.ones_bf[:, :], rhs=sq[:, 0:n],
                                                        start=(kc == 0), stop=(kc == KC - 1)),
                 reads=(sqr,), writes=(bank_res,))
        P.op("act", lambda E: E.activation(out=self.rstd[:, 0:n], in_=bank[:, 0:n], func=AF.Sqrt,
                                           bias=self.eps_col[:, 0:1], scale=1.0 / D),
             reads=(bank_res,), writes=(self.rstd_res,))
        P.op("dve", lambda E: E.reciprocal(out=self.rstd[:, 0:n], in_=self.rstd[:, 0:n]),
             reads=(self.rstd_res,), writes=(self.rstd_res,))
        for kc in range(KC):
            P.op("dve", lambda E, kc=kc: E.scalar_tensor_tensor(
                out=h_out(kc), in0=xs(kc), scalar=self.par(gname, kc), in1=self.rstd[:, 0:n],
                op0=ALU.mult, op1=ALU.mult),
                reads=(xs_res, self.rstd_res), writes=(h_res,))

    def load_xt_block(self, stage, stage_res, t0, n, xt_res_list):
        XT = self.dram["XT"]
        for kc in range(KC):
            self.P.dma("sp", stage[:, kc, 0:n], XT[kc * 128:(kc + 1) * 128, t0:t0 + n],
                       reads=(xt_res_list[kc],), writes=(stage_res,), nowaw=(kc > 0))

    def proj_residual(self, es, wsrc, kchunks, act, act_res_fn, tag):
        P = self.P
        XT = self.dram["XT"]
        width = kchunks * 128
        wb = [self.sb(es, f"{tag}_w{i}", [128, width], BF16) for i in range(2)]
        wr = RL(2)
        NXB, LA = 6, 3
        xb = [self.sb(es, f"{tag}_x{i}", [128, TB], F32) for i in range(NXB)]
        xr = RL(NXB)
        iters = [(j, tb) for j in range(KC) for tb in range(NB)]

        def load(it):
            j, tb = iters[it]
            P.dma("sp", xb[it % NXB][:, :], XT[j * 128:(j + 1) * 128, tb * TB:(tb + 1) * TB],
                  reads=(self.xt_res[j][tb],), writes=(xr[it % NXB],))
        for it in range(min(LA, len(iters))):
            load(it)
        for it, (j, tb) in enumerate(iters):
            if tb == 0:
                self.load_panel(wb[j % 2][:, :], wr[j % 2], wsrc, j, width)
            bank = self.banks[it % 4]
            br = self.bank_res[it % 4]
            x = xb[it % NXB]
            xres = xr[it % NXB]

            def mm(E, j=j, tb=tb, bank=bank):
                ins = None
                for kc in range(kchunks):
                    ins = E.matmul(bank[:, :], lhsT=wb[j % 2][:, kc * 128:(kc + 1) * 128], rhs=act(kc, tb),
                                   start=(kc == 0), stop=(kc == kchunks - 1))
                return ins
            P.op("pe", mm, reads=[wr[j % 2]] + [act_res_fn(kc, tb) for kc in range(kchunks)], writes=(br,))
            P.op("dve", lambda E, x=x, bank=bank: E.tensor_tensor(out=x[:, :], in0=bank[:, :], in1=x[:, :],
                                                                  op=ALU.add),
                 reads=(br, xres), writes=(xres,))
            if it + LA < len(iters):
                load(it + LA)
            P.dma("sp", XT[j * 128:(j + 1) * 128, tb * TB:(tb + 1) * TB], x[:, :],
                  reads=(xres,), writes=(self.xt_res[j][tb],))

    def norm_all(self, es, gname, hT, h_res_fn):
        stage = [self.sb(es, f"nst{i}", [128, KC, TB], F32) for i in range(2)]
        sres = RL(2)
        for tb in range(NB):
            st = stage[tb % 2]
            self.load_xt_block(st, sres[tb % 2], tb * TB, TB, [self.xt_res[kc][tb] for kc in range(KC)])
            self.norm_block(lambda kc, st=st: st[:, kc, :], sres[tb % 2], gname,
                            lambda kc, tb=tb: hT[:, kc, tb * TB:(tb + 1) * TB], h_res_fn(tb), TB,
                            self.banks[4 + tb % 2], self.bank_res[4 + tb % 2])

    def transpose_in(self, es, x, gname=None, hT=None, hres=None, store_xt=True):
        P = self.P
        XT = self.dram["XT"]
        xin = [self.sb(es, f"xin{i}", [128, D], F32) for i in range(2)]
        xin_r = RL(2)
        st = [self.sb(es, f"xst{i}", [128, KC, TB], F32) for i in range(2)]
        st_r = RL(2)
        for tb in range(NB):
            s = st[tb % 2]
            for tt in range(4):
                t = tb * 4 + tt
                xi = xin[t % 2]
                P.dma("sp", xi[:, :], x[t * 128:(t + 1) * 128, :], reads=(), writes=(xin_r[t % 2],))
                for q in range(4):
                    bank = self.banks[q]

                    def tr(E, xi=xi, q=q, bank=bank):
                        ins = None
                        for i in range(4):
                            kc = q * 4 + i
                            ins = E.transpose(out=bank[:, i * 128:(i + 1) * 128],
                                              in_=xi[:, kc * 128:(kc + 1) * 128], identity=self.ident_f[:, :])
                        return ins
                    P.op("pe", tr, reads=(xin_r[t % 2],), writes=(self.bank_res[q],))
                    if q % 2 == 0:
                        P.op("dve", lambda E, s=s, q=q, tt=tt, bank=bank: E.tensor_copy(
                            out=s[:, q * 4:(q + 1) * 4, tt * 128:(tt + 1) * 128],
                            in_=bank[:, :].rearrange("p (a b) -> p a b", a=4)),
                            reads=(self.bank_res[q],), writes=(st_r[tb % 2],))
                    else:
                        P.op("act", lambda E, s=s, q=q, tt=tt, bank=bank: E.copy(
                            out=s[:, q * 4:(q + 1) * 4, tt * 128:(tt + 1) * 128],
                            in_=bank[:, :].rearrange("p (a b) -> p a b", a=4)),
                            reads=(self.bank_res[q],), writes=(st_r[tb % 2],))
            if store_xt:
                for kc in range(KC):
                    P.dma("sp", XT[kc * 128:(kc + 1) * 128, tb * TB:(tb + 1) * TB], s[:, kc, :],
                          reads=(st_r[tb % 2],), writes=(self.xt_res[kc][tb],))
            if hT is not None:
                self.norm_block(lambda kc, s=s: s[:, kc, :], st_r[tb % 2], gname,
                                lambda kc, tb=tb: hT[:, kc, tb * TB:(tb + 1) * TB], hres[tb], TB,
                                self.banks[4 + tb % 2], self.bank_res[4 + tb % 2])

    def phase_ffn(self, l):
        P = self.P
        wg = self.dram[f"wgu{l}"]
        wd = self.dram[f"wd{l}"]
        for sbk in range(2):
            with ExitStack() as es:
                hT = self.sb(es, "f_hT", [128, KC, 1024], BF16)
                h_res = RL(2)
                aT = self.sb(es, "f_aT", [128, FC, 1024], BF16)
                a_res = [RL(2) for _ in range(FC)]
                with ExitStack() as es2:
                    stage = [self.sb(es2, f"f_st{i}", [128, KC, TB], F32) for i in range(2)]
                    sres = RL(2)
                    for hb in range(2):
                        tb = sbk * 2 + hb
                        self.load_xt_block(stage[hb], sres[hb], tb * TB, TB,
                                           [self.xt_res[kc][tb] for kc in range(KC)])
                        self.norm_block(lambda kc, st=stage[hb]: st[:, kc, :], sres[hb], f"ffn_norm{l}",
                                        lambda kc, hb=hb: hT[:, kc, hb * TB:(hb + 1) * TB], h_res[hb], TB,
                                        self.banks[4 + hb], self.bank_res[4 + hb])
                    P.barrier()
                with ExitStack() as es2:
                    wb = [self.sb(es, f"f_wg{i}", [128, KC * 256], BF16) for i in range(3)]
                    wr = RL(3)
                    sg = [self.sb(es, f"f_sg{i}", [128, TB], F32) for i in range(2)]
                    sgr = RL(2)
                    it = 0
                    for j in range(FC):
                        w = wb[j % 3]
                        self.load_panel(w[:, :], wr[j % 3], wg, j, KC * 256)
                        for hb in range(2):
                            bg = self.banks[(it % 2) * 2]
                            bu = self.banks[(it % 2) * 2 + 1]
                            bgr = self.bank_res[(it % 2) * 2]
                            bur = self.bank_res[(it % 2) * 2 + 1]
                            s = sg[it % 2]
                            sr = sgr[it % 2]
                            it += 1

                            def mm(E, w=w, hb=hb, bank=bg, off=0):
                                ins = None
                                for kc in range(KC):
                                    ins = E.matmul(bank[:, :], lhsT=w[:, kc * 256 + off:kc * 256 + off + 128],
                                                   rhs=hT[:, kc, hb * TB:(hb + 1) * TB],
                                                   start=(kc == 0), stop=(kc == KC - 1))
                                return ins
                            P.op("pe", mm, reads=(wr[j % 3], h_res[hb]), writes=(bgr,))
                            P.op("pe", lambda E, w=w, hb=hb, bu=bu, mm=mm: mm(E, w, hb, bu, 128),
                                 reads=(wr[j % 3], h_res[hb]), writes=(bur,))
                            P.op("act", lambda E, s=s, bg=bg: E.activation(out=s[:, :], in_=bg[:, :], func=AF.Silu),
                                 reads=(bgr,), writes=(sr,))
                            P.op("dve", lambda E, s=s, bu=bu, j=j, hb=hb: E.tensor_tensor(
                                out=aT[:, j, hb * TB:(hb + 1) * TB], in0=bu[:, :], in1=s[:, :], op=ALU.mult),
                                reads=(bur, sr), writes=(a_res[j][hb],))
                with ExitStack() as es2:
                    wb = [self.sb(es2, f"f_wd{i}", [128, FC * 128], BF16) for i in range(2)]
                    wr = RL(2)
                    xb = [self.sb(es2, f"f_x{i}", [128, TB], F32) for i in range(3)]
                    xr = RL(3)
                    XT = self.dram["XT"]
                    it = 0
                    for j in range(KC):
                        w = wb[j % 2]
                        self.load_panel(w[:, :], wr[j % 2], wd, j, FC * 128)
                        for hb in range(2):
                            tb = sbk * 2 + hb
                            bank = self.banks[it % 4]
                            br = self.bank_res[it % 4]
                            x = xb[it % 3]
                            xres = xr[it % 3]
                            it += 1
                            P.dma("sp", x[:, :], XT[j * 128:(j + 1) * 128, tb * TB:(tb + 1) * TB],
                                  reads=(self.xt_res[j][tb],), writes=(xres,))

                            def mm(E, w=w, hb=hb, bank=bank):
                                ins = None
                                for fc in range(FC):
                                    ins = E.matmul(bank[:, :], lhsT=w[:, fc * 128:(fc + 1) * 128],
                                                   rhs=aT[:, fc, hb * TB:(hb + 1) * TB],
                                                   start=(fc == 0), stop=(fc == FC - 1))
                                return ins
                            P.op("pe", mm, reads=[wr[j % 2]] + [a_res[fc][hb] for fc in range(FC)], writes=(br,))
                            P.op("dve", lambda E, x=x, bank=bank: E.tensor_tensor(
                                out=x[:, :], in0=bank[:, :], in1=x[:, :], op=ALU.add),
                                reads=(br, xres), writes=(xres,))
                            P.dma("sp", XT[j * 128:(j + 1) * 128, tb * TB:(tb + 1) * TB], x[:, :],
                                  reads=(xres,), writes=(self.xt_res[j][tb],))
                    P.barrier()

    def phase_xattn(self, l):
        P = self.P
        with ExitStack() as es:
            KT = self.sb(es, "xa_KT", [128, KC, NMEM], BF16)
            kres = RL(KC)
            V = self.sb(es, "xa_V", [128, 2, D], BF16)
            vres = RL(2)
            with ExitStack() as es2:
                mi = self.sb(es2, "xa_mi", [128, 2, D], F32)
                mir = RL(2)
                junk = self.sb(es2, "xa_junk", [128, D], F32)
                jr = Res()
                ss = self.sb(es2, "xa_ss", [128, 2], F32)
                ssr = Res()
                mnT = self.sb(es2, "xa_mnT", [128, KC, NMEM], BF16)
                mnr = Res()
                mem = self.dram["mem"]
                for mt in range(2):
                    P.dma("sp", mi[:, mt, :], mem[mt * 128:(mt + 1) * 128, :], writes=(mir[mt],))
                    P.op("act", lambda E, mt=mt: E.activation(out=junk[:, :], in_=mi[:, mt, :], func=AF.Square,
                                                              accum_out=ss[:, mt:mt + 1]),
                         reads=(mir[mt],), writes=(jr, ssr))
                P.op("act", lambda E: E.activation(out=ss[:, :], in_=ss[:, :], func=AF.Sqrt,
                                                   bias=self.eps_col[:, 0:1], scale=1.0 / D),
                     reads=(ssr,), writes=(ssr,))
                P.op("dve", lambda E: E.reciprocal(out=ss[:, :], in_=ss[:, :]), reads=(ssr,), writes=(ssr,))
                for mt in range(2):
                    P.op("dve", lambda E, mt=mt: E.tensor_scalar(out=mi[:, mt, :], in0=mi[:, mt, :],
                                                                 scalar1=ss[:, mt:mt + 1], scalar2=None, op0=ALU.mult),
                         reads=(ssr, mir[mt]), writes=(mir[mt],))
                    for q in range(4):
                        bank = self.banks[q]

                        def tr(E, mt=mt, q=q, bank=bank):
                            ins = None
                            for i in range(4):
                                kc = q * 4 + i
                                ins = E.transpose(out=bank[:, i * 128:(i + 1) * 128],
                                                  in_=mi[:, mt, kc * 128:(kc + 1) * 128], identity=self.ident_f[:, :])
                            return ins
                        P.op("pe", tr, reads=(mir[mt],), writes=(self.bank_res[q],))
                        for i in range(4):
                            kc = q * 4 + i
                            P.op("act", lambda E, mt=mt, kc=kc, i=i, bank=bank: E.activation(
                                out=mnT[:, kc, mt * 128:(mt + 1) * 128], in_=bank[:, i * 128:(i + 1) * 128],
                                func=AF.Identity, scale=self.par(f"mem_norm{l}", kc)),
                                reads=(self.bank_res[q],), writes=(mnr,))
                wb = [self.sb(es2, f"xa_wk{i}", [128, KC * 128], BF16) for i in range(2)]
                wr = RL(2)
                wk = self.dram[f"wkvK{l}"]
                for j in range(KC):
                    w = wb[j % 2]
                    self.load_panel(w[:, :], wr[j % 2], wk, j, KC * 128)
                    bank = self.banks[4 + j % 2]
                    br = self.bank_res[4 + j % 2]

                    def mm(E, w=w, bank=bank):
                        ins = None
                        for kc in range(KC):
                            ins = E.matmul(bank[:, 0:NMEM], lhsT=w[:, kc * 128:(kc + 1) * 128], rhs=mnT[:, kc, :],
                                           start=(kc == 0), stop=(kc == KC - 1))
                        return ins
                    P.op("pe", mm, reads=(wr[j % 2], mnr), writes=(br,))
                    P.op("act", lambda E, j=j, bank=bank: E.copy(out=KT[:, j, :], in_=bank[:, 0:NMEM]),
                         reads=(br,), writes=(kres[j],))
                wb2 = [self.sb(es2, f"xa_wv{i}", [128, KC * 512], BF16) for i in range(2)]
                wr2 = RL(2)
                wv = self.dram[f"wkvV{l}"]
                it = 0
                for pn in range(4):
                    w = wb2[pn % 2]
                    self.load_panel(w[:, :], wr2[pn % 2], wv, pn, KC * 512)
                    for mt in range(2):
                        bank = self.banks[it % 4]
                        br = self.bank_res[it % 4]
                        it += 1

                        def mm(E, w=w, mt=mt, bank=bank):
                            ins = None
                            for kc in range(KC):
                                ins = E.matmul(bank[:, :], lhsT=mnT[:, kc, mt * 128:(mt + 1) * 128],
                                               rhs=w[:, kc * 512:(kc + 1) * 512],
                                               start=(kc == 0), stop=(kc == KC - 1))
                            return ins
                        P.op("pe", mm, reads=(wr2[pn % 2], mnr), writes=(br,))
                        P.op("dve", lambda E, mt=mt, pn=pn, bank=bank: E.tensor_copy(
                            out=V[:, mt, pn * 512:(pn + 1) * 512], in_=bank[:, :]),
                            reads=(br,), writes=(vres[mt],))
                P.barrier()
            hT = self.sb(es, "xa_hT", [128, KC, TOK], BF16)
            hres = RL(NB)
            with ExitStack() as es2:
                self.norm_all(es2, f"xa_norm{l}", hT, lambda tb: hres[tb])
                P.barrier()
            QT = self.sb(es, "xa_QT", [128, KC, TOK], BF16)
            qres = [RL(NB) for _ in range(KC)]
            with ExitStack() as es2:
                wb = [self.sb(es, f"xa_wq{i}", [128, KC * 128], BF16) for i in range(2)]
                wr = RL(2)
                wq = self.dram[f"wq{l}"]
                it = 0
                for j in range(KC):
                    w = wb[j % 2]
                    self.load_panel(w[:, :], wr[j % 2], wq, j, KC * 128)
                    for tb in range(NB):
                        bank = self.banks[it % 4]
                        br = self.bank_res[it % 4]
                        it += 1

                        def mm(E, w=w, tb=tb, bank=bank):
                            ins = None
                            for kc in range(KC):
                                ins = E.matmul(bank[:, :], lhsT=w[:, kc * 128:(kc + 1) * 128],
                                               rhs=hT[:, kc, tb * TB:(tb + 1) * TB],
                                               start=(kc == 0), stop=(kc == KC - 1))
                            return ins
                        P.op("pe", mm, reads=(wr[j % 2], hres[tb]), writes=(br,))
                        P.op("act", lambda E, j=j, tb=tb, bank=bank: E.activation(
                            out=QT[:, j, tb * TB:(tb + 1) * TB], in_=bank[:, :], func=AF.Copy, scale=512.0 ** -0.5),
                            reads=(br,), writes=(qres[j][tb],))
            with ExitStack() as es2:
                pT = [self.sb(es, f"xa_p{i}", [128, TB], BF16) for i in range(4)]
                pr = RL(4)
                rinv = [self.sb(es, f"xa_ri{i}", [128, TB], F32) for i in range(2)]
                rr = RL(2)
                io = 0
                iters = [(h, tb) for h in range(4) for tb in range(NB)]

                def emit_S(it):
                    h, tb = iters[it]
                    for mt in range(2):
                        bank = self.banks[(it % 2) * 2 + mt]
                        br = self.bank_res[(it % 2) * 2 + mt]

                        def mm(E, h=h, tb=tb, mt=mt, bank=bank):
                            ins = None
                            for dc in range(4):
                                ins = E.matmul(bank[:, :], lhsT=KT[:, h * 4 + dc, mt * 128:(mt + 1) * 128],
                                               rhs=QT[:, h * 4 + dc, tb * TB:(tb + 1) * TB],
                                               start=(dc == 0), stop=(dc == 3))
                            return ins
                        P.op("pe", mm, reads=[kres[h * 4 + dc] for dc in range(4)] +
                             [qres[h * 4 + dc][tb] for dc in range(4)], writes=(br,))

                emit_S(0)
                for it, (h, tb) in enumerate(iters):
                    pts = []
                    for mt in range(2):
                        bank = self.banks[(it % 2) * 2 + mt]
                        br = self.bank_res[(it % 2) * 2 + mt]
                        p = pT[(it % 2) * 2 + mt]
                        prs = pr[(it % 2) * 2 + mt]
                        P.op("act", lambda E, p=p, bank=bank: E.activation(out=p[:, :], in_=bank[:, :], func=AF.Exp),
                             reads=(br,), writes=(prs,))
                        pts.append((p, prs))
                    if it + 1 < len(iters):
                        emit_S(it + 1)
                    bl = self.banks[4]
                    blr = self.bank_res[4]

                    def mml(E, pts=pts, bl=bl):
                        ins = None
                        for mt in range(2):
                            ins = E.matmul(bl[:, :], lhsT=self.ones_bf[:, :], rhs=pts[mt][0][:, :],
                                           start=(mt == 0), stop=(mt == 1))
                        return ins
                    P.op("pe", mml, reads=(pts[0][1], pts[1][1]), writes=(blr,))
                    ri = rinv[it % 2]
                    rir = rr[it % 2]
                    P.op("dve", lambda E, ri=ri, bl=bl: E.reciprocal(out=ri[:, :], in_=bl[:, :]),
                         reads=(blr,), writes=(rir,))
                    for dc in range(4):
                        bo = self.banks[5 + io % 2]
                        bor = self.bank_res[5 + io % 2]
                        io += 1

                        def mmo(E, h=h, dc=dc, pts=pts, bo=bo):
                            ins = None
                            for mt in range(2):
                                ins = E.matmul(bo[:, :], lhsT=V[:, mt, h * 512 + dc * 128:h * 512 + (dc + 1) * 128],
                                               rhs=pts[mt][0][:, :], start=(mt == 0), stop=(mt == 1))
                            return ins
                        P.op("pe", mmo, reads=(vres[0], vres[1], pts[0][1], pts[1][1]), writes=(bor,))
                        P.op("dve", lambda E, h=h, dc=dc, tb=tb, bo=bo, ri=ri: E.tensor_tensor(
                            out=hT[:, h * 4 + dc, tb * TB:(tb + 1) * TB], in0=bo[:, :], in1=ri[:, :], op=ALU.mult),
                            reads=(bor, rir), writes=(hres[tb],))
            with ExitStack() as es2:
                self.proj_residual(es2, self.dram[f"wo{l}"], KC, lambda kc, tb: hT[:, kc, tb * TB:(tb + 1) * TB],
                                   lambda kc, tb: hres[tb], "xa_o")
                P.barrier()

    def hgrn_head(self, h, hsrc, hsrc_res, w, wres, W, cols, main, bufs):
        P = self.P
        (fA, fB, fC, fD, fE, fr, KtT, ktr, QtT, qtr, sgT, sgr, o_sb, osr, vsb, vr, Ktsb, Ktr, ATsb, atr,
         Spp, sppr, tmpS, tmpr, esc, escr, nbm, nbmr, d2, S, Sres, ybT, ybres, rs, rsr, tmpO, tmpOr) = bufs
        hp = h % 2
        bf = self.banks[0]; bfr = self.bank_res[0]
        bq = self.banks[1]; bqr = self.bank_res[1]
        bg = self.banks[2]; bgr = self.bank_res[2]
        oml = self.lbv[:, h, 0:1]
        lb = self.lbv[:, h, 1:2]

        def proj(E, bank, col, tb):
            ins = None
            for kc in range(KC):
                ins = E.matmul(bank[:, :], lhsT=w[:, kc * W + col:kc * W + col + 128],
                               rhs=hsrc[:, kc, tb * TB:(tb + 1) * TB], start=(kc == 0), stop=(kc == KC - 1))
            return ins

        for tb in range(NB):
            P.op("pe", lambda E, tb=tb: proj(E, bf, cols["f"], tb), reads=(wres, hsrc_res[tb]), writes=(bfr,))
            P.op("act", lambda E: E.activation(out=fA[:, :], in_=bf[:, :], func=AF.Sigmoid), reads=(bfr,), writes=(fr[0],))
            P.op("dve", lambda E: E.tensor_scalar(out=fA[:, :], in0=fA[:, :], scalar1=oml, scalar2=lb,
                                                  op0=ALU.mult, op1=ALU.add),
                 reads=(fr[0], self.lbv_res), writes=(fr[0],))
            P.op("act", lambda E: E.activation(out=fB[:, :], in_=fA[:, :], func=AF.Ln), reads=(fr[0],), writes=(fr[1],))
            P.op("dve", lambda E: E.tensor_scalar(out=fA[:, :], in0=fA[:, :], scalar1=-1.0, scalar2=1.0,
                                                  op0=ALU.mult, op1=ALU.add), reads=(fr[0],), writes=(fr[0],))
            for tt in range(4):
                P.op("dve", lambda E, tt=tt: E.tensor_tensor_scan(
                    out=fC[:, tt * 128:(tt + 1) * 128], data0=self.ones_f[:, 0:128], data1=fB[:, tt * 128:(tt + 1) * 128],
                    initial=0.0, op0=ALU.mult, op1=ALU.add), reads=(fr[1],), writes=(fr[2],))
            bmid = fC[:, 63:TB:128]
            blast = fC[:, 127:TB:128]
            P.op("dve", lambda E: E.tensor_scalar(out=nbm[:, :], in0=bmid, scalar1=-1.0, scalar2=None, op0=ALU.mult),
                 reads=(fr[2],), writes=(nbmr,))
            P.op("dve", lambda E: E.tensor_tensor(out=d2[:, :], in0=blast, in1=bmid, op=ALU.subtract),
                 reads=(fr[2],), writes=(nbmr,))
            P.op("act", lambda E, tb=tb: E.activation(out=esc[:, 0, tb * 4:(tb + 1) * 4], in_=bmid, func=AF.Exp),
                 reads=(fr[2],), writes=(escr,))
            P.op("act", lambda E, tb=tb: E.activation(out=esc[:, 1, tb * 4:(tb + 1) * 4], in_=d2[:, :], func=AF.Exp),
                 reads=(nbmr,), writes=(escr,))
            P.op("act", lambda E, tb=tb: E.activation(out=esc[:, 2, tb * 4:(tb + 1) * 4], in_=blast, func=AF.Exp),
                 reads=(fr[2],), writes=(escr,))
            for tt in range(4):
                sl = slice(tt * 128, (tt + 1) * 128)
                if main:
                    P.op("act", lambda E, tt=tt, sl=sl: E.activation(out=fD[:, sl], in_=fC[:, sl], func=AF.Exp,
                                                                      bias=nbm[:, tt:tt + 1], scale=1.0),
                         reads=(fr[2], nbmr), writes=(fr[3],))
                P.op("act", lambda E, tt=tt, sl=sl: E.activation(out=fE[:, sl], in_=fC[:, sl], func=AF.Exp,
                                                                  bias=fC[:, 63 + 128 * tt:64 + 128 * tt], scale=-1.0),
                     reads=(fr[2],), writes=(fr[4],))
            P.op("dve", lambda E, tb=tb: E.tensor_tensor(out=KtT[:, tb * TB:(tb + 1) * TB], in0=fA[:, :], in1=fE[:, :],
                                                        op=ALU.mult), reads=(fr[0], fr[4]), writes=(ktr[tb],))
            if main:
                P.op("pe", lambda E, tb=tb: proj(E, bq, cols["q"], tb), reads=(wres, hsrc_res[tb]), writes=(bqr,))
                P.op("act", lambda E: E.activation(out=fB[:, :], in_=bq[:, :], func=AF.Silu), reads=(bqr,), writes=(fr[1],))
                P.op("dve", lambda E, tb=tb: E.scalar_tensor_tensor(
                    out=QtT[:, tb * TB:(tb + 1) * TB], in0=fB[:, :], scalar=128.0 ** -0.5, in1=fD[:, :],
                    op0=ALU.mult, op1=ALU.mult), reads=(fr[1], fr[3]), writes=(qtr[tb],))
                P.op("pe", lambda E, tb=tb: proj(E, bg, cols["g"], tb), reads=(wres, hsrc_res[tb]), writes=(bgr,))
                P.op("act", lambda E, tb=tb: E.activation(out=sgT[:, tb * TB:(tb + 1) * TB], in_=bg[:, :], func=AF.Silu),
                     reads=(bgr,), writes=(sgr[tb],))
            bo = self.banks[5 + hp]
            bor = self.bank_res[5 + hp]
            yield
            for tt in range(4):
                t = tb * 4 + tt
                tsl = slice(t * 128, (t + 1) * 128)
                bi = self.banks[3]
                bir = self.bank_res[3]

                def mmi(E, tsl=tsl, bi=bi):
                    ins = None
                    for kc in range(KC):
                        ins = E.matmul(bi[:, 0:128], lhsT=hsrc[:, kc, tsl], rhs=w[:, kc * W + cols["i"]:kc * W + cols["i"] + 128],
                                       start=(kc == 0), stop=(kc == KC - 1))
                    return ins
                P.op("pe", mmi, reads=(wres, hsrc_res[tb]), writes=(bir,))
                v = vsb[t % 2]
                P.op("act", lambda E, v=v, bi=bi: E.copy(out=v[:, :], in_=bi[:, 0:128]), reads=(bir,), writes=(vr[t % 2],))
                P.op("pe", lambda E, tsl=tsl: E.transpose(out=self.bank_bf[:, 0:128], in_=KtT[:, tsl],
                                                          identity=self.ident_b[:, :]),
                     reads=(ktr[tb],), writes=(self.bank_bf_res,))
                kt = Ktsb[t % 2]
                P.op("dve", lambda E, kt=kt: E.tensor_copy(out=kt[:, :], in_=self.bank_bf[:, 0:128]),
                     reads=(self.bank_bf_res,), writes=(Ktr[t % 2],))
                if main:
                    ba = self.banks[4]
                    bar = self.bank_res[4]
                    P.op("pe", lambda E, tsl=tsl, ba=ba: E.matmul(ba[:, 0:128], lhsT=KtT[:, tsl], rhs=QtT[:, tsl],
                                                                  start=True, stop=True),
                         reads=(ktr[tb], qtr[tb]), writes=(bar,))
                    at = ATsb[t % 2]
                    P.op("dve", lambda E, ba=ba: E.tensor_scalar(out=tmpS[:, :], in0=ba[:, 0:128], scalar1=1e30,
                                                                 scalar2=-1e30, op0=ALU.min, op1=ALU.max),
                         reads=(bar,), writes=(tmpr,))
                    P.op("dve", lambda E, at=at: E.tensor_tensor(out=at[:, :], in0=tmpS[:, :], in1=self.triu_f[:, :],
                                                                 op=ALU.mult),
                         reads=(tmpr,), writes=(atr[t % 2],))
                    P.op("dve", lambda E, t=t: E.tensor_scalar(out=Spp[:, :], in0=S[:, h, :], scalar1=esc[:, 0, t:t + 1],
                                                               scalar2=None, op0=ALU.mult),
                         reads=(Sres[h], escr), writes=(sppr,))

                    def mmo(E, v=v, at=at, tsl=tsl, tt=tt):
                        E.matmul(bo[:, tt * 128:(tt + 1) * 128], lhsT=v[:, :], rhs=at[:, :], start=True, stop=False)
                        return E.matmul(bo[:, tt * 128:(tt + 1) * 128], lhsT=Spp[:, :], rhs=QtT[:, tsl],
                                        start=False, stop=True)
                    P.op("pe", mmo, reads=(vr[t % 2], atr[t % 2], sppr, qtr[tb]), writes=(bor,))
                bu = self.bank_bf[:, :].bitcast(F32)[:, 128:256]
                bur = self.bank_u_res
                P.op("pe", lambda E, kt=kt, v=v, bu=bu: E.matmul(bu, lhsT=kt[:, :], rhs=v[:, :],
                                                                 start=True, stop=True),
                     reads=(Ktr[t % 2], vr[t % 2]), writes=(bur,))
                P.op("dve", lambda E, t=t: E.tensor_scalar(out=tmpS[:, :], in0=S[:, h, :], scalar1=esc[:, 2, t:t + 1],
                                                           scalar2=None, op0=ALU.mult),
                     reads=(Sres[h], escr), writes=(tmpr,))
                P.op("dve", lambda E, t=t, bu=bu: E.scalar_tensor_tensor(
                    out=S[:, h, :], in0=bu, scalar=esc[:, 1, t:t + 1], in1=tmpS[:, :],
                    op0=ALU.mult, op1=ALU.add), reads=(bur, tmpr, escr), writes=(Sres[h],))
                yield
            if main:
                bsl = slice(tb * TB, (tb + 1) * TB)
                P.op("act", lambda E: E.copy(out=o_sb[:, :], in_=bo[:, :]), reads=(bor,), writes=(osr,))
                sq = self.sq[hp]
                P.op("act", lambda E, sq=sq: E.activation(out=sq[:, :], in_=o_sb[:, :], func=AF.Square),
                     reads=(osr,), writes=(self.sq_res[hp],))
                bn = self.banks[4]
                bnr = self.bank_res[4]
                P.op("pe", lambda E, sq=sq, bn=bn: E.matmul(bn[:, :], lhsT=self.ones_bf[:, :], rhs=sq[:, :],
                                                            start=True, stop=True),
                     reads=(self.sq_res[hp],), writes=(bnr,))
                P.op("act", lambda E, bn=bn: E.activation(out=rs[:, :], in_=bn[:, :], func=AF.Sqrt,
                                                          bias=self.eps_col[:, 0:1], scale=1.0 / 128),
                     reads=(bnr,), writes=(rsr,))
                P.op("dve", lambda E: E.reciprocal(out=rs[:, :], in_=rs[:, :]), reads=(rsr,), writes=(rsr,))
                P.op("dve", lambda E: E.scalar_tensor_tensor(out=tmpO[:, :], in0=o_sb[:, :], scalar=self.par("hg_norm"),
                                                             in1=rs[:, :], op0=ALU.mult, op1=ALU.mult),
                     reads=(osr, rsr), writes=(tmpOr,))
                P.op("dve", lambda E, bsl=bsl: E.tensor_tensor(out=ybT[:, h, bsl], in0=tmpO[:, :], in1=sgT[:, bsl],
                                                              op=ALU.mult),
                     reads=(tmpOr, sgr[tb]), writes=(ybres[h][tb],))

    def hgrn_steps(self, h, hsrc, hsrc_res, w, wres, W, cols, main, B):
        P = self.P
        S, Sres, ybT, ybres = B["S"], B["Sres"], B["ybT"], B["ybres"]
        fA, fB, fC, fD, fE, fr = B["fA"], B["fB"], B["fC"], B["fD"], B["fE"], B["fr"]
        KtT, ktr, QtT, qtr, sgT, sgr = B["KtT"], B["ktr"], B["QtT"], B["qtr"], B["sgT"], B["sgr"]
        esc, escr, nbm, nbmr, d2 = B["esc"], B["escr"], B["nbm"], B["nbmr"], B["d2"]
        v_all, var, Kt_all, Ktar, AT_all, ATar = B["v_all"], B["var"], B["Kt_all"], B["Ktar"], B["AT_all"], B["ATar"]
        U_all, Uar, Spp_all, Sppr, tmpA, tmpAr = B["U_all"], B["Uar"], B["Spp_all"], B["Sppr"], B["tmpA"], B["tmpAr"]
        tmpS, tmpr, o_sb, osr, rs, rsr, tmpO, tmpOr = B["tmpS"], B["tmpr"], B["o_sb"], B["osr"], B["rs"], B["rsr"], B["tmpO"], B["tmpOr"]
        bf, bfr = self.banks[0], self.bank_res[0]
        bq, bqr = self.banks[1], self.bank_res[1]
        bg, bgr = self.banks[2], self.bank_res[2]
        bi, bir = self.banks[3], self.bank_res[3]
        ba, bar = self.banks[4], self.bank_res[4]
        bo, bor = self.banks[5], self.bank_res[5]
        bu, bur = self.banks[6], self.bank_res[6]
        oml = self.lbv[:, h, 0:1]
        lb = self.lbv[:, h, 1:2]

        def proj(E, bank, col, tb):
            ins = None
            for kc in range(KC):
                ins = E.matmul(bank[:, :], lhsT=w[:, kc * W + col:kc * W + col + 128],
                               rhs=hsrc[:, kc, tb * TB:(tb + 1) * TB], start=(kc == 0), stop=(kc == KC - 1))
            return ins

        def p1(tb):
            bsl = slice(tb * TB, (tb + 1) * TB)
            t4 = slice(tb * 4, (tb + 1) * 4)
            P.op("pe", lambda E, tb=tb: proj(E, bf, cols["f"], tb), reads=(wres, hsrc_res[tb]), writes=(bfr,))
            P.op("act", lambda E: E.activation(out=fA[:, :], in_=bf[:, :], func=AF.Sigmoid), reads=(bfr,), writes=(fr[0],))
            P.op("dve", lambda E: E.tensor_scalar(out=fA[:, :], in0=fA[:, :], scalar1=oml, scalar2=lb,
                                                  op0=ALU.mult, op1=ALU.add),
                 reads=(fr[0], self.lbv_res), writes=(fr[0],))
            P.op("act", lambda E: E.activation(out=fB[:, :], in_=fA[:, :], func=AF.Ln), reads=(fr[0],), writes=(fr[1],))
            P.op("dve", lambda E: E.tensor_scalar(out=fA[:, :], in0=fA[:, :], scalar1=-1.0, scalar2=1.0,
                                                  op0=ALU.mult, op1=ALU.add), reads=(fr[0],), writes=(fr[0],))
            for tt in range(4):
                P.op("dve", lambda E, tt=tt: E.tensor_tensor_scan(
                    out=fC[:, tt * 128:(tt + 1) * 128], data0=self.ones_f[:, 0:128], data1=fB[:, tt * 128:(tt + 1) * 128],
                    initial=0.0, op0=ALU.mult, op1=ALU.add), reads=(fr[1],), writes=(fr[2],))
            bmid = fC[:, 63:TB:128]
            blast = fC[:, 127:TB:128]
            P.op("dve", lambda E: E.tensor_scalar(out=nbm[:, :], in0=bmid, scalar1=-1.0, scalar2=None, op0=ALU.mult),
                 reads=(fr[2],), writes=(nbmr,))
            P.op("dve", lambda E: E.tensor_tensor(out=d2[:, :], in0=blast, in1=bmid, op=ALU.subtract),
                 reads=(fr[2],), writes=(nbmr,))
            P.op("act", lambda E, t4=t4: E.activation(out=esc[:, 0, t4], in_=bmid, func=AF.Exp), reads=(fr[2],), writes=(escr,))
            P.op("act", lambda E, t4=t4: E.activation(out=esc[:, 1, t4], in_=d2[:, :], func=AF.Exp), reads=(nbmr,), writes=(escr,))
            P.op("act", lambda E, t4=t4: E.activation(out=esc[:, 2, t4], in_=blast, func=AF.Exp), reads=(fr[2],), writes=(escr,))
            for tt in range(4):
                sl = slice(tt * 128, (tt + 1) * 128)
                if main:
                    P.op("act", lambda E, tt=tt, sl=sl: E.activation(out=fD[:, sl], in_=fC[:, sl], func=AF.Exp,
                                                                      bias=nbm[:, tt:tt + 1], scale=1.0),
                         reads=(fr[2], nbmr), writes=(fr[3],))
                P.op("act", lambda E, tt=tt, sl=sl: E.activation(out=fE[:, sl], in_=fC[:, sl], func=AF.Exp,
                                                                  bias=fC[:, 63 + 128 * tt:64 + 128 * tt], scale=-1.0),
                     reads=(fr[2],), writes=(fr[4],))
            P.op("dve", lambda E, bsl=bsl: E.tensor_tensor(out=KtT[:, bsl], in0=fA[:, :], in1=fE[:, :], op=ALU.mult),
                 reads=(fr[0], fr[4]), writes=(ktr[tb],))
            if main:
                P.op("pe", lambda E, tb=tb: proj(E, bq, cols["q"], tb), reads=(wres, hsrc_res[tb]), writes=(bqr,))
                P.op("act", lambda E: E.activation(out=fB[:, :], in_=bq[:, :], func=AF.Silu), reads=(bqr,), writes=(fr[1],))
                P.op("dve", lambda E, bsl=bsl: E.scalar_tensor_tensor(
                    out=QtT[:, bsl], in0=fB[:, :], scalar=128.0 ** -0.5, in1=fD[:, :],
                    op0=ALU.mult, op1=ALU.mult), reads=(fr[1], fr[3]), writes=(qtr[tb],))
                P.op("pe", lambda E, tb=tb: proj(E, bg, cols["g"], tb), reads=(wres, hsrc_res[tb]), writes=(bgr,))
                P.op("act", lambda E, bsl=bsl: E.activation(out=sgT[:, bsl], in_=bg[:, :], func=AF.Silu),
                     reads=(bgr,), writes=(sgr[tb],))

            def mmi(E, tb=tb):
                ins = None
                for tt in range(4):
                    tsl = slice((tb * 4 + tt) * 128, (tb * 4 + tt + 1) * 128)
                    for kc in range(KC):
                        ins = E.matmul(bi[:, tt * 128:(tt + 1) * 128], lhsT=hsrc[:, kc, tsl],
                                       rhs=w[:, kc * W + cols["i"]:kc * W + cols["i"] + 128],
                                       start=(kc == 0), stop=(kc == KC - 1))
                return ins
            P.op("pe", mmi, reads=(wres, hsrc_res[tb]), writes=(bir,))
            P.op("act", lambda E, t4=t4: E.copy(out=v_all[:, t4, :], in_=bi[:, :].rearrange("p (a b) -> p a b", a=4)),
                 reads=(bir,), writes=(var[tb],))

            def trk(E, tb=tb):
                ins = None
                for tt in range(4):
                    tsl = slice((tb * 4 + tt) * 128, (tb * 4 + tt + 1) * 128)
                    ins = E.transpose(out=self.bank_bf[:, tt * 128:(tt + 1) * 128], in_=KtT[:, tsl],
                                      identity=self.ident_b[:, :])
                return ins
            P.op("pe", trk, reads=(ktr[tb],), writes=(self.bank_bf_res,))
            P.op("dve", lambda E, t4=t4: E.tensor_copy(out=Kt_all[:, t4, :],
                                                      in_=self.bank_bf[:, 0:512].rearrange("p (a b) -> p a b", a=4)),
                 reads=(self.bank_bf_res,), writes=(Ktar[tb],))
            if main:
                def mma(E, tb=tb):
                    ins = None
                    for tt in range(4):
                        tsl = slice((tb * 4 + tt) * 128, (tb * 4 + tt + 1) * 128)
                        ins = E.matmul(ba[:, tt * 128:(tt + 1) * 128], lhsT=KtT[:, tsl], rhs=QtT[:, tsl],
                                       start=True, stop=True)
                    return ins
                P.op("pe", mma, reads=(ktr[tb], qtr[tb]), writes=(bar,))
                P.op("dve", lambda E: E.tensor_scalar(out=tmpA[:, :], in0=ba[:, :], scalar1=1e30, scalar2=-1e30,
                                                      op0=ALU.min, op1=ALU.max), reads=(bar,), writes=(tmpAr,))
                P.op("dve", lambda E, t4=t4: E.tensor_tensor(
                    out=AT_all[:, t4, :], in0=tmpA[:, :].rearrange("p (a b) -> p a b", a=4), in1=self.triu4[:, :, :],
                    op=ALU.mult), reads=(tmpAr,), writes=(ATar[tb],))

            def mmu(E, tb=tb):
                ins = None
                for tt in range(4):
                    t = tb * 4 + tt
                    ins = E.matmul(bu[:, tt * 128:(tt + 1) * 128], lhsT=Kt_all[:, t, :], rhs=v_all[:, t, :],
                                   start=True, stop=True)
                return ins
            P.op("pe", mmu, reads=(Ktar[tb], var[tb]), writes=(bur,))
            P.op("act", lambda E, t4=t4: E.copy(out=U_all[:, t4, :], in_=bu[:, :].rearrange("p (a b) -> p a b", a=4)),
                 reads=(bur,), writes=(Uar[tb],))
        def p2(tb):
          for t in range(tb * 4, tb * 4 + 4):
            if main:
                P.op("dve", lambda E, t=t: E.tensor_scalar(out=Spp_all[:, t, :], in0=S[:, h, :], scalar1=esc[:, 0, t:t + 1],
                                                           scalar2=None, op0=ALU.mult),
                     reads=(Sres[h], escr), writes=(Sppr[tb],))
            P.op("dve", lambda E, t=t: E.tensor_scalar(out=tmpS[:, :], in0=S[:, h, :], scalar1=esc[:, 2, t:t + 1],
                                                       scalar2=None, op0=ALU.mult),
                 reads=(Sres[h], escr), writes=(tmpr,))
            P.op("dve", lambda E, t=t: E.scalar_tensor_tensor(
                out=S[:, h, :], in0=U_all[:, t, :], scalar=esc[:, 1, t:t + 1], in1=tmpS[:, :],
                op0=ALU.mult, op1=ALU.add), reads=(Uar[tb], tmpr, escr), writes=(Sres[h],))
        def p3(tb):
            if not main:
                return
            bsl = slice(tb * TB, (tb + 1) * TB)

            def mmo(E, tb=tb):
                ins = None
                for tt in range(4):
                    t = tb * 4 + tt
                    tsl = slice(t * 128, (t + 1) * 128)
                    E.matmul(bo[:, tt * 128:(tt + 1) * 128], lhsT=v_all[:, t, :], rhs=AT_all[:, t, :], start=True, stop=False)
                    ins = E.matmul(bo[:, tt * 128:(tt + 1) * 128], lhsT=Spp_all[:, t, :], rhs=QtT[:, tsl],
                                   start=False, stop=True)
                return ins
            P.op("pe", mmo, reads=(var[tb], ATar[tb], Sppr[tb], qtr[tb]), writes=(bor,))
            P.op("act", lambda E: E.copy(out=o_sb[:, :], in_=bo[:, :]), reads=(bor,), writes=(osr,))
            sq = self.sq[0]
            P.op("act", lambda E, sq=sq: E.activation(out=sq[:, :], in_=o_sb[:, :], func=AF.Square),
                 reads=(osr,), writes=(self.sq_res[0],))
            P.op("pe", lambda E, sq=sq: E.matmul(ba[:, :], lhsT=self.ones_bf[:, :], rhs=sq[:, :], start=True, stop=True),
                 reads=(self.sq_res[0],), writes=(bar,))
            P.op("act", lambda E: E.activation(out=rs[:, :], in_=ba[:, :], func=AF.Sqrt,
                                               bias=self.eps_col[:, 0:1], scale=1.0 / 128),
                 reads=(bar,), writes=(rsr,))
            P.op("dve", lambda E: E.reciprocal(out=rs[:, :], in_=rs[:, :]), reads=(rsr,), writes=(rsr,))
            P.op("dve", lambda E: E.scalar_tensor_tensor(out=tmpO[:, :], in0=o_sb[:, :], scalar=self.par("hg_norm"),
                                                         in1=rs[:, :], op0=ALU.mult, op1=ALU.mult),
                 reads=(osr, rsr), writes=(tmpOr,))
            P.op("dve", lambda E, bsl=bsl: E.tensor_tensor(out=ybT[:, h, bsl], in0=tmpO[:, :], in1=sgT[:, bsl], op=ALU.mult),
                 reads=(tmpOr, sgr[tb]), writes=(ybres[h][tb],))
        return p1, p2, p3

    def hgrn_run(self, hsrc, hsrc_res, wb, wr, wsrc, W, cols, main, B):
        pend = None
        for h in range(8):
            self.load_panel(wb[h % 2][:, :], wr[h % 2], wsrc, h, KC * W)
            p1, p2, p3 = self.hgrn_steps(h, hsrc, hsrc_res, wb[h % 2], wr[h % 2], W, cols, main, B)
            for tb in range(NB):
                p1(tb)
                p2(tb)
                if pend is not None:
                    pend()
                pend = (lambda p3=p3, tb=tb: p3(tb))
        if pend is not None:
            pend()

    def hgrn_bufs2(self, es, S, Sres, ybT, ybres, main):
        sb = self.sb
        B = {"S": S, "Sres": Sres, "ybT": ybT, "ybres": ybres}
        for nm in ("fA", "fB", "fC", "fD", "fE"):
            B[nm] = sb(es, "hg_" + nm, [128, TB], F32)
        B["fr"] = RL(5)
        B["KtT"] = sb(es, "hg_KtT", [128, TOK], BF16); B["ktr"] = RL(NB)
        B["esc"] = sb(es, "hg_esc", [128, 3, 16], F32); B["escr"] = Res()
        B["nbm"] = sb(es, "hg_nbm", [128, 4], F32); B["nbmr"] = Res()
        B["d2"] = sb(es, "hg_d2", [128, 4], F32)
        B["v_all"] = sb(es, "hg_v", [128, 16, 128], BF16); B["var"] = RL(NB)
        B["Kt_all"] = sb(es, "hg_kt", [128, 16, 128], BF16); B["Ktar"] = RL(NB)
        B["U_all"] = sb(es, "hg_U", [128, 16, 128], F32); B["Uar"] = RL(NB)
        B["tmpS"] = sb(es, "hg_tmpS", [128, 128], F32); B["tmpr"] = Res()
        for nm in ("QtT", "qtr", "sgT", "sgr", "AT_all", "ATar", "Spp_all", "Sppr", "tmpA", "tmpAr",
                   "o_sb", "osr", "rs", "rsr", "tmpO", "tmpOr"):
            B[nm] = None
        if main:
            B["QtT"] = sb(es, "hg_QtT", [128, TOK], BF16); B["qtr"] = RL(NB)
            B["sgT"] = sb(es, "hg_sgT", [128, TOK], BF16); B["sgr"] = RL(NB)
            B["AT_all"] = sb(es, "hg_at", [128, 16, 128], BF16); B["ATar"] = RL(NB)
            B["Spp_all"] = sb(es, "hg_spp", [128, 16, 128], BF16); B["Sppr"] = RL(NB)
            B["tmpA"] = sb(es, "hg_tmpA", [128, TB], F32); B["tmpAr"] = Res()
            B["o_sb"] = sb(es, "hg_o", [128, TB], F32); B["osr"] = Res()
            B["rs"] = sb(es, "hg_rs", [128, TB], F32); B["rsr"] = Res()
            B["tmpO"] = sb(es, "hg_tmpO", [128, TB], F32); B["tmpOr"] = Res()
        return B

    def run_heads(self, make_gen, nheads=8, stagger=10):
        active = [make_gen(0)]
        nxt = 1
        steps = 0
        while active:
            for g in list(active):
                try:
                    next(g)
                except StopIteration:
                    active.remove(g)
            steps += 1
            while len(active) < 2 and nxt < nheads and (steps >= stagger or not active):
                active.append(make_gen(nxt))
                nxt += 1

    def hgrn_bufs(self, es, S, Sres, ybT, ybres):
        sb = self.sb
        fA = sb(es, "hg_fA", [128, TB], F32); fB = sb(es, "hg_fB", [128, TB], F32)
        fC = sb(es, "hg_fC", [128, TB], F32); fD = sb(es, "hg_fD", [128, TB], F32)
        fE = sb(es, "hg_fE", [128, TB], F32)
        fr = RL(5)
        KtT = sb(es, "hg_KtT", [128, TOK], BF16); ktr = RL(NB)
        QtT = sb(es, "hg_QtT", [128, TOK], BF16); qtr = RL(NB)
        sgT = sb(es, "hg_sgT", [128, TOK], BF16); sgr = RL(NB)
        o_sb = sb(es, "hg_o", [128, TB], F32); osr = Res()
        vsb = [sb(es, f"hg_v{i}", [128, 128], BF16) for i in range(2)]; vr = RL(2)
        Ktsb = [sb(es, f"hg_kt{i}", [128, 128], BF16) for i in range(2)]; Ktr = RL(2)
        ATsb = [sb(es, f"hg_at{i}", [128, 128], BF16) for i in range(2)]; atr = RL(2)
        Spp = sb(es, "hg_spp", [128, 128], BF16); sppr = Res()
        tmpS = sb(es, "hg_tmpS", [128, 128], F32); tmpr = Res()
        esc = sb(es, "hg_esc", [128, 3, 16], F32); escr = Res()
        nbm = sb(es, "hg_nbm", [128, 4], F32); nbmr = Res()
        d2 = sb(es, "hg_d2", [128, 4], F32)
        rs = sb(es, "hg_rs", [128, TB], F32); rsr = Res()
        tmpO = sb(es, "hg_tmpO", [128, TB], F32); tmpOr = Res()
        return (fA, fB, fC, fD, fE, fr, KtT, ktr, QtT, qtr, sgT, sgr, o_sb, osr, vsb, vr, Ktsb, Ktr, ATsb, atr,
                Spp, sppr, tmpS, tmpr, esc, escr, nbm, nbmr, d2, S, Sres, ybT, ybres, rs, rsr, tmpO, tmpOr)

    def phase_l0(self):
        P = self.P
        with ExitStack() as es:
            S = self.sb(es, "l0_S", [128, 8, 128], F32)
            Sres = RL(8)
            hhalo = self.sb(es, "l0_halo", [128, KC, 128], BF16)
            halo_res = Res()
            self.lbv = self.sb(es, "l0_lbv", [128, 8, 2], F32)
            self.lbv_res = Res()
            inv16 = self.sb(es, "l0_inv16", [128, 4, 16], F32)
            inv_res = Res()
            ex = self.sb(es, "l0_ex", [128, 4, 8], F32)
            exr = Res()
            for i in range(3):
                P.op("act", lambda E, i=i: E.activation(out=ex[:, i, :], in_=self.par(f"lb{i}", 0, 8), func=AF.Exp),
                     writes=(exr,))
            P.op("dve", lambda E: E.tensor_tensor(out=ex[:, 3, :], in0=ex[:, 0, :], in1=ex[:, 1, :], op=ALU.add),
                 reads=(exr,), writes=(exr,))
            P.op("dve", lambda E: E.tensor_tensor(out=ex[:, 3, :], in0=ex[:, 3, :], in1=ex[:, 2, :], op=ALU.add),
                 reads=(exr,), writes=(exr,))
            P.op("dve", lambda E: E.reciprocal(out=ex[:, 3, :], in_=ex[:, 3, :]), reads=(exr,), writes=(exr,))
            P.op("dve", lambda E: E.tensor_tensor(out=self.lbv[:, :, 1], in0=ex[:, 1, :], in1=ex[:, 3, :], op=ALU.mult),
                 reads=(exr,), writes=(self.lbv_res,))
            P.op("dve", lambda E: E.tensor_scalar(out=self.lbv[:, :, 0], in0=self.lbv[:, :, 1], scalar1=-1.0, scalar2=1.0,
                                                  op0=ALU.mult, op1=ALU.add),
                 reads=(self.lbv_res,), writes=(self.lbv_res,))
            for g in range(4):
                P.op("dve", lambda E, g=g: E.tensor_scalar(out=inv16[:, g, :], in0=self.par("iota16", 0, 16),
                                                           scalar1=self.par("tok0"), scalar2=float(2 ** (g + 1)),
                                                           op0=ALU.add, op1=ALU.min), writes=(inv_res,))
            P.op("dve", lambda E: E.reciprocal(out=inv16[:, :, :], in_=inv16[:, :, :]), reads=(inv_res,), writes=(inv_res,))
            P.op("dve", lambda E: E.memset(S[:, :, :], 0.0), writes=Sres)
            P.barrier()
            P.scope = "l0.pre_norm"
            with ExitStack() as es2:
                hp = self.sb(es2, "l0_hp", [128, KC, TOK], BF16)
                hpr = RL(NB)
                with ExitStack() as es3:
                    self.transpose_in(es3, self.dram["x_prev"], "ev_norm", hp, hpr, store_xt=False)
                    P.barrier()
                P.op("dve", lambda E: E.tensor_copy(out=hhalo[:, :, :], in_=hp[:, :, TOK - 128:TOK]),
                     reads=(hpr[3],), writes=(halo_res,))
                P.scope = "l0.pre_hgrn"
                B = self.hgrn_bufs2(es2, S, Sres, None, None, False)
                wb = [self.sb(es2, f"l0_wfi{i}", [128, KC * 256], BF16) for i in range(2)]
                wr = RL(2)
                self.hgrn_run(hp, hpr, wb, wr, self.dram["w_fi"], 256, {"f": 0, "i": 128}, False, B)
                P.barrier()
            P.scope = "l0.own_norm"
            with ExitStack() as es2:
                hT = self.sb(es2, "l0_hT", [128, KC, TOK], BF16)
                hres = RL(NB)
                with ExitStack() as es3:
                    self.transpose_in(es3, self.dram["x_own"], "ev_norm", hT, hres, store_xt=True)
                    P.barrier()
                P.scope = "l0.pool"
                with ExitStack() as es3:
                    yaT = self.sb(es3, "l0_yaT", [128, 8, TOK], BF16)
                    yares = [RL(NB) for _ in range(8)]
                    L = 16 + TOK
                    ub = self.sb(es3, "l0_ub", [128, L], F32); ubr = Res()
                    sA = self.sb(es3, "l0_sA", [128, L], F32); sAr = Res()
                    sB = self.sb(es3, "l0_sB", [128, L], F32); sBr = Res()
                    pT = self.sb(es3, "l0_pT", [128, 2, TOK], BF16); pTr = RL(2)
                    wpool = self.sb(es3, "l0_wpool", [128, 4 * 2 * 256], BF16); wpr = Res()
                    P.dma("pool", wpool[:, :], self.dram["w_pool"][:, :], writes=(wpr,), max_dma_last_dim=4096)
                    wb = [self.sb(es3, f"l0_wu{i}", [128, KC * 512], BF16) for i in range(2)]
                    wr = RL(2)
                    it = 0
                    for c in range(8):
                        g = c // 2
                        cc = c % 2
                        wd_ = 2 ** (g + 1)
                        if c % 4 == 0:
                            self.load_panel(wb[(c // 4) % 2][:, :], wr[(c // 4) % 2], self.dram["w_u"], c // 4, KC * 512)
                        w = wb[(c // 4) % 2]
                        wres = wr[(c // 4) % 2]
                        off = (c % 4) * 128
                        bank = self.banks[it % 4]; br = self.bank_res[it % 4]; it += 1

                        def mmh(E, w=w, off=off, bank=bank):
                            ins = None
                            for kc in range(KC):
                                ins = E.matmul(bank[:, 0:128], lhsT=w[:, kc * 512 + off:kc * 512 + off + 128],
                                               rhs=hhalo[:, kc, :], start=(kc == 0), stop=(kc == KC - 1))
                            return ins
                        P.op("pe", mmh, reads=(wres, halo_res), writes=(br,))
                        P.op("act", lambda E, bank=bank: E.copy(out=ub[:, 0:16], in_=bank[:, 112:128]),
                             reads=(br,), writes=(ubr,))
                        for tb in range(NB):
                            bank = self.banks[it % 4]; br = self.bank_res[it % 4]; it += 1

                            def mm(E, w=w, off=off, bank=bank, tb=tb):
                                ins = None
                                for kc in range(KC):
                                    ins = E.matmul(bank[:, :], lhsT=w[:, kc * 512 + off:kc * 512 + off + 128],
                                                   rhs=hT[:, kc, tb * TB:(tb + 1) * TB],
                                                   start=(kc == 0), stop=(kc == KC - 1))
                                return ins
                            P.op("pe", mm, reads=(wres, hres[tb]), writes=(br,))
                            P.op("act", lambda E, bank=bank, tb=tb: E.copy(out=ub[:, 16 + tb * TB:16 + (tb + 1) * TB],
                                                                          in_=bank[:, :]),
                                 reads=(br,), writes=(ubr,))
                        P.op("dve", lambda E: E.tensor_tensor(out=sA[:, 1:L], in0=ub[:, 1:L], in1=ub[:, 0:L - 1], op=ALU.add),
                             reads=(ubr,), writes=(sAr,))
                        sw, swr = sA, sAr
                        if wd_ >= 4:
                            P.op("dve", lambda E: E.tensor_tensor(out=sB[:, 3:L], in0=sA[:, 3:L], in1=sA[:, 1:L - 2], op=ALU.add),
                                 reads=(sAr,), writes=(sBr,))
                            sw, swr = sB, sBr
                        if wd_ >= 8:
                            P.op("dve", lambda E: E.tensor_tensor(out=sA[:, 7:L], in0=sB[:, 7:L], in1=sB[:, 3:L - 4], op=ALU.add),
                                 reads=(sBr,), writes=(sAr,))
                            sw, swr = sA, sAr
                        if wd_ >= 16:
                            P.op("dve", lambda E: E.tensor_tensor(out=sB[:, 15:L], in0=sA[:, 15:L], in1=sA[:, 7:L - 8], op=ALU.add),
                                 reads=(sAr,), writes=(sBr,))
                            sw, swr = sB, sBr
                        P.op("dve", lambda E, sw=sw, cc=cc, wd_=wd_: E.scalar_tensor_tensor(
                            out=pT[:, cc, :], in0=sw[:, 16:L], scalar=1.0 / wd_, in1=ub[:, 16:L],
                            op0=ALU.mult, op1=ALU.subtract), reads=(swr, ubr), writes=(pTr[cc],))
                        P.op("dve", lambda E, sw=sw, g=g: E.tensor_tensor(out=sw[:, 16:32], in0=sw[:, 16:32], in1=inv16[:, g, :],
                                                                          op=ALU.mult),
                             reads=(swr, inv_res), writes=(swr,))
                        P.op("dve", lambda E, sw=sw, cc=cc: E.tensor_tensor(out=pT[:, cc, 0:16], in0=sw[:, 16:32], in1=ub[:, 16:32],
                                                                            op=ALU.subtract),
                             reads=(swr, ubr), writes=(pTr[cc],))
                        if cc == 1:
                            for dc in range(2):
                                for tb in range(NB):
                                    bank = self.banks[4 + it % 2]; br = self.bank_res[4 + it % 2]; it += 1

                                    def mmp(E, g=g, dc=dc, tb=tb, bank=bank):
                                        ins = None
                                        for c2 in range(2):
                                            o_ = (g * 2 + c2) * 256 + dc * 128
                                            ins = E.matmul(bank[:, :], lhsT=wpool[:, o_:o_ + 128],
                                                           rhs=pT[:, c2, tb * TB:(tb + 1) * TB],
                                                           start=(c2 == 0), stop=(c2 == 1))
                                        return ins
                                    P.op("pe", mmp, reads=(wpr, pTr[0], pTr[1]), writes=(br,))
                                    P.op("act", lambda E, g=g, dc=dc, tb=tb, bank=bank: E.activation(
                                        out=yaT[:, g * 2 + dc, tb * TB:(tb + 1) * TB], in_=bank[:, :], func=AF.Identity,
                                        scale=self.par("pool_scale", g * 2 + dc)),
                                        reads=(br,), writes=(yares[g * 2 + dc][tb],))
                    P.scope = "l0.wout_a"
                    with ExitStack() as es4:
                        self.proj_residual(es4, self.dram["wout_a"], 8, lambda kc, tb: yaT[:, kc, tb * TB:(tb + 1) * TB],
                                           lambda kc, tb: yares[kc][tb], "l0_oa")
                        P.barrier()
                P.scope = "l0.hgrn"
                with ExitStack() as es3:
                    ybT = self.sb(es3, "l0_ybT", [128, 8, TOK], BF16)
                    ybres = [RL(NB) for _ in range(8)]
                    with ExitStack() as es4:
                        B = self.hgrn_bufs2(es4, S, Sres, ybT, ybres, True)
                        wb = [self.sb(es4, f"l0_wh{i}", [128, KC * 512], BF16) for i in range(2)]
                        wr = RL(2)
                        self.hgrn_run(hT, hres, wb, wr, self.dram["w_h"], 512,
                                      {"q": 0, "f": 128, "i": 256, "g": 384}, True, B)
                        P.barrier()
                    P.scope = "l0.wout_b"
                    with ExitStack() as es4:
                        self.proj_residual(es4, self.dram["wout_b"], 8, lambda kc, tb: ybT[:, kc, tb * TB:(tb + 1) * TB],
                                           lambda kc, tb: ybres[kc][tb], "l0_ob")
                        P.barrier()

    def fox_decl(self, scratch):
        d = {}
        d["KV"] = [scratch(f"KVd{h}", [256, TOK], BF16) for h in range(16)]
        d["KVg"] = [scratch(f"KVg{h}", [512, TOK], BF16) for h in range(16)]
        d["Fx"] = scratch("Fx", [128, 272], F32)
        d["Fg"] = scratch("Fg", [256, 272], F32)
        d["QT"] = scratch("QTd", [KC * 128, TOK], BF16)
        d["R"] = scratch("Rd", [128, 64], F32)
        return d

    def phase_fox_a(self, scratch):
        P = self.P
        self.fox_own = self.fox_decl(scratch)
        o = self.fox_own
        self.fox_res_q = Res()
        self.fox_res_f = Res()
        self.fox_res_kv = RL(16)
        self.fox_prev_f = Res()
        self.fox_prev_kv = RL(16)
        groups = [[0, 1], [2, 3], [4, 5], [6, 7]]
        with ExitStack() as es:
            hT = self.sb(es, "fx_hT", [128, KC, TOK], BF16)
            hres = RL(NB)
            with ExitStack() as es2:
                self.norm_all(es2, "od_norm", hT, lambda tb: hres[tb])
                P.barrier()
            with ExitStack() as es2:
                wfl = self.sb(es2, "fx_wfl", [128, KC * 16], BF16)
                wflr = Res()
                self.load_panel(wfl[:, :], wflr, self.dram["ffl"], 0, KC * 16)
                Floc = self.sb(es2, "fx_Floc", [128, 16, 16], F32)
                Flr = Res()
                tot = self.sb(es2, "fx_tot", [128, 16], F32)
                totr = Res()
                Rbc = self.sb(es2, "fx_Rbc", [128, 4, 16], F32)
                Rr = Res()
                zs = [self.sb(es2, f"fx_z{i}", [128, 16], F32) for i in range(2)]
                zrs = RL(2)
                P.op("dve", lambda E: E.memset(tot[:, :], 0.0), writes=(totr,))
                for t in range(16):
                    z = zs[t % 2]
                    zr = zrs[t % 2]
                    bank = self.banks[4]
                    br = self.bank_res[4]

                    def mmf(E, t=t, bank=bank):
                        ins = None
                        for kc in range(KC):
                            ins = E.matmul(bank[:, 0:16], lhsT=hT[:, kc, t * 128:(t + 1) * 128],
                                           rhs=wfl[:, kc * 16:(kc + 1) * 16], start=(kc == 0), stop=(kc == KC - 1))
                        return ins
                    P.op("pe", mmf, reads=(wflr, hres[t // 4]), writes=(br,))
                    P.op("dve", lambda E, z=z, bank=bank: E.tensor_tensor(out=z[:, :], in0=bank[:, 0:16],
                                                                          in1=self.par("b_f", 0, 16), op=ALU.add),
                         reads=(br,), writes=(zr,))
                    P.op("act", lambda E, z=z: E.activation(out=z[:, :], in_=z[:, :], func=AF.Exp, scale=-1.0),
                         reads=(zr,), writes=(zr,))
                    P.op("act", lambda E, z=z: E.activation(out=z[:, :], in_=z[:, :], func=AF.Ln, bias=self.ones_f[:, 0:1],
                                                            scale=1.0), reads=(zr,), writes=(zr,))
                    P.op("dve", lambda E, z=z: E.tensor_scalar(out=z[:, :], in0=z[:, :], scalar1=-1.0, scalar2=None,
                                                               op0=ALU.mult), reads=(zr,), writes=(zr,))
                    if t % 4 == 2:
                        P.op("dve", lambda E, t=t: E.tensor_copy(out=Rbc[:, t // 4, :], in_=tot[:, :]),
                             reads=(totr,), writes=(Rr,))
                    bc = self.banks[5]
                    bcr = self.bank_res[5]
                    bt = self.banks[6]
                    btr = self.bank_res[6]
                    P.op("pe", lambda E, z=z, bc=bc: E.matmul(bc[:, 0:16], lhsT=self.triu_f[:, :], rhs=z[:, :], start=True, stop=True),
                         reads=(zr,), writes=(bcr,))
                    P.op("pe", lambda E, z=z, bt=bt: E.matmul(bt[:, 0:16], lhsT=self.ones_f[:, :], rhs=z[:, :], start=True, stop=True),
                         reads=(zr,), writes=(btr,))
                    P.op("dve", lambda E, t=t, bc=bc: E.tensor_tensor(out=Floc[:, t, :], in0=bc[:, 0:16], in1=tot[:, :], op=ALU.add),
                         reads=(bcr, totr), writes=(Flr,))
                    P.op("dve", lambda E, bt=bt: E.tensor_tensor(out=tot[:, :], in0=bt[:, 0:16], in1=tot[:, :], op=ALU.add),
                         reads=(btr, totr), writes=(totr,))
                P.dma("sp", o["Fx"][:, 0:256], Floc[:, :, :].rearrange("p a b -> p (a b)"), reads=(Flr,),
                      writes=(self.fox_res_f,), nowaw=True)
                P.dma("sp", o["Fx"][:, 256:272], tot[:, :], reads=(totr,), writes=(self.fox_res_f,), nowaw=True)
                P.dma("sp", o["R"][:, :], Rbc[:, :, :].rearrange("p a b -> p (a b)"), reads=(Rr,),
                      writes=(self.fox_res_f,), nowaw=True)
                P.coll(o["Fx"].opt(), o["Fg"].opt(), groups, reads=(self.fox_res_f,), writes=(self.fox_prev_f,))
                wb = [self.sb(es2, f"fx_w{i}", [128, KC * 128], BF16) for i in range(2)]
                wr = RL(2)
                ost = [self.sb(es2, f"fx_o{i}", [128, TB], BF16) for i in range(3)]
                osr = RL(3)
                wv = [self.sb(es2, f"fx_wv{i}", [128, KC * 512], BF16) for i in range(2)]
                wvr = RL(2)
                vst = [self.sb(es2, f"fx_vs{i}", [128, 512], BF16) for i in range(3)]
                vsr = RL(3)
                cnt = {"it": 0, "ip": 0, "iv": 0}

                def qk_panel(src, j, isq, scale):
                    w = wb[cnt["ip"] % 2]
                    wres = wr[cnt["ip"] % 2]
                    cnt["ip"] += 1
                    self.load_panel(w[:, :], wres, src, j, KC * 128)
                    for tb in range(NB):
                        it = cnt["it"]
                        cnt["it"] += 1
                        bank = self.banks[it % 4]
                        br = self.bank_res[it % 4]
                        ob = ost[it % 3]
                        obr = osr[it % 3]

                        def mm(E, w=w, tb=tb, bank=bank):
                            ins = None
                            for kc in range(KC):
                                ins = E.matmul(bank[:, :], lhsT=w[:, kc * 128:(kc + 1) * 128],
                                               rhs=hT[:, kc, tb * TB:(tb + 1) * TB],
                                               start=(kc == 0), stop=(kc == KC - 1))
                            return ins
                        P.op("pe", mm, reads=(wres, hres[tb]), writes=(br,))
                        P.op("act", lambda E, ob=ob, bank=bank, scale=scale: E.activation(
                            out=ob[:, :], in_=bank[:, :], func=AF.Copy, scale=scale), reads=(br,), writes=(obr,))
                        if isq:
                            P.dma("sp", o["QT"][j * 128:(j + 1) * 128, tb * TB:(tb + 1) * TB], ob[:, :],
                                  reads=(obr,), writes=(self.fox_res_q,), nowaw=True)
                        else:
                            P.dma("sp", o["KV"][j][0:128, tb * TB:(tb + 1) * TB], ob[:, :],
                                  reads=(obr,), writes=(self.fox_res_kv[j],), nowaw=True)

                for j in range(KC):
                    qk_panel(self.dram["fq"], j, True, 128.0 ** -0.5)
                for pn in range(4):
                    for hh in range(4):
                        qk_panel(self.dram["fk"], pn * 4 + hh, False, 1.0)
                    w = wv[pn % 2]
                    self.load_panel(w[:, :], wvr[pn % 2], self.dram["fv"], pn, KC * 512)
                    for t in range(16):
                        it = cnt["it"]
                        cnt["it"] += 1
                        bank = self.banks[it % 4]
                        br = self.bank_res[it % 4]
                        vs = vst[it % 3]
                        vr_ = vsr[it % 3]

                        def mm(E, w=w, t=t, bank=bank):
                            ins = None
                            for kc in range(KC):
                                ins = E.matmul(bank[:, :], lhsT=hT[:, kc, t * 128:(t + 1) * 128],
                                               rhs=w[:, kc * 512:(kc + 1) * 512], start=(kc == 0), stop=(kc == KC - 1))
                            return ins
                        P.op("pe", mm, reads=(wvr[pn % 2], hres[t // 4]), writes=(br,))
                        if it % 2 == 0:
                            P.op("act", lambda E, vs=vs, bank=bank: E.copy(out=vs[:, :], in_=bank[:, :]),
                                 reads=(br,), writes=(vr_,))
                        else:
                            P.op("dve", lambda E, vs=vs, bank=bank: E.tensor_copy(out=vs[:, :], in_=bank[:, :]),
                                 reads=(br,), writes=(vr_,))
                        for hh in range(4):
                            hd = pn * 4 + hh
                            P.dma("sp", o["KV"][hd][128:256, t * 128:(t + 1) * 128],
                                  vs[:, hh * 128:(hh + 1) * 128], reads=(vr_,), writes=(self.fox_res_kv[hd],), nowaw=True)
                    for hh in range(4):
                        hd = pn * 4 + hh
                        P.coll(o["KV"][hd].opt(), o["KVg"][hd].opt(), groups, reads=(self.fox_res_kv[hd],),
                               writes=(self.fox_prev_kv[hd],))
                P.barrier_compute()

    def phase_fox_b(self, scratch):
        P = self.P
        o = self.fox_own
        with ExitStack() as es:
            aT = self.sb(es, "fb_aT", [128, KC, TOK], BF16)
            ares = RL(NB)
            with ExitStack() as es2:
                Fall = self.sb(es2, "fb_Fall", [128, 32, 16], F32)
                Fr = Res()
                Ftp = self.sb(es2, "fb_Ftp", [128, 16], F32)
                Rbc = self.sb(es2, "fb_Rbc", [128, 4, 16], F32)
                negF = self.sb(es2, "fb_negF", [128, 16, 32], F32)
                nFr = Res()
                P.dma("sp", Fall[:, 0:16, :].rearrange("p a b -> p (a b)"), o["Fg"][0:128, 0:256], reads=(self.fox_prev_f,), writes=(Fr,))
                P.dma("sp", Fall[:, 16:32, :].rearrange("p a b -> p (a b)"), o["Fx"][:, 0:256], reads=(self.fox_res_f,), writes=(Fr,), nowaw=True)
                P.dma("sp", Ftp[:, :], o["Fg"][0:128, 256:272], reads=(self.fox_prev_f,), writes=(Fr,), nowaw=True)
                P.dma("sp", Rbc[:, :, :].rearrange("p a b -> p (a b)"), o["R"][:, :], reads=(self.fox_res_f,), writes=(Fr,), nowaw=True)
                P.op("dve", lambda E: E.tensor_scalar(out=negF[:, :, :], in0=Fall[:, :, :].rearrange("p k h -> p h k"),
                                                      scalar1=-1.0, scalar2=None, op0=ALU.mult), reads=(Fr,), writes=(nFr,))
                for h in range(16):
                    P.op("dve", lambda E, h=h: E.tensor_scalar(out=negF[:, h, 0:16], in0=negF[:, h, 0:16],
                                                               scalar1=Ftp[:, h:h + 1], scalar2=self.par("prevmask"),
                                                               op0=ALU.add, op1=ALU.add), reads=(Fr, nFr), writes=(nFr,))
                KTh = [self.sb(es2, f"fb_K{i}", [128, 2 * TOK], BF16) for i in range(2)]
                Vh = [self.sb(es2, f"fb_V{i}", [128, 2 * TOK], BF16) for i in range(2)]
                QTh = [self.sb(es2, f"fb_Q{i}", [128, TOK], BF16) for i in range(2)]
                hr = RL(2)
                biasq = [self.sb(es2, f"fb_b{i}", [128, 32], F32) for i in range(2)]
                bqr = RL(2)
                NPB = 6
                pb = [self.sb(es2, f"fb_p{i}", [128, TB], BF16) for i in range(NPB)]
                pbr = RL(NPB)
                rinv = [self.sb(es2, f"fb_ri{i}", [128, TB], F32) for i in range(2)]
                rir = RL(2)
                psum2 = [self.sb(es2, f"fb_p2{i}", [128, TB], BF16) for i in range(2)]
                psum2r = RL(2)

                def load_head(h):
                    K_, V_, Q_, hres_ = KTh[h % 2], Vh[h % 2], QTh[h % 2], hr[h % 2]
                    rows = slice(h * 128, (h + 1) * 128)
                    P.dma("sp", K_[:, 0:TOK], o["KVg"][h][0:128, :], reads=(self.fox_prev_kv[h],), writes=(hres_,))
                    P.dma("sp", K_[:, TOK:2 * TOK], o["KV"][h][0:128, :], reads=(self.fox_res_kv[h],), writes=(hres_,), nowaw=True)
                    P.dma("sp", V_[:, 0:TOK], o["KVg"][h][128:256, :], reads=(self.fox_prev_kv[h],), writes=(hres_,), nowaw=True)
                    P.dma("sp", V_[:, TOK:2 * TOK], o["KV"][h][128:256, :], reads=(self.fox_res_kv[h],), writes=(hres_,), nowaw=True)
                    P.dma("sp", Q_[:, :], o["QT"][rows, :], reads=(self.fox_res_q,), writes=(hres_,), nowaw=True)

                units = []
                g = 0
                for h in range(16):
                    for qb in range(NB):
                        nk = 16 + qb * 4 + 4
                        for kt0 in range(0, nk, 2):
                            kts = []
                            for kt in (kt0, kt0 + 1):
                                j = kt - 16 - qb * 4
                                kts.append((kt, 128 * j if j > 0 else 0, j >= 0))
                            units.append((h, qb, kts, kt0 == 0, kt0 + 2 == nk, g))
                        g += 1
                NU = len(units)
                LA = 1
                bank8 = self.bank_bf[:, :].bitcast(F32)
                sbanks = [(self.banks[0], self.bank_res[0]), (self.banks[1], self.bank_res[1]),
                          (self.banks[2], self.bank_res[2]), (self.banks[3], self.bank_res[3])]
                obanks = [(self.banks[4], self.bank_res[4]), (self.banks[5], self.bank_res[5])]
                lbanks = [(self.banks[6], self.bank_res[6]), (bank8, self.bank_bf_res)]

                def emit_S(u):
                    h, qb, kts, first, last, g = units[u]
                    K_, Q_, hres_ = KTh[h % 2], QTh[h % 2], hr[h % 2]

                    def mm(E):
                        ins = None
                        for i, (kt, off, diag) in enumerate(kts):
                            bs = sbanks[(u % 2) * 2 + i][0]
                            ins = E.matmul(bs[:, off:TB], lhsT=K_[:, kt * 128:(kt + 1) * 128],
                                           rhs=Q_[:, qb * TB + off:(qb + 1) * TB], start=True, stop=True)
                        return ins
                    P.op("pe", mm, reads=(hres_,), writes=[sbanks[(u % 2) * 2 + i][1] for i in range(2)])

                def emit_bias(g):
                    h, qb = g // NB, g % NB
                    bq_ = biasq[g % 2]
                    P.op("dve", lambda E: E.tensor_scalar(
                        out=bq_[:, :], in0=negF[:, h, :], scalar1=Rbc[:, qb, h:h + 1], scalar2=None, op0=ALU.add),
                        reads=(nFr, Fr), writes=(bqr[g % 2],))

                pendL = []
                pendF = []
                load_head(0)
                load_head(1)
                for u in range(min(LA, NU)):
                    emit_S(u)
                for u in range(NU):
                    h, qb, kts, first, last, g = units[u]
                    V_, hres_ = Vh[h % 2], hr[h % 2]
                    bq_ = biasq[g % 2]
                    bqr_ = bqr[g % 2]
                    bo, bor = obanks[g % 2]
                    bl, blr = lbanks[g % 2]
                    ps = [pb[(u % 3) * 2 + i] for i in range(2)]
                    prs = [pbr[(u % 3) * 2 + i] for i in range(2)]
                    sb_ = [sbanks[(u % 2) * 2 + i] for i in range(2)]
                    if first:
                        if qb == 0 and h >= 1 and h + 1 < 16:
                            load_head(h + 1)
                        if g == 0:
                            emit_bias(0)
                        if g + 1 < 64:
                            emit_bias(g + 1)

                    def ex(E, kts=kts, ps=ps, sb_=sb_, bq_=bq_):
                        ins = None
                        for i, (kt, off, diag) in enumerate(kts):
                            ins = E.activation(out=ps[i][:, off:TB], in_=sb_[i][0][:, off:TB], func=AF.Exp,
                                               bias=bq_[:, kt:kt + 1], scale=1.0)
                        return ins
                    P.op("act", ex, reads=(sb_[0][1], sb_[1][1], bqr_), writes=prs)
                    if any(d for _, _, d in kts):
                        def mk(E, kts=kts, ps=ps):
                            ins = None
                            for i, (kt, off, diag) in enumerate(kts):
                                if diag:
                                    ins = E.tensor_tensor(out=ps[i][:, off:off + 128], in0=ps[i][:, off:off + 128],
                                                          in1=self.triu_b[:, :], op=ALU.mult)
                            return ins
                        P.op("pool", mk, reads=prs, writes=prs)

                    same = kts[0][1] == kts[1][1] and not kts[0][2] and not kts[1][2]
                    if same:
                        p2 = psum2[u % 2]
                        p2r = psum2r[u % 2]
                        P.op("dve", lambda E, ps=ps, p2=p2: E.tensor_tensor(out=p2[:, :], in0=ps[0][:, :], in1=ps[1][:, :],
                                                                            op=ALU.add), reads=prs, writes=(p2r,))

                    def mmo(E, V_=V_, kts=kts, ps=ps, bo=bo, first=first, last=last):
                        ins = None
                        for i, (kt, off, diag) in enumerate(kts):
                            ins = E.matmul(bo[:, off:TB], lhsT=V_[:, kt * 128:(kt + 1) * 128], rhs=ps[i][:, off:TB],
                                           start=(first and i == 0), stop=(last and i == 1))
                        return ins
                    if u + LA < NU:
                        emit_S(u + LA)
                    P.op("pe", mmo, reads=[hres_] + prs, writes=(bor,))
                    if pendL:
                        pendL.pop()()
                    if same:
                        pendL.append(lambda p2=p2, p2r=p2r, bl=bl, blr=blr, first=first, last=last: P.op(
                            "pe", lambda E: E.matmul(bl[:, :], lhsT=self.ones_bf[:, :], rhs=p2[:, :], start=first, stop=last),
                            reads=(p2r,), writes=(blr,)))
                    else:
                        def mml(E, kts=kts, ps=ps, bl=bl, first=first, last=last):
                            ins = None
                            for i, (kt, off, diag) in enumerate(kts):
                                ins = E.matmul(bl[:, off:TB], lhsT=self.ones_bf[:, :], rhs=ps[i][:, off:TB],
                                               start=(first and i == 0), stop=(last and i == 1))
                            return ins
                        pendL.append(lambda mml=mml, prs=prs, blr=blr: P.op("pe", mml, reads=prs, writes=(blr,)))
                    for fu, fn in list(pendF):
                        if fu <= u:
                            pendF.remove((fu, fn))
                            fn()
                    if last:
                        ri = rinv[g % 2]
                        rr_ = rir[g % 2]

                        def fin(h=h, qb=qb, bo=bo, bor=bor, bl=bl, blr=blr, ri=ri, rr_=rr_):
                            P.op("dve", lambda E: E.reciprocal(out=ri[:, :], in_=bl[:, :]), reads=(blr,), writes=(rr_,))
                            P.op("dve", lambda E: E.tensor_tensor(
                                out=aT[:, h, qb * TB:(qb + 1) * TB], in0=bo[:, :], in1=ri[:, :], op=ALU.mult),
                                reads=(bor, rr_), writes=(ares[qb],))
                        pendF.append((u + 4, fin))
                while pendL:
                    pendL.pop()()
                for fu, fn in pendF:
                    fn()
                self.proj_residual(es2, self.dram["fo"], KC, lambda kc, tb: aT[:, kc, tb * TB:(tb + 1) * TB],
                                   lambda kc, tb: ares[tb], "fb_o")
                P.barrier()

    def phase_final(self):
        P = self.P
        out = self.dram["out"]
        with ExitStack() as es:
            stage = [self.sb(es, f"o_st{i}", [128, KC, TB], F32) for i in range(2)]
            sres = RL(2)
            yT = [self.sb(es, f"o_y{i}", [128, KC, TB], F32) for i in range(2)]
            yres = RL(2)
            ot = [self.sb(es, f"o_t{i}", [128, D], F32) for i in range(2)]
            otr = RL(2)
            for tb in range(NB):
                st = stage[tb % 2]
                y = yT[tb % 2]
                self.load_xt_block(st, sres[tb % 2], tb * TB, TB, [self.xt_res[kc][tb] for kc in range(KC)])
                self.norm_block(lambda kc, st=st: st[:, kc, :], sres[tb % 2], "final_norm",
                                lambda kc, y=y: y[:, kc, :], yres[tb % 2], TB,
                                self.banks[4 + tb % 2], self.bank_res[4 + tb % 2])
                for tt in range(4):
                    t = tb * 4 + tt
                    o = ot[t % 2]
                    for q in range(4):
                        bank = self.banks[q]

                        def tr(E, y=y, q=q, tt=tt, bank=bank):
                            ins = None
                            for i in range(4):
                                kc = q * 4 + i
                                ins = E.transpose(out=bank[:, i * 128:(i + 1) * 128],
                                                  in_=y[:, kc, tt * 128:(tt + 1) * 128], identity=self.ident_f[:, :])
                            return ins
                        P.op("pe", tr, reads=(yres[tb % 2],), writes=(self.bank_res[q],))
                        if q % 2 == 0:
                            P.op("dve", lambda E, o=o, q=q, bank=bank: E.tensor_copy(
                                out=o[:, q * 512:(q + 1) * 512], in_=bank[:, :]),
                                reads=(self.bank_res[q],), writes=(otr[t % 2],))
                        else:
                            P.op("act", lambda E, o=o, q=q, bank=bank: E.copy(
                                out=o[:, q * 512:(q + 1) * 512], in_=bank[:, :]),
                                reads=(self.bank_res[q],), writes=(otr[t % 2],))
                    P.dma("sp", out[t * 128:(t + 1) * 128, :], o[:, :], reads=(otr[t % 2],), writes=(self.out_res,))
        P.barrier()

    def build(self, stages, ext_in=(), ext_out=()):
        nc = self.nc
        P = self.P

        def scratch(name, shape, dt=F32):
            if name in ext_in:
                return self.din(name, shape, dt)
            if name in ext_out:
                return self.dout(name, shape, dt)
            return self.dint(name, shape, dt)
        self.din("params", [128, NPAR])
        self.din("ident", [128, 128])
        self.din("triu", [128, 128])
        if "l0" in stages:
            self.din("x_own", [TOK, D])
            self.din("x_prev", [TOK, D])
            self.din("w_fi", [8 * 128, KC * 256])
            self.din("w_u", [2 * 128, KC * 512])
            self.din("w_h", [8 * 128, KC * 512])
            self.din("w_pool", [128, 2048])
            self.din("wout_a", [KC * 128, 8 * 128])
            self.din("wout_b", [KC * 128, 8 * 128])
        for l in range(2):
            if f"xa{l}" in stages:
                self.din("mem", [NMEM, D]) if "mem" not in self.dram else None
                self.din(f"wkvK{l}", [KC * 128, KC * 128])
                self.din(f"wkvV{l}", [4 * 128, KC * 512])
                self.din(f"wq{l}", [KC * 128, KC * 128])
                self.din(f"wo{l}", [KC * 128, KC * 128])
            if f"ffn{l}" in stages:
                self.din(f"wgu{l}", [FC * 128, KC * 256])
                self.din(f"wd{l}", [KC * 128, FC * 128])
        if "fox_a" in stages:
            self.din("fq", [KC * 128, KC * 128])
            self.din("fk", [KC * 128, KC * 128])
            self.din("fv", [4 * 128, KC * 512])
            self.din("ffl", [128, KC * 16])
        if "fox_b" in stages:
            self.din("fo", [KC * 128, KC * 128])
        if "XT_in" in ext_in:
            self.din("XT_in", [D, TOK])
        scratch("XT", [D, TOK])
        if "final" in stages:
            self.dout("out", [TOK, D])
        self.xt_res = [RL(NB) for _ in range(KC)]
        self.out_res = Res()

        with ExitStack() as es:
            for nm in P.semnames:
                P.sems[nm] = es.enter_context(nc.semaphore(f"s_{nm}"))
            self.banks = [es.enter_context(nc.psum_tensor(f"bank{i}", [128, 512], F32)) for i in range(7)]
            self.bank_res = RL(7)
            self.bank_bf = es.enter_context(nc.psum_tensor("bankbf", [128, 1024], BF16))
            self.bank_bf_res = Res()
            self.bank_u_res = Res()
            self.params = self.sb(es, "params", [128, NPAR], F32)
            self.ident_f = self.sb(es, "ident_f", [128, 128], F32)
            self.ident_b = self.sb(es, "ident_b", [128, 128], BF16)
            self.triu_f = self.sb(es, "triu_f", [128, 128], F32)
            self.triu_b = self.sb(es, "triu_b", [128, 128], BF16)
            self.triu4 = self.sb(es, "triu4", [128, 4, 128], F32)
            self.ones_bf = self.sb(es, "ones_bf", [128, 128], BF16)
            self.ones_f = self.sb(es, "ones_f", [128, 128], F32)
            self.eps_col = self.sb(es, "eps_col", [128, 1], F32)
            self.sq = [self.sb(es, f"sq{i}", [128, TB], BF16) for i in range(4)]
            self.sq_res = RL(4)
            self.rstd = self.sb(es, "rstd", [128, TB], F32)
            self.rstd_res = Res()
            cres = Res()
            P.dma("sp", self.params[:, :], self.dram["params"][:, :], writes=(cres,))
            P.dma("sp", self.ident_f[:, :], self.dram["ident"][:, :], writes=(cres,))
            P.dma("pool", self.ident_b[:, :], self.dram["ident"][:, :], writes=(cres,))
            P.dma("sp", self.triu_f[:, :], self.dram["triu"][:, :], writes=(cres,))
            P.dma("pool", self.triu_b[:, :], self.dram["triu"][:, :], writes=(cres,))
            for i4 in range(4):
                P.dma("sp", self.triu4[:, i4, :], self.dram["triu"][:, :], writes=(cres,), nowaw=True)
            P.op("dve", lambda E: E.memset(self.ones_bf[:, :], 1.0), writes=(cres,))
            P.op("dve", lambda E: E.memset(self.ones_f[:, :], 1.0), writes=(cres,))
            P.op("dve", lambda E: E.memset(self.eps_col[:, :], EPS), writes=(cres,))
            if "XT_in" in ext_in:
                P.dma("sp", self.dram["XT"][:, :], self.dram["XT_in"][:, :],
                      writes=[r for row in self.xt_res for r in row])
            P.barrier()

            for st in stages:
                P.scope = st
                if st == "l0":
                    self.phase_l0()
                elif st.startswith("xa"):
                    self.phase_xattn(int(st[2:]))
                elif st.startswith("ffn"):
                    self.phase_ffn(int(st[3:]))
                elif st == "fox_a":
                    self.phase_fox_a(scratch)
                elif st == "fox_b":
                    self.phase_fox_b(scratch)
                elif st == "final":
                    self.phase_final()
                else:
                    raise ValueError(st)
            with nc.Block() as block:
                P.emit(block)
        return nc


def prep_inputs(inputs):
    common = {}
    par = np.zeros((128, NPAR), np.float32)

    def put(name, arr2d):
        o, n = PCOL[name]
        par[:, o:o + n] = arr2d
    put("ev_norm", colvec(inputs["ev_norm"][0]))
    put("xa_norm0", colvec(inputs["xa_norm"][0]))
    put("xa_norm1", colvec(inputs["xa_norm"][1]))
    put("ffn_norm0", colvec(inputs["ffn_norm"][0]))
    put("ffn_norm1", colvec(inputs["ffn_norm"][1]))
    put("od_norm", colvec(inputs["od_norm"][0]))
    put("final_norm", colvec(inputs["final_norm"]))
    put("mem_norm0", colvec(inputs["xa_mem_norm"][0]))
    put("mem_norm1", colvec(inputs["xa_mem_norm"][1]))
    put("pool_scale", colvec(inputs["ev_pool_scale"][0]))
    put("hg_norm", colvec(inputs["ev_hg_norm"][0]))
    for i in range(3):
        put(f"lb{i}", colvec(inputs["lb_table"][i]))
    put("b_f", np.tile(inputs["od_b_f"][0][None, :], (128, 1)))
    put("iota16", np.tile(np.arange(1, 17, dtype=np.float32)[None, :], (128, 1)))
    common["params"] = par
    common["ident"] = np.eye(128, dtype=np.float32)
    common["triu"] = np.triu(np.ones((128, 128), np.float32))
    W = inputs["ev_w_in"][0]
    common["w_u"] = panelize(W[:, 0:1024], 512)
    q, f, i_, g = (W[:, 1024 + k * 1024:1024 + (k + 1) * 1024].reshape(D, 8, 128) for k in range(4))
    common["w_h"] = panelize(np.stack([q, f, i_, g], axis=2).reshape(D, 8 * 512), 512)
    common["w_fi"] = panelize(np.stack([f, i_], axis=2).reshape(D, 8 * 256), 256)
    common["w_pool"] = np.ascontiguousarray(
        inputs["ev_w_pool"][0].reshape(4, 2, 128, 256).transpose(2, 0, 1, 3).reshape(128, 2048))
    wo = inputs["ev_w_out"][0]
    common["wout_a"] = panelize(wo[0:1024], 128)
    common["wout_b"] = panelize(wo[1024:2048], 128)
    for l in range(2):
        wkv = inputs["xa_wkv"][l]
        common[f"wkvK{l}"] = panelize(wkv[:, 0:D], 128)
        common[f"wkvV{l}"] = panelize(wkv[:, D:2 * D], 512)
        common[f"wq{l}"] = panelize(inputs["xa_wq"][l], 128)
        common[f"wo{l}"] = panelize(inputs["xa_wo"][l], 128)
        wg = inputs["ffn_w_gate"][l]
        wu = inputs["ffn_w_up"][l]
        wgu = np.concatenate([wg.reshape(D, FC, 1, 128), wu.reshape(D, FC, 1, 128)], axis=2).reshape(D, FC * 256)
        common[f"wgu{l}"] = panelize(wgu, 256)
        common[f"wd{l}"] = panelize(inputs["ffn_w_down"][l], 128)
    W1 = inputs["od_w_in"][0]
    common["fq"] = panelize(W1[:, 0:D], 128)
    common["fk"] = panelize(W1[:, D:2 * D], 128)
    common["fv"] = panelize(W1[:, 2 * D:3 * D], 512)
    common["ffl"] = panelize(W1[:, 3 * D:3 * D + 16], 16)
    common["fo"] = panelize(inputs["od_w_out"][0], 128)
    return common


def core_inputs(inputs, common, c):
    b, half = c // 2, c % 2
    m = dict(common)
    par = common["params"].copy()
    par[:, PCOL["tok0"][0]] = half * TOK
    par[:, PCOL["prevmask"][0]] = 0.0 if half == 1 else -30000.0
    m["params"] = par
    x = inputs["x"]
    m["x_own"] = np.ascontiguousarray(x[b, half * TOK:(half + 1) * TOK])
    m["x_prev"] = np.ascontiguousarray(x[b, 0:TOK]) if half == 1 else np.zeros((TOK, D), np.float32)
    m["mem"] = np.ascontiguousarray(inputs["mem"][b])
    return m


STAGES = ["l0", "xa0", "ffn0", "fox_a", "fox_b", "xa1", "ffn1", "final"]


def kernel(**inputs):
    inputs = {k: np.asarray(v) for k, v in inputs.items()}
    common = prep_inputs(inputs)
    maps = [core_inputs(inputs, common, c) for c in range(NCORES)]
    b = Builder(0, 0)
    nc = b.build(STAGES)
    names = set(b.dram.keys())
    in_maps = [{k: v for k, v in m.items() if k in names} for m in maps]
    res = run_bass_kernel_spmd(nc, in_maps, core_ids=list(range(NCORES))).results
    x = inputs["x"]
    out = np.zeros(x.shape, np.float32)
    for c in range(NCORES):
        bb, half = c // 2, c % 2
        out[bb, half * TOK:(half + 1) * TOK] = res[c]["out"]
    return out
```

```python
import numpy as np
from contextlib import ExitStack
import concourse.bass as bass
import concourse.mybir as mybir
from concourse.bass_utils import run_bass_kernel_spmd

F32 = mybir.dt.float32
BF16 = mybir.dt.bfloat16
AF = mybir.ActivationFunctionType
ALU = mybir.AluOpType

D = 2048
KC = 16
TOK = 2048
NB = 4
TB = 512
DFF = 5632
FC = DFF // 128
NMEM = 256
EPS = 1e-6
NCORES = 8

ENG = ("pe", "act", "dve", "pool", "sp")


class Res:
    __slots__ = ("w", "r")

    def __init__(self):
        self.w = {}
        self.r = {}


def RL(n):
    return [Res() for _ in range(n)]


class Prog:
    def __init__(self, nc):
        self.nc = nc
        self.streams = {e: [] for e in ENG}
        self.cnt = {e: 0 for e in ENG}
        self.seen = {e: {} for e in ENG}
        self.nslots = {"sp": 8, "pool": 8}
        self.slotnext = {"sp": 0, "pool": 0}
        self.slotval = {}
        for q, n in self.nslots.items():
            for i in range(n):
                self.slotval[f"{q}{i}"] = 0
        self.scope = "init"
        self.ccval = 0
        self.ccnames = [f"cc{i}" for i in range(20)]
        self.semnames = list(ENG) + list(self.slotval) + self.ccnames
        self.sems = {}

    def _wait(self, eng, ev):
        s, v = ev
        if v <= 0 or self.seen[eng].get(s, 0) >= v or (eng == "pe" and s == "pe"):
            return
        self.seen[eng][s] = v
        self.streams[eng].append(("w", s, v, self.scope))

    def _deps(self, eng, reads, writes, nowaw):
        for r in reads:
            for ev in r.w.items():
                self._wait(eng, ev)
        for w in writes:
            if not nowaw:
                for ev in w.w.items():
                    self._wait(eng, ev)
            for ev in w.r.items():
                self._wait(eng, ev)

    @staticmethod
    def _commit(ev, reads, writes):
        for r in reads:
            r.r[ev[0]] = ev[1]
        for w in writes:
            w.w[ev[0]] = ev[1]

    def op(self, eng, fn, reads=(), writes=(), nowaw=False):
        self._deps(eng, reads, writes, nowaw)
        self.cnt[eng] += 1
        ev = (eng, self.cnt[eng])
        self.streams[eng].append(("i", fn, eng, 1, self.scope))
        self._commit(ev, reads, writes)

    def dma(self, q, out, in_, reads=(), writes=(), nowaw=False, **kw):
        i = self.slotnext[q]
        self.slotnext[q] = (i + 1) % self.nslots[q]
        s = f"{q}{i}"
        self._wait(q, (s, self.slotval[s]))
        self._deps(q, reads, writes, nowaw)
        self.slotval[s] += 16
        ev = (s, self.slotval[s])
        self.streams[q].append(("i", lambda E: E.dma_start(out=out, in_=in_, **kw), s, 16, self.scope))
        self._commit(ev, reads, writes)

    def coll(self, ins, outs, groups, reads=(), writes=()):
        q = "pool"
        self._deps(q, reads, writes, False)
        name = self.ccnames[self.ccval]
        self.ccval += 1
        ev = (name, 1)
        self.streams[q].append(("c", lambda E: E.collective_compute(
            "AllGather", ALU.bypass, replica_groups=groups, ins=[ins], outs=[outs]), name, None, self.scope))
        self._commit(ev, reads, writes)

    def barrier_compute(self):
        cur = [(e, self.cnt[e]) for e in ENG] + list(self.slotval.items())
        for e in ENG:
            for ev in cur:
                self._wait(e, ev)

    def barrier(self):
        cur = [(e, self.cnt[e]) for e in ENG] + list(self.slotval.items()) + [(n, 1) for n in self.ccnames[:self.ccval]]
        for e in ENG:
            for ev in cur:
                self._wait(e, ev)

    def emit(self, block):
        self.barrier()
        sems = self.sems

        def run(E, name):
            items = self.streams[name]
            i = 0
            while i < len(items):
                lab = items[i][-1]
                j = i
                while j < len(items) and items[j][-1] == lab:
                    j += 1
                with self.nc.named_scope(lab):
                    for it in items[i:j]:
                        if it[0] == "w":
                            E.wait_ge(sems[it[1]], it[2])
                        elif it[0] == "c":
                            it[1](E).then_inc(sems[it[2]])
                        else:
                            it[1](E).then_inc(sems[it[2]], it[3])
                i = j

        @block.tensor
        def _(E):
            run(E, "pe")

        @block.scalar
        def _(E):
            run(E, "act")

        @block.vector
        def _(E):
            run(E, "dve")

        @block.gpsimd
        def _(E):
            run(E, "pool")

        @block.sync
        def _(E):
            run(E, "sp")


def panelize(W, pc):
    K, N = W.shape
    kc = K // 128
    npan = N // pc
    return np.ascontiguousarray(
        W.reshape(kc, 128, npan, pc).transpose(2, 1, 0, 3).reshape(npan * 128, kc * pc))


def colvec(v):
    return np.ascontiguousarray(v.reshape(-1, 128).T)


PCOL = {}
_off = 0
for _nm, _n in (("ev_norm", 16), ("xa_norm0", 16), ("xa_norm1", 16), ("ffn_norm0", 16), ("ffn_norm1", 16),
                ("od_norm", 16), ("final_norm", 16), ("mem_norm0", 16), ("mem_norm1", 16),
                ("pool_scale", 8), ("hg_norm", 1), ("lb0", 8), ("lb1", 8), ("lb2", 8),
                ("b_f", 16), ("tok0", 1), ("prevmask", 1), ("iota16", 16)):
    PCOL[_nm] = (_off, _n)
    _off += _n
NPAR = _off


class Builder:
    def __init__(self, stage_lo, stage_hi, dbg=False):
        self.stage_lo = stage_lo
        self.stage_hi = stage_hi
        self.dbg = dbg
        self.nc = bass.Bass("TRN2", target_bir_lowering=False)
        self.P = Prog(self.nc)
        self.dram = {}

    def din(self, name, shape, dt=F32):
        t = self.nc.dram_tensor(name, list(shape), dt, kind="ExternalInput").ap()
        self.dram[name] = t
        return t

    def dout(self, name, shape, dt=F32):
        t = self.nc.dram_tensor(name, list(shape), dt, kind="ExternalOutput").ap()
        self.dram[name] = t
        return t

    def dint(self, name, shape, dt=F32):
        t = self.nc.dram_tensor(name, list(shape), dt, kind="Internal").ap()
        self.dram[name] = t
        return t

    def sb(self, es, name, shape, dt):
        self._uid = getattr(self, "_uid", 0) + 1
        return es.enter_context(self.nc.sbuf_tensor(f"sb{self._uid}_{name}", list(shape), dt))

    def par(self, name, j=0, n=1):
        o, _ = PCOL[name]
        return self.params[:, o + j:o + j + n]

    def load_panel(self, wbuf_ap, wres, src, pi, width):
        self.P.dma("pool", wbuf_ap, src[pi * 128:(pi + 1) * 128, 0:width], reads=(), writes=(wres,),
                   max_dma_last_dim=4096)

    def norm_block(self, xs, xs_res, gname, h_out, h_res, n, bank, bank_res):
        P = self.P
        for kc in range(KC):
            sq = self.sq[kc % 4]
            sqr = self.sq_res[kc % 4]
            P.op("act", lambda E, kc=kc, sq=sq: E.activation(out=sq[:, 0:n], in_=xs(kc), func=AF.Square),
                 reads=(xs_res,), writes=(sqr,))
            P.op("pe", lambda E, kc=kc, sq=sq: E.matmul(bank[:, 0:n], lhsT=self.ones_bf[:, :], rhs=sq[:, 0:n],
                                                        start=(kc == 0), stop=(kc == KC - 1)),
                 reads=(sqr,), writes=(bank_res,))
        P.op("act", lambda E: E.activation(out=self.rstd[:, 0:n], in_=bank[:, 0:n], func=AF.Sqrt,
                                           bias=self.eps_col[:, 0:1], scale=1.0 / D),
             reads=(bank_res,), writes=(self.rstd_res,))
        P.op("dve", lambda E: E.reciprocal(out=self.rstd[:, 0:n], in_=self.rstd[:, 0:n]),
             reads=(self.rstd_res,), writes=(self.rstd_res,))
        for kc in range(KC):
            P.op("dve", lambda E, kc=kc: E.scalar_tensor_tensor(
                out=h_out(kc), in0=xs(kc), scalar=self.par(gname, kc), in1=self.rstd[:, 0:n],
                op0=ALU.mult, op1=ALU.mult),
                reads=(xs_res, self.rstd_res), writes=(h_res,))

    def load_xt_block(self, stage, stage_res, t0, n, xt_res_list):
        XT = self.dram["XT"]
        for kc in range(KC):
            self.P.dma("sp", stage[:, kc, 0:n], XT[kc * 128:(kc + 1) * 128, t0:t0 + n],
                       reads=(xt_res_list[kc],), writes=(stage_res,), nowaw=(kc > 0))

    def proj_residual(self, es, wsrc, kchunks, act, act_res_fn, tag):
        P = self.P
        XT = self.dram["XT"]
        width = kchunks * 128
        wb = [self.sb(es, f"{tag}_w{i}", [128, width], BF16) for i in range(2)]
        wr = RL(2)
        NXB, LA = 6, 3
        xb = [self.sb(es, f"{tag}_x{i}", [128, TB], F32) for i in range(NXB)]
        xr = RL(NXB)
        iters = [(j, tb) for j in range(KC) for tb in range(NB)]

        def load(it):
            j, tb = iters[it]
            P.dma("sp", xb[it % NXB][:, :], XT[j * 128:(j + 1) * 128, tb * TB:(tb + 1) * TB],
                  reads=(self.xt_res[j][tb],), writes=(xr[it % NXB],))
        for it in range(min(LA, len(iters))):
            load(it)
        for it, (j, tb) in enumerate(iters):
            if tb == 0:
                self.load_panel(wb[j % 2][:, :], wr[j % 2], wsrc, j, width)
            bank = self.banks[it % 4]
            br = self.bank_res[it % 4]
            x = xb[it % NXB]
            xres = xr[it % NXB]

            def mm(E, j=j, tb=tb, bank=bank):
                ins = None
                for kc in range(kchunks):
                    ins = E.matmul(bank[:, :], lhsT=wb[j % 2][:, kc * 128:(kc + 1) * 128], rhs=act(kc, tb),
                                   start=(kc == 0), stop=(kc == kchunks - 1))
                return ins
            P.op("pe", mm, reads=[wr[j % 2]] + [act_res_fn(kc, tb) for kc in range(kchunks)], writes=(br,))
            P.op("dve", lambda E, x=x, bank=bank: E.tensor_tensor(out=x[:, :], in0=bank[:, :], in1=x[:, :],
                                                                  op=ALU.add),
                 reads=(br, xres), writes=(xres,))
            if it + LA < len(iters):
                load(it + LA)
            P.dma("sp", XT[j * 128:(j + 1) * 128, tb * TB:(tb + 1) * TB], x[:, :],
                  reads=(xres,), writes=(self.xt_res[j][tb],))

    def norm_all(self, es, gname, hT, h_res_fn):
        stage = [self.sb(es, f"nst{i}", [128, KC, TB], F32) for i in range(2)]
        sres = RL(2)
        for tb in range(NB):
            st = stage[tb % 2]
            self.load_xt_block(st, sres[tb % 2], tb * TB, TB, [self.xt_res[kc][tb] for kc in range(KC)])
            self.norm_block(lambda kc, st=st: st[:, kc, :], sres[tb % 2], gname,
                            lambda kc, tb=tb: hT[:, kc, tb * TB:(tb + 1) * TB], h_res_fn(tb), TB,
                            self.banks[4 + tb % 2], self.bank_res[4 + tb % 2])

    def transpose_in(self, es, x, gname=None, hT=None, hres=None, store_xt=True):
        P = self.P
        XT = self.dram["XT"]
        xin = [self.sb(es, f"xin{i}", [128, D], F32) for i in range(2)]
        xin_r = RL(2)
        st = [self.sb(es, f"xst{i}", [128, KC, TB], F32) for i in range(2)]
        st_r = RL(2)
        for tb in range(NB):
            s = st[tb % 2]
            for tt in range(4):
                t = tb * 4 + tt
                xi = xin[t % 2]
                P.dma("sp", xi[:, :], x[t * 128:(t + 1) * 128, :], reads=(), writes=(xin_r[t % 2],))
                for q in range(4):
                    bank = self.banks[q]

                    def tr(E, xi=xi, q=q, bank=bank):
                        ins = None
                        for i in range(4):
                            kc = q * 4 + i
                            ins = E.transpose(out=bank[:, i * 128:(i + 1) * 128],
                                              in_=xi[:, kc * 128:(kc + 1) * 128], identity=self.ident_f[:, :])
                        return ins
                    P.op("pe", tr, reads=(xin_r[t % 2],), writes=(self.bank_res[q],))
                    if q % 2 == 0:
                        P.op("dve", lambda E, s=s, q=q, tt=tt, bank=bank: E.tensor_copy(
                            out=s[:, q * 4:(q + 1) * 4, tt * 128:(tt + 1) * 128],
                            in_=bank[:, :].rearrange("p (a b) -> p a b", a=4)),
                            reads=(self.bank_res[q],), writes=(st_r[tb % 2],))
                    else:
                        P.op("act", lambda E, s=s, q=q, tt=tt, bank=bank: E.copy(
                            out=s[:, q * 4:(q + 1) * 4, tt * 128:(tt + 1) * 128],
                            in_=bank[:, :].rearrange("p (a b) -> p a b", a=4)),
                            reads=(self.bank_res[q],), writes=(st_r[tb % 2],))
            if store_xt:
                for kc in range(KC):
                    P.dma("sp", XT[kc * 128:(kc + 1) * 128, tb * TB:(tb + 1) * TB], s[:, kc, :],
                          reads=(st_r[tb % 2],), writes=(self.xt_res[kc][tb],))
            if hT is not None:
                self.norm_block(lambda kc, s=s: s[:, kc, :], st_r[tb % 2], gname,
                                lambda kc, tb=tb: hT[:, kc, tb * TB:(tb + 1) * TB], hres[tb], TB,
                                self.banks[4 + tb % 2], self.bank_res[4 + tb % 2])

    def phase_ffn(self, l):
        P = self.P
        wg = self.dram[f"wgu{l}"]
        wd = self.dram[f"wd{l}"]
        for sbk in range(2):
            with ExitStack() as es:
                hT = self.sb(es, "f_hT", [128, KC, 1024], BF16)
                h_res = RL(2)
                aT = self.sb(es, "f_aT", [128, FC, 1024], BF16)
                a_res = [RL(2) for _ in range(FC)]
                with ExitStack() as es2:
                    stage = [self.sb(es2, f"f_st{i}", [128, KC, TB], F32) for i in range(2)]
                    sres = RL(2)
                    for hb in range(2):
                        tb = sbk * 2 + hb
                        self.load_xt_block(stage[hb], sres[hb], tb * TB, TB,
                                           [self.xt_res[kc][tb] for kc in range(KC)])
                        self.norm_block(lambda kc, st=stage[hb]: st[:, kc, :], sres[hb], f"ffn_norm{l}",
                                        lambda kc, hb=hb: hT[:, kc, hb * TB:(hb + 1) * TB], h_res[hb], TB,
                                        self.banks[4 + hb], self.bank_res[4 + hb])
                    P.barrier()
                with ExitStack() as es2:
                    wb = [self.sb(es, f"f_wg{i}", [128, KC * 256], BF16) for i in range(3)]
                    wr = RL(3)
                    sg = [self.sb(es, f"f_sg{i}", [128, TB], F32) for i in range(2)]
                    sgr = RL(2)
                    it = 0
                    for j in range(FC):
                        w = wb[j % 3]
                        self.load_panel(w[:, :], wr[j % 3], wg, j, KC * 256)
                        for hb in range(2):
                            bg = self.banks[(it % 2) * 2]
                            bu = self.banks[(it % 2) * 2 + 1]
                            bgr = self.bank_res[(it % 2) * 2]
                            bur = self.bank_res[(it % 2) * 2 + 1]
                            s = sg[it % 2]
                            sr = sgr[it % 2]
                            it += 1

                            def mm(E, w=w, hb=hb, bank=bg, off=0):
                                ins = None
                                for kc in range(KC):
                                    ins = E.matmul(bank[:, :], lhsT=w[:, kc * 256 + off:kc * 256 + off + 128],
                                                   rhs=hT[:, kc, hb * TB:(hb + 1) * TB],
                                                   start=(kc == 0), stop=(kc == KC - 1))
                                return ins
                            P.op("pe", mm, reads=(wr[j % 3], h_res[hb]), writes=(bgr,))
                            P.op("pe", lambda E, w=w, hb=hb, bu=bu, mm=mm: mm(E, w, hb, bu, 128),
                                 reads=(wr[j % 3], h_res[hb]), writes=(bur,))
                            P.op("act", lambda E, s=s, bg=bg: E.activation(out=s[:, :], in_=bg[:, :], func=AF.Silu),
                                 reads=(bgr,), writes=(sr,))
                            P.op("dve", lambda E, s=s, bu=bu, j=j, hb=hb: E.tensor_tensor(
                                out=aT[:, j, hb * TB:(hb + 1) * TB], in0=bu[:, :], in1=s[:, :], op=ALU.mult),
                                reads=(bur, sr), writes=(a_res[j][hb],))
                with ExitStack() as es2:
                    wb = [self.sb(es2, f"f_wd{i}", [128, FC * 128], BF16) for i in range(2)]
                    wr = RL(2)
                    xb = [self.sb(es2, f"f_x{i}", [128, TB], F32) for i in range(3)]
                    xr = RL(3)
                    XT = self.dram["XT"]
                    it = 0
                    for j in range(KC):
                        w = wb[j % 2]
                        self.load_panel(w[:, :], wr[j % 2], wd, j, FC * 128)
                        for hb in range(2):
                            tb = sbk * 2 + hb
                            bank = self.banks[it % 4]
                            br = self.bank_res[it % 4]
                            x = xb[it % 3]
                            xres = xr[it % 3]
                            it += 1
                            P.dma("sp", x[:, :], XT[j * 128:(j + 1) * 128, tb * TB:(tb + 1) * TB],
                                  reads=(self.xt_res[j][tb],), writes=(xres,))

                            def mm(E, w=w, hb=hb, bank=bank):
                                ins = None
                                for fc in range(FC):
                                    ins = E.matmul(bank[:, :], lhsT=w[:, fc * 128:(fc + 1) * 128],
                                                   rhs=aT[:, fc, hb * TB:(hb + 1) * TB],
                                                   start=(fc == 0), stop=(fc == FC - 1))
                                return ins
                            P.op("pe", mm, reads=[wr[j % 2]] + [a_res[fc][hb] for fc in range(FC)], writes=(br,))
                            P.op("dve", lambda E, x=x, bank=bank: E.tensor_tensor(
                                out=x[:, :], in0=bank[:, :], in1=x[:, :], op=ALU.add),
                                reads=(br, xres), writes=(xres,))
                            P.dma("sp", XT[j * 128:(j + 1) * 128, tb * TB:(tb + 1) * TB], x[:, :],
                                  reads=(xres,), writes=(self.xt_res[j][tb],))
                    P.barrier()

    def phase_xattn(self, l):
        P = self.P
        with ExitStack() as es:
            KT = self.sb(es, "xa_KT", [128, KC, NMEM], BF16)
            kres = RL(KC)
            V = self.sb(es, "xa_V", [128, 2, D], BF16)
            vres = RL(2)
            with ExitStack() as es2:
                mi = self.sb(es2, "xa_mi", [128, 2, D], F32)
                mir = RL(2)
                junk = self.sb(es2, "xa_junk", [128, D], F32)
                jr = Res()
                ss = self.sb(es2, "xa_ss", [128, 2], F32)
                ssr = Res()
                mnT = self.sb(es2, "xa_mnT", [128, KC, NMEM], BF16)
                mnr = Res()
                mem = self.dram["mem"]
                for mt in range(2):
                    P.dma("sp", mi[:, mt, :], mem[mt * 128:(mt + 1) * 128, :], writes=(mir[mt],))
                    P.op("act", lambda E, mt=mt: E.activation(out=junk[:, :], in_=mi[:, mt, :], func=AF.Square,
                                                              accum_out=ss[:, mt:mt + 1]),
                         reads=(mir[mt],), writes=(jr, ssr))
                P.op("act", lambda E: E.activation(out=ss[:, :], in_=ss[:, :], func=AF.Sqrt,
                                                   bias=self.eps_col[:, 0:1], scale=1.0 / D),
                     reads=(ssr,), writes=(ssr,))
                P.op("dve", lambda E: E.reciprocal(out=ss[:, :], in_=ss[:, :]), reads=(ssr,), writes=(ssr,))
                for mt in range(2):
                    P.op("dve", lambda E, mt=mt: E.tensor_scalar(out=mi[:, mt, :], in0=mi[:, mt, :],
                                                                 scalar1=ss[:, mt:mt + 1], scalar2=None, op0=ALU.mult),
                         reads=(ssr, mir[mt]), writes=(mir[mt],))
                    for q in range(4):
                        bank = self.banks[q]

                        def tr(E, mt=mt, q=q, bank=bank):
                            ins = None
                            for i in range(4):
                                kc = q * 4 + i
                                ins = E.transpose(out=bank[:, i * 128:(i + 1) * 128],
                                                  in_=mi[:, mt, kc * 128:(kc + 1) * 128], identity=self.ident_f[:, :])
                            return ins
                        P.op("pe", tr, reads=(mir[mt],), writes=(self.bank_res[q],))
                        for i in range(4):
                            kc = q * 4 + i
                            P.op("act", lambda E, mt=mt, kc=kc, i=i, bank=bank: E.activation(
                                out=mnT[:, kc, mt * 128:(mt + 1) * 128], in_=bank[:, i * 128:(i + 1) * 128],
                                func=AF.Identity, scale=self.par(f"mem_norm{l}", kc)),
                                reads=(self.bank_res[q],), writes=(mnr,))
                wb = [self.sb(es2, f"xa_wk{i}", [128, KC * 128], BF16) for i in range(2)]
                wr = RL(2)
                wk = self.dram[f"wkvK{l}"]
                for j in range(KC):
                    w = wb[j % 2]
                    self.load_panel(w[:, :], wr[j % 2], wk, j, KC * 128)
                    bank = self.banks[4 + j % 2]
                    br = self.bank_res[4 + j % 2]

                    def mm(E, w=w, bank=bank):
                        ins = None
                        for kc in range(KC):
                            ins = E.matmul(bank[:, 0:NMEM], lhsT=w[:, kc * 128:(kc + 1) * 128], rhs=mnT[:, kc, :],
                                           start=(kc == 0), stop=(kc == KC - 1))
                        return ins
                    P.op("pe", mm, reads=(wr[j % 2], mnr), writes=(br,))
                    P.op("act", lambda E, j=j, bank=bank: E.copy(out=KT[:, j, :], in_=bank[:, 0:NMEM]),
                         reads=(br,), writes=(kres[j],))
                wb2 = [self.sb(es2, f"xa_wv{i}", [128, KC * 512], BF16) for i in range(2)]
                wr2 = RL(2)
                wv = self.dram[f"wkvV{l}"]
                it = 0
                for pn in range(4):
                    w = wb2[pn % 2]
                    self.load_panel(w[:, :], wr2[pn % 2], wv, pn, KC * 512)
                    for mt in range(2):
                        bank = self.banks[it % 4]
                        br = self.bank_res[it % 4]
                        it += 1

                        def mm(E, w=w, mt=mt, bank=bank):
                            ins = None
                            for kc in range(KC):
                                ins = E.matmul(bank[:, :], lhsT=mnT[:, kc, mt * 128:(mt + 1) * 128],
                                               rhs=w[:, kc * 512:(kc + 1) * 512],
                                               start=(kc == 0), stop=(kc == KC - 1))
                            return ins
                        P.op("pe", mm, reads=(wr2[pn % 2], mnr), writes=(br,))
                        P.op("dve", lambda E, mt=mt, pn=pn, bank=bank: E.tensor_copy(
                            out=V[:, mt, pn * 512:(pn + 1) * 512], in_=bank[:, :]),
                            reads=(br,), writes=(vres[mt],))
                P.barrier()
            hT = self.sb(es, "xa_hT", [128, KC, TOK], BF16)
            hres = RL(NB)
            with ExitStack() as es2:
                self.norm_all(es2, f"xa_norm{l}", hT, lambda tb: hres[tb])
                P.barrier()
            QT = self.sb(es, "xa_QT", [128, KC, TOK], BF16)
            qres = [RL(NB) for _ in range(KC)]
            with ExitStack() as es2:
                wb = [self.sb(es, f"xa_wq{i}", [128, KC * 128], BF16) for i in range(2)]
                wr = RL(2)
                wq = self.dram[f"wq{l}"]
                it = 0
                for j in range(KC):
                    w = wb[j % 2]
                    self.load_panel(w[:, :], wr[j % 2], wq, j, KC * 128)
                    for tb in range(NB):
                        bank = self.banks[it % 4]
                        br = self.bank_res[it % 4]
                        it += 1

                        def mm(E, w=w, tb=tb, bank=bank):
                            ins = None
                            for kc in range(KC):
                                ins = E.matmul(bank[:, :], lhsT=w[:, kc * 128:(kc + 1) * 128],
                                               rhs=hT[:, kc, tb * TB:(tb + 1) * TB],
                                               start=(kc == 0), stop=(kc == KC - 1))
                            return ins
                        P.op("pe", mm, reads=(wr[j % 2], hres[tb]), writes=(br,))
                        P.op("act", lambda E, j=j, tb=tb, bank=bank: E.activation(
                            out=QT[:, j, tb * TB:(tb + 1) * TB], in_=bank[:, :], func=AF.Copy, scale=512.0 ** -0.5),
                            reads=(br,), writes=(qres[j][tb],))
            with ExitStack() as es2:
                pT = [self.sb(es, f"xa_p{i}", [128, TB], BF16) for i in range(4)]
                pr = RL(4)
                rinv = [self.sb(es, f"xa_ri{i}", [128, TB], F32) for i in range(2)]
                rr = RL(2)
                io = 0
                iters = [(h, tb) for h in range(4) for tb in range(NB)]

                def emit_S(it):
                    h, tb = iters[it]
                    for mt in range(2):
                        bank = self.banks[(it % 2) * 2 + mt]
                        br = self.bank_res[(it % 2) * 2 + mt]

                        def mm(E, h=h, tb=tb, mt=mt, bank=bank):
                            ins = None
                            for dc in range(4):
                                ins = E.matmul(bank[:, :], lhsT=KT[:, h * 4 + dc, mt * 128:(mt + 1) * 128],
                                               rhs=QT[:, h * 4 + dc, tb * TB:(tb + 1) * TB],
                                               start=(dc == 0), stop=(dc == 3))
                            return ins
                        P.op("pe", mm, reads=[kres[h * 4 + dc] for dc in range(4)] +
                             [qres[h * 4 + dc][tb] for dc in range(4)], writes=(br,))

                emit_S(0)
                for it, (h, tb) in enumerate(iters):
                    pts = []
                    for mt in range(2):
                        bank = self.banks[(it % 2) * 2 + mt]
                        br = self.bank_res[(it % 2) * 2 + mt]
                        p = pT[(it % 2) * 2 + mt]
                        prs = pr[(it % 2) * 2 + mt]
                        P.op("act", lambda E, p=p, bank=bank: E.activation(out=p[:, :], in_=bank[:, :], func=AF.Exp),
                             reads=(br,), writes=(prs,))
                        pts.append((p, prs))
                    if it + 1 < len(iters):
                        emit_S(it + 1)
                    bl = self.banks[4]
                    blr = self.bank_res[4]

                    def mml(E, pts=pts, bl=bl):
                        ins = None
                        for mt in range(2):
                            ins = E.matmul(bl[:, :], lhsT=self.ones_bf[:, :], rhs=pts[mt][0][:, :],
                                           start=(mt == 0), stop=(mt == 1))
                        return ins
                    P.op("pe", mml, reads=(pts[0][1], pts[1][1]), writes=(blr,))
                    ri = rinv[it % 2]
                    rir = rr[it % 2]
                    P.op("dve", lambda E, ri=ri, bl=bl: E.reciprocal(out=ri[:, :], in_=bl[:, :]),
                         reads=(blr,), writes=(rir,))
                    for dc in range(4):
                        bo = self.banks[5 + io % 2]
                        bor = self.bank_res[5 + io % 2]
                        io += 1

                        def mmo(E, h=h, dc=dc, pts=pts, bo=bo):
                            ins = None
                            for mt in range(2):
                                ins = E.matmul(bo[:, :], lhsT=V[:, mt, h * 512 + dc * 128:h * 512 + (dc + 1) * 128],
                                               rhs=pts[mt][0][:, :], start=(mt == 0), stop=(mt == 1))
                            return ins
                        P.op("pe", mmo, reads=(vres[0], vres[1], pts[0][1], pts[1][1]), writes=(bor,))
                        P.op("dve", lambda E, h=h, dc=dc, tb=tb, bo=bo, ri=ri: E.tensor_tensor(
                            out=hT[:, h * 4 + dc, tb * TB:(tb + 1) * TB], in0=bo[:, :], in1=ri[:, :], op=ALU.mult),
                            reads=(bor, rir), writes=(hres[tb],))
            with ExitStack() as es2:
                self.proj_residual(es2, self.dram[f"wo{l}"], KC, lambda kc, tb: hT[:, kc, tb * TB:(tb + 1) * TB],
                                   lambda kc, tb: hres[tb], "xa_o")
                P.barrier()

    def hgrn_head(self, h, hsrc, hsrc_res, w, wres, W, cols, main, bufs):
        P = self.P
        (fA, fB, fC, fD, fE, fr, KtT, ktr, QtT, qtr, sgT, sgr, o_sb, osr, vsb, vr, Ktsb, Ktr, ATsb, atr,
         Spp, sppr, tmpS, tmpr, esc, escr, nbm, nbmr, d2, S, Sres, ybT, ybres, rs, rsr, tmpO, tmpOr) = bufs
        hp = h % 2
        bf = self.banks[0]; bfr = self.bank_res[0]
        bq = self.banks[1]; bqr = self.bank_res[1]
        bg = self.banks[2]; bgr = self.bank_res[2]
        oml = self.lbv[:, h, 0:1]
        lb = self.lbv[:, h, 1:2]

        def proj(E, bank, col, tb):
            ins = None
            for kc in range(KC):
                ins = E.matmul(bank[:, :], lhsT=w[:, kc * W + col:kc * W + col + 128],
                               rhs=hsrc[:, kc, tb * TB:(tb + 1) * TB], start=(kc == 0), stop=(kc == KC - 1))
            return ins

        for tb in range(NB):
            P.op("pe", lambda E, tb=tb: proj(E, bf, cols["f"], tb), reads=(wres, hsrc_res[tb]), writes=(bfr,))
            P.op("act", lambda E: E.activation(out=fA[:, :], in_=bf[:, :], func=AF.Sigmoid), reads=(bfr,), writes=(fr[0],))
            P.op("dve", lambda E: E.tensor_scalar(out=fA[:, :], in0=fA[:, :], scalar1=oml, scalar2=lb,
                                                  op0=ALU.mult, op1=ALU.add),
                 reads=(fr[0], self.lbv_res), writes=(fr[0],))
            P.op("act", lambda E: E.activation(out=fB[:, :], in_=fA[:, :], func=AF.Ln), reads=(fr[0],), writes=(fr[1],))
            P.op("dve", lambda E: E.tensor_scalar(out=fA[:, :], in0=fA[:, :], scalar1=-1.0, scalar2=1.0,
                                                  op0=ALU.mult, op1=ALU.add), reads=(fr[0],), writes=(fr[0],))
            for tt in range(4):
                P.op("dve", lambda E, tt=tt: E.tensor_tensor_scan(
                    out=fC[:, tt * 128:(tt + 1) * 128], data0=self.ones_f[:, 0:128], data1=fB[:, tt * 128:(tt + 1) * 128],
                    initial=0.0, op0=ALU.mult, op1=ALU.add), reads=(fr[1],), writes=(fr[2],))
            bmid = fC[:, 63:TB:128]
            blast = fC[:, 127:TB:128]
            P.op("dve", lambda E: E.tensor_scalar(out=nbm[:, :], in0=bmid, scalar1=-1.0, scalar2=None, op0=ALU.mult),
                 reads=(fr[2],), writes=(nbmr,))
            P.op("dve", lambda E: E.tensor_tensor(out=d2[:, :], in0=blast, in1=bmid, op=ALU.subtract),
                 reads=(fr[2],), writes=(nbmr,))
            P.op("act", lambda E, tb=tb: E.activation(out=esc[:, 0, tb * 4:(tb + 1) * 4], in_=bmid, func=AF.Exp),
                 reads=(fr[2],), writes=(escr,))
            P.op("act", lambda E, tb=tb: E.activation(out=esc[:, 1, tb * 4:(tb + 1) * 4], in_=d2[:, :], func=AF.Exp),
                 reads=(nbmr,), writes=(escr,))
            P.op("act", lambda E, tb=tb: E.activation(out=esc[:, 2, tb * 4:(tb + 1) * 4], in_=blast, func=AF.Exp),
                 reads=(fr[2],), writes=(escr,))
            for tt in range(4):
                sl = slice(tt * 128, (tt + 1) * 128)
                if main:
                    P.op("act", lambda E, tt=tt, sl=sl: E.activation(out=fD[:, sl], in_=fC[:, sl], func=AF.Exp,
                                                                      bias=nbm[:, tt:tt + 1], scale=1.0),
                         reads=(fr[2], nbmr), writes=(fr[3],))
                P.op("act", lambda E, tt=tt, sl=sl: E.activation(out=fE[:, sl], in_=fC[:, sl], func=AF.Exp,
                                                                  bias=fC[:, 63 + 128 * tt:64 + 128 * tt], scale=-1.0),
                     reads=(fr[2],), writes=(fr[4],))
            P.op("dve", lambda E, tb=tb: E.tensor_tensor(out=KtT[:, tb * TB:(tb + 1) * TB], in0=fA[:, :], in1=fE[:, :],
                                                        op=ALU.mult), reads=(fr[0], fr[4]), writes=(ktr[tb],))
            if main:
                P.op("pe", lambda E, tb=tb: proj(E, bq, cols["q"], tb), reads=(wres, hsrc_res[tb]), writes=(bqr,))
                P.op("act", lambda E: E.activation(out=fB[:, :], in_=bq[:, :], func=AF.Silu), reads=(bqr,), writes=(fr[1],))
                P.op("dve", lambda E, tb=tb: E.scalar_tensor_tensor(
                    out=QtT[:, tb * TB:(tb + 1) * TB], in0=fB[:, :], scalar=128.0 ** -0.5, in1=fD[:, :],
                    op0=ALU.mult, op1=ALU.mult), reads=(fr[1], fr[3]), writes=(qtr[tb],))
                P.op("pe", lambda E, tb=tb: proj(E, bg, cols["g"], tb), reads=(wres, hsrc_res[tb]), writes=(bgr,))
                P.op("act", lambda E, tb=tb: E.activation(out=sgT[:, tb * TB:(tb + 1) * TB], in_=bg[:, :], func=AF.Silu),
                     reads=(bgr,), writes=(sgr[tb],))
            bo = self.banks[5 + hp]
            bor = self.bank_res[5 + hp]
            yield
            for tt in range(4):
                t = tb * 4 + tt
                tsl = slice(t * 128, (t + 1) * 128)
                bi = self.banks[3]
                bir = self.bank_res[3]

                def mmi(E, tsl=tsl, bi=bi):
                    ins = None
                    for kc in range(KC):
                        ins = E.matmul(bi[:, 0:128], lhsT=hsrc[:, kc, tsl], rhs=w[:, kc * W + cols["i"]:kc * W + cols["i"] + 128],
                                       start=(kc == 0), stop=(kc == KC - 1))
                    return ins
                P.op("pe", mmi, reads=(wres, hsrc_res[tb]), writes=(bir,))
                v = vsb[t % 2]
                P.op("act", lambda E, v=v, bi=bi: E.copy(out=v[:, :], in_=bi[:, 0:128]), reads=(bir,), writes=(vr[t % 2],))
                P.op("pe", lambda E, tsl=tsl: E.transpose(out=self.bank_bf[:, 0:128], in_=KtT[:, tsl],
                                                          identity=self.ident_b[:, :]),
                     reads=(ktr[tb],), writes=(self.bank_bf_res,))
                kt = Ktsb[t % 2]
                P.op("dve", lambda E, kt=kt: E.tensor_copy(out=kt[:, :], in_=self.bank_bf[:, 0:128]),
                     reads=(self.bank_bf_res,), writes=(Ktr[t % 2],))
                if main:
                    ba = self.banks[4]
                    bar = self.bank_res[4]
                    P.op("pe", lambda E, tsl=tsl, ba=ba: E.matmul(ba[:, 0:128], lhsT=KtT[:, tsl], rhs=QtT[:, tsl],
                                                                  start=True, stop=True),
                         reads=(ktr[tb], qtr[tb]), writes=(bar,))
                    at = ATsb[t % 2]
                    P.op("dve", lambda E, ba=ba: E.tensor_scalar(out=tmpS[:, :], in0=ba[:, 0:128], scalar1=1e30,
                                                                 scalar2=-1e30, op0=ALU.min, op1=ALU.max),
                         reads=(bar,), writes=(tmpr,))
                    P.op("dve", lambda E, at=at: E.tensor_tensor(out=at[:, :], in0=tmpS[:, :], in1=self.triu_f[:, :],
                                                                 op=ALU.mult),
                         reads=(tmpr,), writes=(atr[t % 2],))
                    P.op("dve", lambda E, t=t: E.tensor_scalar(out=Spp[:, :], in0=S[:, h, :], scalar1=esc[:, 0, t:t + 1],
                                                               scalar2=None, op0=ALU.mult),
                         reads=(Sres[h], escr), writes=(sppr,))

                    def mmo(E, v=v, at=at, tsl=tsl, tt=tt):
                        E.matmul(bo[:, tt * 128:(tt + 1) * 128], lhsT=v[:, :], rhs=at[:, :], start=True, stop=False)
                        return E.matmul(bo[:, tt * 128:(tt + 1) * 128], lhsT=Spp[:, :], rhs=QtT[:, tsl],
                                        start=False, stop=True)
                    P.op("pe", mmo, reads=(vr[t % 2], atr[t % 2], sppr, qtr[tb]), writes=(bor,))
                bu = self.bank_bf[:, :].bitcast(F32)[:, 128:256]
                bur = self.bank_u_res
                P.op("pe", lambda E, kt=kt, v=v, bu=bu: E.matmul(bu, lhsT=kt[:, :], rhs=v[:, :],
                                                                 start=True, stop=True),
                     reads=(Ktr[t % 2], vr[t % 2]), writes=(bur,))
                P.op("dve", lambda E, t=t: E.tensor_scalar(out=tmpS[:, :], in0=S[:, h, :], scalar1=esc[:, 2, t:t + 1],
                                                           scalar2=None, op0=ALU.mult),
                     reads=(Sres[h], escr), writes=(tmpr,))
                P.op("dve", lambda E, t=t, bu=bu: E.scalar_tensor_tensor(
                    out=S[:, h, :], in0=bu, scalar=esc[:, 1, t:t + 1], in1=tmpS[:, :],
                    op0=ALU.mult, op1=ALU.add), reads=(bur, tmpr, escr), writes=(Sres[h],))
                yield
            if main:
                bsl = slice(tb * TB, (tb + 1) * TB)
                P.op("act", lambda E: E.copy(out=o_sb[:, :], in_=bo[:, :]), reads=(bor,), writes=(osr,))
                sq = self.sq[hp]
                P.op("act", lambda E, sq=sq: E.activation(out=sq[:, :], in_=o_sb[:, :], func=AF.Square),
                     reads=(osr,), writes=(self.sq_res[hp],))
                bn = self.banks[4]
                bnr = self.bank_res[4]
                P.op("pe", lambda E, sq=sq, bn=bn: E.matmul(bn[:, :], lhsT=self.ones_bf[:, :], rhs=sq[:, :],
                                                            start=True, stop=True),
                     reads=(self.sq_res[hp],), writes=(bnr,))
                P.op("act", lambda E, bn=bn: E.activation(out=rs[:, :], in_=bn[:, :], func=AF.Sqrt,
                                                          bias=self.eps_col[:, 0:1], scale=1.0 / 128),
                     reads=(bnr,), writes=(rsr,))
                P.op("dve", lambda E: E.reciprocal(out=rs[:, :], in_=rs[:, :]), reads=(rsr,), writes=(rsr,))
                P.op("dve", lambda E: E.scalar_tensor_tensor(out=tmpO[:, :], in0=o_sb[:, :], scalar=self.par("hg_norm"),
                                                             in1=rs[:, :], op0=ALU.mult, op1=ALU.mult),
                     reads=(osr, rsr), writes=(tmpOr,))
                P.op("dve", lambda E, bsl=bsl: E.tensor_tensor(out=ybT[:, h, bsl], in0=tmpO[:, :], in1=sgT[:, bsl],
                                                              op=ALU.mult),
                     reads=(tmpOr, sgr[tb]), writes=(ybres[h][tb],))

    def hgrn_steps(self, h, hsrc, hsrc_res, w, wres, W, cols, main, B):
        P = self.P
        S, Sres, ybT, ybres = B["S"], B["Sres"], B["ybT"], B["ybres"]
        fA, fB, fC, fD, fE, fr = B["fA"], B["fB"], B["fC"], B["fD"], B["fE"], B["fr"]
        KtT, ktr, QtT, qtr, sgT, sgr = B["KtT"], B["ktr"], B["QtT"], B["qtr"], B["sgT"], B["sgr"]
        esc, escr, nbm, nbmr, d2 = B["esc"], B["escr"], B["nbm"], B["nbmr"], B["d2"]
        v_all, var, Kt_all, Ktar, AT_all, ATar = B["v_all"], B["var"], B["Kt_all"], B["Ktar"], B["AT_all"], B["ATar"]
        U_all, Uar, Spp_all, Sppr, tmpA, tmpAr = B["U_all"], B["Uar"], B["Spp_all"], B["Sppr"], B["tmpA"], B["tmpAr"]
        tmpS, tmpr, o_sb, osr, rs, rsr, tmpO, tmpOr = B["tmpS"], B["tmpr"], B["o_sb"], B["osr"], B["rs"], B["rsr"], B["tmpO"], B["tmpOr"]
        bf, bfr = self.banks[0], self.bank_res[0]
        bq, bqr = self.banks[1], self.bank_res[1]
        bg, bgr = self.banks[2], self.bank_res[2]
        bi, bir = self.banks[3], self.bank_res[3]
        ba, bar = self.banks[4], self.bank_res[4]
        bo, bor = self.banks[5], self.bank_res[5]
        bu, bur = self.banks[6], self.bank_res[6]
        oml = self.lbv[:, h, 0:1]
        lb = self.lbv[:, h, 1:2]

        def proj(E, bank, col, tb):
            ins = None
            for kc in range(KC):
                ins = E.matmul(bank[:, :], lhsT=w[:, kc * W + col:kc * W + col + 128],
                               rhs=hsrc[:, kc, tb * TB:(tb + 1) * TB], start=(kc == 0), stop=(kc == KC - 1))
            return ins

        def p1(tb):
            bsl = slice(tb * TB, (tb + 1) * TB)
            t4 = slice(tb * 4, (tb + 1) * 4)
            P.op("pe", lambda E, tb=tb: proj(E, bf, cols["f"], tb), reads=(wres, hsrc_res[tb]), writes=(bfr,))
            P.op("act", lambda E: E.activation(out=fA[:, :], in_=bf[:, :], func=AF.Sigmoid), reads=(bfr,), writes=(fr[0],))
            P.op("dve", lambda E: E.tensor_scalar(out=fA[:, :], in0=fA[:, :], scalar1=oml, scalar2=lb,
                                                  op0=ALU.mult, op1=ALU.add),
                 reads=(fr[0], self.lbv_res), writes=(fr[0],))
            P.op("act", lambda E: E.activation(out=fB[:, :], in_=fA[:, :], func=AF.Ln), reads=(fr[0],), writes=(fr[1],))
            P.op("dve", lambda E: E.tensor_scalar(out=fA[:, :], in0=fA[:, :], scalar1=-1.0, scalar2=1.0,
                                                  op0=ALU.mult, op1=ALU.add), reads=(fr[0],), writes=(fr[0],))
            for tt in range(4):
                P.op("dve", lambda E, tt=tt: E.tensor_tensor_scan(
                    out=fC[:, tt * 128:(tt + 1) * 128], data0=self.ones_f[:, 0:128], data1=fB[:, tt * 128:(tt + 1) * 128],
                    initial=0.0, op0=ALU.mult, op1=ALU.add), reads=(fr[1],), writes=(fr[2],))
            bmid = fC[:, 63:TB:128]
            blast = fC[:, 127:TB:128]
            P.op("dve", lambda E: E.tensor_scalar(out=nbm[:, :], in0=bmid, scalar1=-1.0, scalar2=None, op0=ALU.mult),
                 reads=(fr[2],), writes=(nbmr,))
            P.op("dve", lambda E: E.tensor_tensor(out=d2[:, :], in0=blast, in1=bmid, op=ALU.subtract),
                 reads=(fr[2],), writes=(nbmr,))
            P.op("act", lambda E, t4=t4: E.activation(out=esc[:, 0, t4], in_=bmid, func=AF.Exp), reads=(fr[2],), writes=(escr,))
            P.op("act", lambda E, t4=t4: E.activation(out=esc[:, 1, t4], in_=d2[:, :], func=AF.Exp), reads=(nbmr,), writes=(escr,))
            P.op("act", lambda E, t4=t4: E.activation(out=esc[:, 2, t4], in_=blast, func=AF.Exp), reads=(fr[2],), writes=(escr,))
            for tt in range(4):
                sl = slice(tt * 128, (tt + 1) * 128)
                if main:
                    P.op("act", lambda E, tt=tt, sl=sl: E.activation(out=fD[:, sl], in_=fC[:, sl], func=AF.Exp,
                                                                      bias=nbm[:, tt:tt + 1], scale=1.0),
                         reads=(fr[2], nbmr), writes=(fr[3],))
                P.op("act", lambda E, tt=tt, sl=sl: E.activation(out=fE[:, sl], in_=fC[:, sl], func=AF.Exp,
                                                                  bias=fC[:, 63 + 128 * tt:64 + 128 * tt], scale=-1.0),
                     reads=(fr[2],), writes=(fr[4],))
            P.op("dve", lambda E, bsl=bsl: E.tensor_tensor(out=KtT[:, bsl], in0=fA[:, :], in1=fE[:, :], op=ALU.mult),
                 reads=(fr[0], fr[4]), writes=(ktr[tb],))
            if main:
                P.op("pe", lambda E, tb=tb: proj(E, bq, cols["q"], tb), reads=(wres, hsrc_res[tb]), writes=(bqr,))
                P.op("act", lambda E: E.activation(out=fB[:, :], in_=bq[:, :], func=AF.Silu), reads=(bqr,), writes=(fr[1],))
                P.op("dve", lambda E, bsl=bsl: E.scalar_tensor_tensor(
                    out=QtT[:, bsl], in0=fB[:, :], scalar=128.0 ** -0.5, in1=fD[:, :],
                    op0=ALU.mult, op1=ALU.mult), reads=(fr[1], fr[3]), writes=(qtr[tb],))
                P.op("pe", lambda E, tb=tb: proj(E, bg, cols["g"], tb), reads=(wres, hsrc_res[tb]), writes=(bgr,))
                P.op("act", lambda E, bsl=bsl: E.activation(out=sgT[:, bsl], in_=bg[:, :], func=AF.Silu),
                     reads=(bgr,), writes=(sgr[tb],))

            def mmi(E, tb=tb):
                ins = None
                for tt in range(4):
                    tsl = slice((tb * 4 + tt) * 128, (tb * 4 + tt + 1) * 128)
                    for kc in range(KC):
                        ins = E.matmul(bi[:, tt * 128:(tt + 1) * 128], lhsT=hsrc[:, kc, tsl],
                                       rhs=w[:, kc * W + cols["i"]:kc * W + cols["i"] + 128],
                                       start=(kc == 0), stop=(kc == KC - 1))
                return ins
            P.op("pe", mmi, reads=(wres, hsrc_res[tb]), writes=(bir,))
            P.op("act", lambda E, t4=t4: E.copy(out=v_all[:, t4, :], in_=bi[:, :].rearrange("p (a b) -> p a b", a=4)),
                 reads=(bir,), writes=(var[tb],))

            def trk(E, tb=tb):
                ins = None
                for tt in range(4):
                    tsl = slice((tb * 4 + tt) * 128, (tb * 4 + tt + 1) * 128)
                    ins = E.transpose(out=self.bank_bf[:, tt * 128:(tt + 1) * 128], in_=KtT[:, tsl],
                                      identity=self.ident_b[:, :])
                return ins
            P.op("pe", trk, reads=(ktr[tb],), writes=(self.bank_bf_res,))
            P.op("dve", lambda E, t4=t4: E.tensor_copy(out=Kt_all[:, t4, :],
                                                      in_=self.bank_bf[:, 0:512].rearrange("p (a b) -> p a b", a=4)),
                 reads=(self.bank_bf_res,), writes=(Ktar[tb],))
            if main:
                def mma(E, tb=tb):
                    ins = None
                    for tt in range(4):
                        tsl = slice((tb * 4 + tt) * 128, (tb * 4 + tt + 1) * 128)
                        ins = E.matmul(ba[:, tt * 128:(tt + 1) * 128], lhsT=KtT[:, tsl], rhs=QtT[:, tsl],
                                       start=True, stop=True)
                    return ins
                P.op("pe", mma, reads=(ktr[tb], qtr[tb]), writes=(bar,))
                P.op("dve", lambda E: E.tensor_scalar(out=tmpA[:, :], in0=ba[:, :], scalar1=1e30, scalar2=-1e30,
                                                      op0=ALU.min, op1=ALU.max), reads=(bar,), writes=(tmpAr,))
                P.op("dve", lambda E, t4=t4: E.tensor_tensor(
                    out=AT_all[:, t4, :], in0=tmpA[:, :].rearrange("p (a b) -> p a b", a=4), in1=self.triu4[:, :, :],
                    op=ALU.mult), reads=(tmpAr,), writes=(ATar[tb],))

            def mmu(E, tb=tb):
                ins = None
                for tt in range(4):
                    t = tb * 4 + tt
                    ins = E.matmul(bu[:, tt * 128:(tt + 1) * 128], lhsT=Kt_all[:, t, :], rhs=v_all[:, t, :],
                                   start=True, stop=True)
                return ins
            P.op("pe", mmu, reads=(Ktar[tb], var[tb]), writes=(bur,))
            P.op("act", lambda E, t4=t4: E.copy(out=U_all[:, t4, :], in_=bu[:, :].rearrange("p (a b) -> p a b", a=4)),
                 reads=(bur,), writes=(Uar[tb],))
        def p2(tb):
          for t in range(tb * 4, tb * 4 + 4):
            if main:
                P.op("dve", lambda E, t=t: E.tensor_scalar(out=Spp_all[:, t, :], in0=S[:, h, :], scalar1=esc[:, 0, t:t + 1],
                                                           scalar2=None, op0=ALU.mult),
                     reads=(Sres[h], escr), writes=(Sppr[tb],))
            P.op("dve", lambda E, t=t: E.tensor_scalar(out=tmpS[:, :], in0=S[:, h, :], scalar1=esc[:, 2, t:t + 1],
                                                       scalar2=None, op0=ALU.mult),
                 reads=(Sres[h], escr), writes=(tmpr,))
            P.op("dve", lambda E, t=t: E.scalar_tensor_tensor(
                out=S[:, h, :], in0=U_all[:, t, :], scalar=esc[:, 1, t:t + 1], in1=tmpS[:, :],
                op0=ALU.mult, op1=ALU.add), reads=(Uar[tb], tmpr, escr), writes=(Sres[h],))
        def p3(tb):
            if not main:
                return
            bsl = slice(tb * TB, (tb + 1) * TB)

            def mmo(E, tb=tb):
                ins = None
                for tt in range(4):
                    t = tb * 4 + tt
                    tsl = slice(t * 128, (t + 1) * 128)
                    E.matmul(bo[:, tt * 128:(tt + 1) * 128], lhsT=v_all[:, t, :], rhs=AT_all[:, t, :], start=True, stop=False)
                    ins = E.matmul(bo[:, tt * 128:(tt + 1) * 128], lhsT=Spp_all[:, t, :], rhs=QtT[:, tsl],
                                   start=False, stop=True)
                return ins
            P.op("pe", mmo, reads=(var[tb], ATar[tb], Sppr[tb], qtr[tb]), writes=(bor,))
            P.op("act", lambda E: E.copy(out=o_sb[:, :], in_=bo[:, :]), reads=(bor,), writes=(osr,))
            sq = self.sq[0]
            P.op("act", lambda E, sq=sq: E.activation(out=sq[:, :], in_=o_sb[:, :], func=AF.Square),
                 reads=(osr,), writes=(self.sq_res[0],))
            P.op("pe", lambda E, sq=sq: E.matmul(ba[:, :], lhsT=self.ones_bf[:, :], rhs=sq[:, :], start=True, stop=True),
                 reads=(self.sq_res[0],), writes=(bar,))
            P.op("act", lambda E: E.activation(out=rs[:, :], in_=ba[:, :], func=AF.Sqrt,
                                               bias=self.eps_col[:, 0:1], scale=1.0 / 128),
                 reads=(bar,), writes=(rsr,))
            P.op("dve", lambda E: E.reciprocal(out=rs[:, :], in_=rs[:, :]), reads=(rsr,), writes=(rsr,))
            P.op("dve", lambda E: E.scalar_tensor_tensor(out=tmpO[:, :], in0=o_sb[:, :], scalar=self.par("hg_norm"),
                                                         in1=rs[:, :], op0=ALU.mult, op1=ALU.mult),
                 reads=(osr, rsr), writes=(tmpOr,))
            P.op("dve", lambda E, bsl=bsl: E.tensor_tensor(out=ybT[:, h, bsl], in0=tmpO[:, :], in1=sgT[:, bsl], op=ALU.mult),
                 reads=(tmpOr, sgr[tb]), writes=(ybres[h][tb],))
        return p1, p2, p3

    def hgrn_run(self, hsrc, hsrc_res, wb, wr, wsrc, W, cols, main, B):
        pend = None
        for h in range(8):
            self.load_panel(wb[h % 2][:, :], wr[h % 2], wsrc, h, KC * W)
            p1, p2, p3 = self.hgrn_steps(h, hsrc, hsrc_res, wb[h % 2], wr[h % 2], W, cols, main, B)
            for tb in range(NB):
                p1(tb)
                p2(tb)
                if pend is not None:
                    pend()
                pend = (lambda p3=p3, tb=tb: p3(tb))
        if pend is not None:
            pend()

    def hgrn_bufs2(self, es, S, Sres, ybT, ybres, main):
        sb = self.sb
        B = {"S": S, "Sres": Sres, "ybT": ybT, "ybres": ybres}
        for nm in ("fA", "fB", "fC", "fD", "fE"):
            B[nm] = sb(es, "hg_" + nm, [128, TB], F32)
        B["fr"] = RL(5)
        B["KtT"] = sb(es, "hg_KtT", [128, TOK], BF16); B["ktr"] = RL(NB)
        B["esc"] = sb(es, "hg_esc", [128, 3, 16], F32); B["escr"] = Res()
        B["nbm"] = sb(es, "hg_nbm", [128, 4], F32); B["nbmr"] = Res()
        B["d2"] = sb(es, "hg_d2", [128, 4], F32)
        B["v_all"] = sb(es, "hg_v", [128, 16, 128], BF16); B["var"] = RL(NB)
        B["Kt_all"] = sb(es, "hg_kt", [128, 16, 128], BF16); B["Ktar"] = RL(NB)
        B["U_all"] = sb(es, "hg_U", [128, 16, 128], F32); B["Uar"] = RL(NB)
        B["tmpS"] = sb(es, "hg_tmpS", [128, 128], F32); B["tmpr"] = Res()
        for nm in ("QtT", "qtr", "sgT", "sgr", "AT_all", "ATar", "Spp_all", "Sppr", "tmpA", "tmpAr",
                   "o_sb", "osr", "rs", "rsr", "tmpO", "tmpOr"):
            B[nm] = None
        if main:
            B["QtT"] = sb(es, "hg_QtT", [128, TOK], BF16); B["qtr"] = RL(NB)
            B["sgT"] = sb(es, "hg_sgT", [128, TOK], BF16); B["sgr"] = RL(NB)
            B["AT_all"] = sb(es, "hg_at", [128, 16, 128], BF16); B["ATar"] = RL(NB)
            B["Spp_all"] = sb(es, "hg_spp", [128, 16, 128], BF16); B["Sppr"] = RL(NB)
            B["tmpA"] = sb(es, "hg_tmpA", [128, TB], F32); B["tmpAr"] = Res()
            B["o_sb"] = sb(es, "hg_o", [128, TB], F32); B["osr"] = Res()
            B["rs"] = sb(es, "hg_rs", [128, TB], F32); B["rsr"] = Res()
            B["tmpO"] = sb(es, "hg_tmpO", [128, TB], F32); B["tmpOr"] = Res()
        return B

    def run_heads(self, make_gen, nheads=8, stagger=10):
        active = [make_gen(0)]
        nxt = 1
        steps = 0
        while active:
            for g in list(active):
                try:
                    next(g)
                except StopIteration:
                    active.remove(g)
            steps += 1
            while len(active) < 2 and nxt < nheads and (steps >= stagger or not active):
                active.append(make_gen(nxt))
                nxt += 1

    def hgrn_bufs(self, es, S, Sres, ybT, ybres):
        sb = self.sb
        fA = sb(es, "hg_fA", [128, TB], F32); fB = sb(es, "hg_fB", [128, TB], F32)
        fC = sb(es, "hg_fC", [128, TB], F32); fD = sb(es, "hg_fD", [128, TB], F32)
        fE = sb(es, "hg_fE", [128, TB], F32)
        fr = RL(5)
        KtT = sb(es, "hg_KtT", [128, TOK], BF16); ktr = RL(NB)
        QtT = sb(es, "hg_QtT", [128, TOK], BF16); qtr = RL(NB)
        sgT = sb(es, "hg_sgT", [128, TOK], BF16); sgr = RL(NB)
        o_sb = sb(es, "hg_o", [128, TB], F32); osr = Res()
        vsb = [sb(es, f"hg_v{i}", [128, 128], BF16) for i in range(2)]; vr = RL(2)
        Ktsb = [sb(es, f"hg_kt{i}", [128, 128], BF16) for i in range(2)]; Ktr = RL(2)
        ATsb = [sb(es, f"hg_at{i}", [128, 128], BF16) for i in range(2)]; atr = RL(2)
        Spp = sb(es, "hg_spp", [128, 128], BF16); sppr = Res()
        tmpS = sb(es, "hg_tmpS", [128, 128], F32); tmpr = Res()
        esc = sb(es, "hg_esc", [128, 3, 16], F32); escr = Res()
        nbm = sb(es, "hg_nbm", [128, 4], F32); nbmr = Res()
        d2 = sb(es, "hg_d2", [128, 4], F32)
        rs = sb(es, "hg_rs", [128, TB], F32); rsr = Res()
        tmpO = sb(es, "hg_tmpO", [128, TB], F32); tmpOr = Res()
        return (fA, fB, fC, fD, fE, fr, KtT, ktr, QtT, qtr, sgT, sgr, o_sb, osr, vsb, vr, Ktsb, Ktr, ATsb, atr,
                Spp, sppr, tmpS, tmpr, esc, escr, nbm, nbmr, d2, S, Sres, ybT, ybres, rs, rsr, tmpO, tmpOr)

    def phase_l0(self):
        P = self.P
        with ExitStack() as es:
            S = self.sb(es, "l0_S", [128, 8, 128], F32)
            Sres = RL(8)
            hhalo = self.sb(es, "l0_halo", [128, KC, 128], BF16)
            halo_res = Res()
            self.lbv = self.sb(es, "l0_lbv", [128, 8, 2], F32)
            self.lbv_res = Res()
            inv16 = self.sb(es, "l0_inv16", [128, 4, 16], F32)
            inv_res = Res()
            ex = self.sb(es, "l0_ex", [128, 4, 8], F32)
            exr = Res()
            for i in range(3):
                P.op("act", lambda E, i=i: E.activation(out=ex[:, i, :], in_=self.par(f"lb{i}", 0, 8), func=AF.Exp),
                     writes=(exr,))
            P.op("dve", lambda E: E.tensor_tensor(out=ex[:, 3, :], in0=ex[:, 0, :], in1=ex[:, 1, :], op=ALU.add),
                 reads=(exr,), writes=(exr,))
            P.op("dve", lambda E: E.tensor_tensor(out=ex[:, 3, :], in0=ex[:, 3, :], in1=ex[:, 2, :], op=ALU.add),
                 reads=(exr,), writes=(exr,))
            P.op("dve", lambda E: E.reciprocal(out=ex[:, 3, :], in_=ex[:, 3, :]), reads=(exr,), writes=(exr,))
            P.op("dve", lambda E: E.tensor_tensor(out=self.lbv[:, :, 1], in0=ex[:, 1, :], in1=ex[:, 3, :], op=ALU.mult),
                 reads=(exr,), writes=(self.lbv_res,))
            P.op("dve", lambda E: E.tensor_scalar(out=self.lbv[:, :, 0], in0=self.lbv[:, :, 1], scalar1=-1.0, scalar2=1.0,
                                                  op0=ALU.mult, op1=ALU.add),
                 reads=(self.lbv_res,), writes=(self.lbv_res,))
            for g in range(4):
                P.op("dve", lambda E, g=g: E.tensor_scalar(out=inv16[:, g, :], in0=self.par("iota16", 0, 16),
                                                           scalar1=self.par("tok0"), scalar2=float(2 ** (g + 1)),
                                                           op0=ALU.add, op1=ALU.min), writes=(inv_res,))
            P.op("dve", lambda E: E.reciprocal(out=inv16[:, :, :], in_=inv16[:, :, :]), reads=(inv_res,), writes=(inv_res,))
            P.op("dve", lambda E: E.memset(S[:, :, :], 0.0), writes=Sres)
            P.barrier()
            P.scope = "l0.pre_norm"
            with ExitStack() as es2:
                hp = self.sb(es2, "l0_hp", [128, KC, TOK], BF16)
                hpr = RL(NB)
                with ExitStack() as es3:
                    self.transpose_in(es3, self.dram["x_prev"], "ev_norm", hp, hpr, store_xt=False)
                    P.barrier()
                P.op("dve", lambda E: E.tensor_copy(out=hhalo[:, :, :], in_=hp[:, :, TOK - 128:TOK]),
                     reads=(hpr[3],), writes=(halo_res,))
                P.scope = "l0.pre_hgrn"
                B = self.hgrn_bufs2(es2, S, Sres, None, None, False)
                wb = [self.sb(es2, f"l0_wfi{i}", [128, KC * 256], BF16) for i in range(2)]
                wr = RL(2)
                self.hgrn_run(hp, hpr, wb, wr, self.dram["w_fi"], 256, {"f": 0, "i": 128}, False, B)
                P.barrier()
            P.scope = "l0.own_norm"
            with ExitStack() as es2:
                hT = self.sb(es2, "l0_hT", [128, KC, TOK], BF16)
                hres = RL(NB)
                with ExitStack() as es3:
                    self.transpose_in(es3, self.dram["x_own"], "ev_norm", hT, hres, store_xt=True)
                    P.barrier()
                P.scope = "l0.pool"
                with ExitStack() as es3:
                    yaT = self.sb(es3, "l0_yaT", [128, 8, TOK], BF16)
                    yares = [RL(NB) for _ in range(8)]
                    L = 16 + TOK
                    ub = self.sb(es3, "l0_ub", [128, L], F32); ubr = Res()
                    sA = self.sb(es3, "l0_sA", [128, L], F32); sAr = Res()
                    sB = self.sb(es3, "l0_sB", [128, L], F32); sBr = Res()
                    pT = self.sb(es3, "l0_pT", [128, 2, TOK], BF16); pTr = RL(2)
                    wpool = self.sb(es3, "l0_wpool", [128, 4 * 2 * 256], BF16); wpr = Res()
                    P.dma("pool", wpool[:, :], self.dram["w_pool"][:, :], writes=(wpr,), max_dma_last_dim=4096)
                    wb = [self.sb(es3, f"l0_wu{i}", [128, KC * 512], BF16) for i in range(2)]
                    wr = RL(2)
                    it = 0
                    for c in range(8):
                        g = c // 2
                        cc = c % 2
                        wd_ = 2 ** (g + 1)
                        if c % 4 == 0:
                            self.load_panel(wb[(c // 4) % 2][:, :], wr[(c // 4) % 2], self.dram["w_u"], c // 4, KC * 512)
                        w = wb[(c // 4) % 2]
                        wres = wr[(c // 4) % 2]
                        off = (c % 4) * 128
                        bank = self.banks[it % 4]; br = self.bank_res[it % 4]; it += 1

                        def mmh(E, w=w, off=off, bank=bank):
                            ins = None
                            for kc in range(KC):
                                ins = E.matmul(bank[:, 0:128], lhsT=w[:, kc * 512 + off:kc * 512 + off + 128],
                                               rhs=hhalo[:, kc, :], start=(kc == 0), stop=(kc == KC - 1))
                            return ins
                        P.op("pe", mmh, reads=(wres, halo_res), writes=(br,))
                        P.op("act", lambda E, bank=bank: E.copy(out=ub[:, 0:16], in_=bank[:, 112:128]),
                             reads=(br,), writes=(ubr,))
                        for tb in range(NB):
                            bank = self.banks[it % 4]; br = self.bank_res[it % 4]; it += 1

                            def mm(E, w=w, off=off, bank=bank, tb=tb):
                                ins = None
                                for kc in range(KC):
                                    ins = E.matmul(bank[:, :], lhsT=w[:, kc * 512 + off:kc * 512 + off + 128],
                                                   rhs=hT[:, kc, tb * TB:(tb + 1) * TB],
                                                   start=(kc == 0), stop=(kc == KC - 1))
                                return ins
                            P.op("pe", mm, reads=(wres, hres[tb]), writes=(br,))
                            P.op("act", lambda E, bank=bank, tb=tb: E.copy(out=ub[:, 16 + tb * TB:16 + (tb + 1) * TB],
                                                                          in_=bank[:, :]),
                                 reads=(br,), writes=(ubr,))
                        P.op("dve", lambda E: E.tensor_tensor(out=sA[:, 1:L], in0=ub[:, 1:L], in1=ub[:, 0:L - 1], op=ALU.add),
                             reads=(ubr,), writes=(sAr,))
                        sw, swr = sA, sAr
                        if wd_ >= 4:
                            P.op("dve", lambda E: E.tensor_tensor(out=sB[:, 3:L], in0=sA[:, 3:L], in1=sA[:, 1:L - 2], op=ALU.add),
                                 reads=(sAr,), writes=(sBr,))
                            sw, swr = sB, sBr
                        if wd_ >= 8:
                            P.op("dve", lambda E: E.tensor_tensor(out=sA[:, 7:L], in0=sB[:, 7:L], in1=sB[:, 3:L - 4], op=ALU.add),
                                 reads=(sBr,), writes=(sAr,))
                            sw, swr = sA, sAr
                        if wd_ >= 16:
                            P.op("dve", lambda E: E.tensor_tensor(out=sB[:, 15:L], in0=sA[:, 15:L], in1=sA[:, 7:L - 8], op=ALU.add),
                                 reads=(sAr,), writes=(sBr,))
                            sw, swr = sB, sBr
                        P.op("dve", lambda E, sw=sw, cc=cc, wd_=wd_: E.scalar_tensor_tensor(
                            out=pT[:, cc, :], in0=sw[:, 16:L], scalar=1.0 / wd_, in1=ub[:, 16:L],
                            op0=ALU.mult, op1=ALU.subtract), reads=(swr, ubr), writes=(pTr[cc],))
                        P.op("dve", lambda E, sw=sw, g=g: E.tensor_tensor(out=sw[:, 16:32], in0=sw[:, 16:32], in1=inv16[:, g, :],
                                                                          op=ALU.mult),
                             reads=(swr, inv_res), writes=(swr,))
                        P.op("dve", lambda E, sw=sw, cc=cc: E.tensor_tensor(out=pT[:, cc, 0:16], in0=sw[:, 16:32], in1=ub[:, 16:32],
                                                                            op=ALU.subtract),
                             reads=(swr, ubr), writes=(pTr[cc],))
                        if cc == 1:
                            for dc in range(2):
                                for tb in range(NB):
                                    bank = self.banks[4 + it % 2]; br = self.bank_res[4 + it % 2]; it += 1

                                    def mmp(E, g=g, dc=dc, tb=tb, bank=bank):
                                        ins = None
                                        for c2 in range(2):
                                            o_ = (g * 2 + c2) * 256 + dc * 128
                                            ins = E.matmul(bank[:, :], lhsT=wpool[:, o_:o_ + 128],
                                                           rhs=pT[:, c2, tb * TB:(tb + 1) * TB],
                                                           start=(c2 == 0), stop=(c2 == 1))
                                        return ins
                                    P.op("pe", mmp, reads=(wpr, pTr[0], pTr[1]), writes=(br,))
                                    P.op("act", lambda E, g=g, dc=dc, tb=tb, bank=bank: E.activation(
                                        out=yaT[:, g * 2 + dc, tb * TB:(tb + 1) * TB], in_=bank[:, :], func=AF.Identity,
                                        scale=self.par("pool_scale", g * 2 + dc)),
                                        reads=(br,), writes=(yares[g * 2 + dc][tb],))
                    P.scope = "l0.wout_a"
                    with ExitStack() as es4:
                        self.proj_residual(es4, self.dram["wout_a"], 8, lambda kc, tb: yaT[:, kc, tb * TB:(tb + 1) * TB],
                                           lambda kc, tb: yares[kc][tb], "l0_oa")
                        P.barrier()
                P.scope = "l0.hgrn"
                with ExitStack() as es3:
                    ybT = self.sb(es3, "l0_ybT", [128, 8, TOK], BF16)
                    ybres = [RL(NB) for _ in range(8)]
                    with ExitStack() as es4:
                        B = self.hgrn_bufs2(es4, S, Sres, ybT, ybres, True)
                        wb = [self.sb(es4, f"l0_wh{i}", [128, KC * 512], BF16) for i in range(2)]
                        wr = RL(2)
                        self.hgrn_run(hT, hres, wb, wr, self.dram["w_h"], 512,
                                      {"q": 0, "f": 128, "i": 256, "g": 384}, True, B)
                        P.barrier()
                    P.scope = "l0.wout_b"
                    with ExitStack() as es4:
                        self.proj_residual(es4, self.dram["wout_b"], 8, lambda kc, tb: ybT[:, kc, tb * TB:(tb + 1) * TB],
                                           lambda kc, tb: ybres[kc][tb], "l0_ob")
                        P.barrier()

    def fox_decl(self, scratch):
        d = {}
        d["KV"] = [scratch(f"KVd{h}", [256, TOK], BF16) for h in range(16)]
        d["KVg"] = [scratch(f"KVg{h}", [512, TOK], BF16) for h in range(16)]
        d["Fx"] = scratch("Fx", [128, 272], F32)
        d["Fg"] = scratch("Fg", [256, 272], F32)
        d["QT"] = scratch("QTd", [KC * 128, TOK], BF16)
        d["R"] = scratch("Rd", [128, 64], F32)
        return d

    def phase_fox_a(self, scratch):
        P = self.P
        self.fox_own = self.fox_decl(scratch)
        o = self.fox_own
        self.fox_res_q = Res()
        self.fox_res_f = Res()
        self.fox_res_kv = RL(16)
        self.fox_prev_f = Res()
        self.fox_prev_kv = RL(16)
        groups = [[0, 1], [2, 3], [4, 5], [6, 7]]
        with ExitStack() as es:
            hT = self.sb(es, "fx_hT", [128, KC, TOK], BF16)
            hres = RL(NB)
            with ExitStack() as es2:
                self.norm_all(es2, "od_norm", hT, lambda tb: hres[tb])
                P.barrier()
            with ExitStack() as es2:
                wfl = self.sb(es2, "fx_wfl", [128, KC * 16], BF16)
                wflr = Res()
                self.load_panel(wfl[:, :], wflr, self.dram["ffl"], 0, KC * 16)
                Floc = self.sb(es2, "fx_Floc", [128, 16, 16], F32)
                Flr = Res()
                tot = self.sb(es2, "fx_tot", [128, 16], F32)
                totr = Res()
                Rbc = self.sb(es2, "fx_Rbc", [128, 4, 16], F32)
                Rr = Res()
                zs = [self.sb(es2, f"fx_z{i}", [128, 16], F32) for i in range(2)]
                zrs = RL(2)
                P.op("dve", lambda E: E.memset(tot[:, :], 0.0), writes=(totr,))
                for t in range(16):
                    z = zs[t % 2]
                    zr = zrs[t % 2]
                    bank = self.banks[4]
                    br = self.bank_res[4]

                    def mmf(E, t=t, bank=bank):
                        ins = None
                        for kc in range(KC):
                            ins = E.matmul(bank[:, 0:16], lhsT=hT[:, kc, t * 128:(t + 1) * 128],
                                           rhs=wfl[:, kc * 16:(kc + 1) * 16], start=(kc == 0), stop=(kc == KC - 1))
                        return ins
                    P.op("pe", mmf, reads=(wflr, hres[t // 4]), writes=(br,))
                    P.op("dve", lambda E, z=z, bank=bank: E.tensor_tensor(out=z[:, :], in0=bank[:, 0:16],
                                                                          in1=self.par("b_f", 0, 16), op=ALU.add),
                         reads=(br,), writes=(zr,))
                    P.op("act", lambda E, z=z: E.activation(out=z[:, :], in_=z[:, :], func=AF.Exp, scale=-1.0),
                         reads=(zr,), writes=(zr,))
                    P.op("act", lambda E, z=z: E.activation(out=z[:, :], in_=z[:, :], func=AF.Ln, bias=self.ones_f[:, 0:1],
                                                            scale=1.0), reads=(zr,), writes=(zr,))
                    P.op("dve", lambda E, z=z: E.tensor_scalar(out=z[:, :], in0=z[:, :], scalar1=-1.0, scalar2=None,
                                                               op0=ALU.mult), reads=(zr,), writes=(zr,))
                    if t % 4 == 2:
                        P.op("dve", lambda E, t=t: E.tensor_copy(out=Rbc[:, t // 4, :], in_=tot[:, :]),
                             reads=(totr,), writes=(Rr,))
                    bc = self.banks[5]
                    bcr = self.bank_res[5]
                    bt = self.banks[6]
                    btr = self.bank_res[6]
                    P.op("pe", lambda E, z=z, bc=bc: E.matmul(bc[:, 0:16], lhsT=self.triu_f[:, :], rhs=z[:, :], start=True, stop=True),
                         reads=(zr,), writes=(bcr,))
                    P.op("pe", lambda E, z=z, bt=bt: E.matmul(bt[:, 0:16], lhsT=self.ones_f[:, :], rhs=z[:, :], start=True, stop=True),
                         reads=(zr,), writes=(btr,))
                    P.op("dve", lambda E, t=t, bc=bc: E.tensor_tensor(out=Floc[:, t, :], in0=bc[:, 0:16], in1=tot[:, :], op=ALU.add),
                         reads=(bcr, totr), writes=(Flr,))
                    P.op("dve", lambda E, bt=bt: E.tensor_tensor(out=tot[:, :], in0=bt[:, 0:16], in1=tot[:, :], op=ALU.add),
                         reads=(btr, totr), writes=(totr,))
                P.dma("sp", o["Fx"][:, 0:256], Floc[:, :, :].rearrange("p a b -> p (a b)"), reads=(Flr,),
                      writes=(self.fox_res_f,), nowaw=True)
                P.dma("sp", o["Fx"][:, 256:272], tot[:, :], reads=(totr,), writes=(self.fox_res_f,), nowaw=True)
                P.dma("sp", o["R"][:, :], Rbc[:, :, :].rearrange("p a b -> p (a b)"), reads=(Rr,),
                      writes=(self.fox_res_f,), nowaw=True)
                P.coll(o["Fx"].opt(), o["Fg"].opt(), groups, reads=(self.fox_res_f,), writes=(self.fox_prev_f,))
                wb = [self.sb(es2, f"fx_w{i}", [128, KC * 128], BF16) for i in range(2)]
                wr = RL(2)
                ost = [self.sb(es2, f"fx_o{i}", [128, TB], BF16) for i in range(3)]
                osr = RL(3)
                wv = [self.sb(es2, f"fx_wv{i}", [128, KC * 512], BF16) for i in range(2)]
                wvr = RL(2)
                vst = [self.sb(es2, f"fx_vs{i}", [128, 512], BF16) for i in range(3)]
                vsr = RL(3)
                cnt = {"it": 0, "ip": 0, "iv": 0}

                def qk_panel(src, j, isq, scale):
                    w = wb[cnt["ip"] % 2]
                    wres = wr[cnt["ip"] % 2]
                    cnt["ip"] += 1
                    self.load_panel(w[:, :], wres, src, j, KC * 128)
                    for tb in range(NB):
                        it = cnt["it"]
                        cnt["it"] += 1
                        bank = self.banks[it % 4]
                        br = self.bank_res[it % 4]
                        ob = ost[it % 3]
                        obr = osr[it % 3]

                        def mm(E, w=w, tb=tb, bank=bank):
                            ins = None
                            for kc in range(KC):
                                ins = E.matmul(bank[:, :], lhsT=w[:, kc * 128:(kc + 1) * 128],
                                               rhs=hT[:, kc, tb * TB:(tb + 1) * TB],
                                               start=(kc == 0), stop=(kc == KC - 1))
                            return ins
                        P.op("pe", mm, reads=(wres, hres[tb]), writes=(br,))
                        P.op("act", lambda E, ob=ob, bank=bank, scale=scale: E.activation(
                            out=ob[:, :], in_=bank[:, :], func=AF.Copy, scale=scale), reads=(br,), writes=(obr,))
                        if isq:
                            P.dma("sp", o["QT"][j * 128:(j + 1) * 128, tb * TB:(tb + 1) * TB], ob[:, :],
                                  reads=(obr,), writes=(self.fox_res_q,), nowaw=True)
                        else:
                            P.dma("sp", o["KV"][j][0:128, tb * TB:(tb + 1) * TB], ob[:, :],
                                  reads=(obr,), writes=(self.fox_res_kv[j],), nowaw=True)

                for j in range(KC):
                    qk_panel(self.dram["fq"], j, True, 128.0 ** -0.5)
                for pn in range(4):
                    for hh in range(4):
                        qk_panel(self.dram["fk"], pn * 4 + hh, False, 1.0)
                    w = wv[pn % 2]
                    self.load_panel(w[:, :], wvr[pn % 2], self.dram["fv"], pn, KC * 512)
                    for t in range(16):
                        it = cnt["it"]
                        cnt["it"] += 1
                        bank = self.banks[it % 4]
                        br = self.bank_res[it % 4]
                        vs = vst[it % 3]
                        vr_ = vsr[it % 3]

                        def mm(E, w=w, t=t, bank=bank):
                            ins = None
                            for kc in range(KC):
                                ins = E.matmul(bank[:, :], lhsT=hT[:, kc, t * 128:(t + 1) * 128],
                                               rhs=w[:, kc * 512:(kc + 1) * 512], start=(kc == 0), stop=(kc == KC - 1))
                            return ins
                        P.op("pe", mm, reads=(wvr[pn % 2], hres[t // 4]), writes=(br,))
                        if it % 2 == 0:
                            P.op("act", lambda E, vs=vs, bank=bank: E.copy(out=vs[:, :], in_=bank[:, :]),
                                 reads=(br,), writes=(vr_,))
                        else:
                            P.op("dve", lambda E, vs=vs, bank=bank: E.tensor_copy(out=vs[:, :], in_=bank[:, :]),
                                 reads=(br,), writes=(vr_,))
                        for hh in range(4):
                            hd = pn * 4 + hh
                            P.dma("sp", o["KV"][hd][128:256, t * 128:(t + 1) * 128],
                                  vs[:, hh * 128:(hh + 1) * 128], reads=(vr_,), writes=(self.fox_res_kv[hd],), nowaw=True)
                    for hh in range(4):
                        hd = pn * 4 + hh
                        P.coll(o["KV"][hd].opt(), o["KVg"][hd].opt(), groups, reads=(self.fox_res_kv[hd],),
                               writes=(self.fox_prev_kv[hd],))
                P.barrier_compute()

    def phase_fox_b(self, scratch):
        P = self.P
        o = self.fox_own
        with ExitStack() as es:
            aT = self.sb(es, "fb_aT", [128, KC, TOK], BF16)
            ares = RL(NB)
            with ExitStack() as es2:
                Fall = self.sb(es2, "fb_Fall", [128, 32, 16], F32)
                Fr = Res()
                Ftp = self.sb(es2, "fb_Ftp", [128, 16], F32)
                Rbc = self.sb(es2, "fb_Rbc", [128, 4, 16], F32)
                negF = self.sb(es2, "fb_negF", [128, 16, 32], F32)
                nFr = Res()
                P.dma("sp", Fall[:, 0:16, :].rearrange("p a b -> p (a b)"), o["Fg"][0:128, 0:256], reads=(self.fox_prev_f,), writes=(Fr,))
                P.dma("sp", Fall[:, 16:32, :].rearrange("p a b -> p (a b)"), o["Fx"][:, 0:256], reads=(self.fox_res_f,), writes=(Fr,), nowaw=True)
                P.dma("sp", Ftp[:, :], o["Fg"][0:128, 256:272], reads=(self.fox_prev_f,), writes=(Fr,), nowaw=True)
                P.dma("sp", Rbc[:, :, :].rearrange("p a b -> p (a b)"), o["R"][:, :], reads=(self.fox_res_f,), writes=(Fr,), nowaw=True)
                P.op("dve", lambda E: E.tensor_scalar(out=negF[:, :, :], in0=Fall[:, :, :].rearrange("p k h -> p h k"),
                                                      scalar1=-1.0, scalar2=None, op0=ALU.mult), reads=(Fr,), writes=(nFr,))
                for h in range(16):
                    P.op("dve", lambda E, h=h: E.tensor_scalar(out=negF[:, h, 0:16], in0=negF[:, h, 0:16],
                                                               scalar1=Ftp[:, h:h + 1], scalar2=self.par("prevmask"),
                                                               op0=ALU.add, op1=ALU.add), reads=(Fr, nFr), writes=(nFr,))
                KTh = [self.sb(es2, f"fb_K{i}", [128, 2 * TOK], BF16) for i in range(2)]
                Vh = [self.sb(es2, f"fb_V{i}", [128, 2 * TOK], BF16) for i in range(2)]
                QTh = [self.sb(es2, f"fb_Q{i}", [128, TOK], BF16) for i in range(2)]
                hr = RL(2)
                biasq = [self.sb(es2, f"fb_b{i}", [128, 32], F32) for i in range(2)]
                bqr = RL(2)
                NPB = 6
                pb = [self.sb(es2, f"fb_p{i}", [128, TB], BF16) for i in range(NPB)]
                pbr = RL(NPB)
                rinv = [self.sb(es2, f"fb_ri{i}", [128, TB], F32) for i in range(2)]
                rir = RL(2)
                psum2 = [self.sb(es2, f"fb_p2{i}", [128, TB], BF16) for i in range(2)]
                psum2r = RL(2)

                def load_head(h):
                    K_, V_, Q_, hres_ = KTh[h % 2], Vh[h % 2], QTh[h % 2], hr[h % 2]
                    rows = slice(h * 128, (h + 1) * 128)
                    P.dma("sp", K_[:, 0:TOK], o["KVg"][h][0:128, :], reads=(self.fox_prev_kv[h],), writes=(hres_,))
                    P.dma("sp", K_[:, TOK:2 * TOK], o["KV"][h][0:128, :], reads=(self.fox_res_kv[h],), writes=(hres_,), nowaw=True)
                    P.dma("sp", V_[:, 0:TOK], o["KVg"][h][128:256, :], reads=(self.fox_prev_kv[h],), writes=(hres_,), nowaw=True)
                    P.dma("sp", V_[:, TOK:2 * TOK], o["KV"][h][128:256, :], reads=(self.fox_res_kv[h],), writes=(hres_,), nowaw=True)
                    P.dma("sp", Q_[:, :], o["QT"][rows, :], reads=(self.fox_res_q,), writes=(hres_,), nowaw=True)

                units = []
                g = 0
                for h in range(16):
                    for qb in range(NB):
                        nk = 16 + qb * 4 + 4
                        for kt0 in range(0, nk, 2):
                            kts = []
                            for kt in (kt0, kt0 + 1):
                                j = kt - 16 - qb * 4
                                kts.append((kt, 128 * j if j > 0 else 0, j >= 0))
                            units.append((h, qb, kts, kt0 == 0, kt0 + 2 == nk, g))
                        g += 1
                NU = len(units)
                LA = 1
                bank8 = self.bank_bf[:, :].bitcast(F32)
                sbanks = [(self.banks[0], self.bank_res[0]), (self.banks[1], self.bank_res[1]),
                          (self.banks[2], self.bank_res[2]), (self.banks[3], self.bank_res[3])]
                obanks = [(self.banks[4], self.bank_res[4]), (self.banks[5], self.bank_res[5])]
                lbanks = [(self.banks[6], self.bank_res[6]), (bank8, self.bank_bf_res)]

                def emit_S(u):
                    h, qb, kts, first, last, g = units[u]
                    K_, Q_, hres_ = KTh[h % 2], QTh[h % 2], hr[h % 2]

                    def mm(E):
                        ins = None
                        for i, (kt, off, diag) in enumerate(kts):
                            bs = sbanks[(u % 2) * 2 + i][0]
                            ins = E.matmul(bs[:, off:TB], lhsT=K_[:, kt * 128:(kt + 1) * 128],
                                           rhs=Q_[:, qb * TB + off:(qb + 1) * TB], start=True, stop=True)
                        return ins
                    P.op("pe", mm, reads=(hres_,), writes=[sbanks[(u % 2) * 2 + i][1] for i in range(2)])

                def emit_bias(g):
                    h, qb = g // NB, g % NB
                    bq_ = biasq[g % 2]
                    P.op("dve", lambda E: E.tensor_scalar(
                        out=bq_[:, :], in0=negF[:, h, :], scalar1=Rbc[:, qb, h:h + 1], scalar2=None, op0=ALU.add),
                        reads=(nFr, Fr), writes=(bqr[g % 2],))

                pendL = []
                pendF = []
                load_head(0)
                load_head(1)
                for u in range(min(LA, NU)):
                    emit_S(u)
                for u in range(NU):
                    h, qb, kts, first, last, g = units[u]
                    V_, hres_ = Vh[h % 2], hr[h % 2]
                    bq_ = biasq[g % 2]
                    bqr_ = bqr[g % 2]
                    bo, bor = obanks[g % 2]
                    bl, blr = lbanks[g % 2]
                    ps = [pb[(u % 3) * 2 + i] for i in range(2)]
                    prs = [pbr[(u % 3) * 2 + i] for i in range(2)]
                    sb_ = [sbanks[(u % 2) * 2 + i] for i in range(2)]
                    if first:
                        if qb == 0 and h >= 1 and h + 1 < 16:
                            load_head(h + 1)
                        if g == 0:
                            emit_bias(0)
                        if g + 1 < 64:
                            emit_bias(g + 1)

                    def ex(E, kts=kts, ps=ps, sb_=sb_, bq_=bq_):
                        ins = None
                        for i, (kt, off, diag) in enumerate(kts):
                            ins = E.activation(out=ps[i][:, off:TB], in_=sb_[i][0][:, off:TB], func=AF.Exp,
                                               bias=bq_[:, kt:kt + 1], scale=1.0)
                        return ins
                    P.op("act", ex, reads=(sb_[0][1], sb_[1][1], bqr_), writes=prs)
                    if any(d for _, _, d in kts):
                        def mk(E, kts=kts, ps=ps):
                            ins = None
                            for i, (kt, off, diag) in enumerate(kts):
                                if diag:
                                    ins = E.tensor_tensor(out=ps[i][:, off:off + 128], in0=ps[i][:, off:off + 128],
                                                          in1=self.triu_b[:, :], op=ALU.mult)
                            return ins
                        P.op("dve", mk, reads=prs, writes=prs)

                    same = kts[0][1] == kts[1][1] and not kts[0][2] and not kts[1][2]
                    if same:
                        p2 = psum2[u % 2]
                        p2r = psum2r[u % 2]
                        P.op("dve", lambda E, ps=ps, p2=p2: E.tensor_tensor(out=p2[:, :], in0=ps[0][:, :], in1=ps[1][:, :],
                                                                            op=ALU.add), reads=prs, writes=(p2r,))

                    def mmo(E, V_=V_, kts=kts, ps=ps, bo=bo, first=first, last=last):
                        ins = None
                        for i, (kt, off, diag) in enumerate(kts):
                            ins = E.matmul(bo[:, off:TB], lhsT=V_[:, kt * 128:(kt + 1) * 128], rhs=ps[i][:, off:TB],
                                           start=(first and i == 0), stop=(last and i == 1))
                        return ins
                    if u + LA < NU:
                        emit_S(u + LA)
                    P.op("pe", mmo, reads=[hres_] + prs, writes=(bor,))
                    if pendL:
                        pendL.pop()()
                    if same:
                        pendL.append(lambda p2=p2, p2r=p2r, bl=bl, blr=blr, first=first, last=last: P.op(
                            "pe", lambda E: E.matmul(bl[:, :], lhsT=self.ones_bf[:, :], rhs=p2[:, :], start=first, stop=last),
                            reads=(p2r,), writes=(blr,)))
                    else:
                        def mml(E, kts=kts, ps=ps, bl=bl, first=first, last=last):
                            ins = None
                            for i, (kt, off, diag) in enumerate(kts):
                                ins = E.matmul(bl[:, off:TB], lhsT=self.ones_bf[:, :], rhs=ps[i][:, off:TB],
                                               start=(first and i == 0), stop=(last and i == 1))
                            return ins
                        pendL.append(lambda mml=mml, prs=prs, blr=blr: P.op("pe", mml, reads=prs, writes=(blr,)))
                    for fu, fn in list(pendF):
                        if fu <= u:
                            pendF.remove((fu, fn))
                            fn()
                    if last:
                        ri = rinv[g % 2]
                        rr_ = rir[g % 2]

                        def fin(h=h, qb=qb, bo=bo, bor=bor, bl=bl, blr=blr, ri=ri, rr_=rr_):
                            P.op("dve", lambda E: E.reciprocal(out=ri[:, :], in_=bl[:, :]), reads=(blr,), writes=(rr_,))
                            P.op("dve", lambda E: E.tensor_tensor(
                                out=aT[:, h, qb * TB:(qb + 1) * TB], in0=bo[:, :], in1=ri[:, :], op=ALU.mult),
                                reads=(bor, rr_), writes=(ares[qb],))
                        pendF.append((u + 4, fin))
                while pendL:
                    pendL.pop()()
                for fu, fn in pendF:
                    fn()
                self.proj_residual(es2, self.dram["fo"], KC, lambda kc, tb: aT[:, kc, tb * TB:(tb + 1) * TB],
                                   lambda kc, tb: ares[tb], "fb_o")
                P.barrier()

    def phase_final(self):
        P = self.P
        out = self.dram["out"]
        with ExitStack() as es:
            stage = [self.sb(es, f"o_st{i}", [128, KC, TB], F32) for i in range(2)]
            sres = RL(2)
            yT = [self.sb(es, f"o_y{i}", [128, KC, TB], F32) for i in range(2)]
            yres = RL(2)
            ot = [self.sb(es, f"o_t{i}", [128, D], F32) for i in range(2)]
            otr = RL(2)
            for tb in range(NB):
                st = stage[tb % 2]
                y = yT[tb % 2]
                self.load_xt_block(st, sres[tb % 2], tb * TB, TB, [self.xt_res[kc][tb] for kc in range(KC)])
                self.norm_block(lambda kc, st=st: st[:, kc, :], sres[tb % 2], "final_norm",
                                lambda kc, y=y: y[:, kc, :], yres[tb % 2], TB,
                                self.banks[4 + tb % 2], self.bank_res[4 + tb % 2])
                for tt in range(4):
                    t = tb * 4 + tt
                    o = ot[t % 2]
                    for q in range(4):
                        bank = self.banks[q]

                        def tr(E, y=y, q=q, tt=tt, bank=bank):
                            ins = None
                            for i in range(4):
                                kc = q * 4 + i
                                ins = E.transpose(out=bank[:, i * 128:(i + 1) * 128],
                                                  in_=y[:, kc, tt * 128:(tt + 1) * 128], identity=self.ident_f[:, :])
                            return ins
                        P.op("pe", tr, reads=(yres[tb % 2],), writes=(self.bank_res[q],))
                        if q % 2 == 0:
                            P.op("dve", lambda E, o=o, q=q, bank=bank: E.tensor_copy(
                                out=o[:, q * 512:(q + 1) * 512], in_=bank[:, :]),
                                reads=(self.bank_res[q],), writes=(otr[t % 2],))
                        else:
                            P.op("act", lambda E, o=o, q=q, bank=bank: E.copy(
                                out=o[:, q * 512:(q + 1) * 512], in_=bank[:, :]),
                                reads=(self.bank_res[q],), writes=(otr[t % 2],))
                    P.dma("sp", out[t * 128:(t + 1) * 128, :], o[:, :], reads=(otr[t % 2],), writes=(self.out_res,))
        P.barrier()

    def build(self, stages, ext_in=(), ext_out=()):
        nc = self.nc
        P = self.P

        def scratch(name, shape, dt=F32):
            if name in ext_in:
                return self.din(name, shape, dt)
            if name in ext_out:
                return self.dout(name, shape, dt)
            return self.dint(name, shape, dt)
        self.din("params", [128, NPAR])
        self.din("ident", [128, 128])
        self.din("triu", [128, 128])
        if "l0" in stages:
            self.din("x_own", [TOK, D])
            self.din("x_prev", [TOK, D])
            self.din("w_fi", [8 * 128, KC * 256])
            self.din("w_u", [2 * 128, KC * 512])
            self.din("w_h", [8 * 128, KC * 512])
            self.din("w_pool", [128, 2048])
            self.din("wout_a", [KC * 128, 8 * 128])
            self.din("wout_b", [KC * 128, 8 * 128])
        for l in range(2):
            if f"xa{l}" in stages:
                self.din("mem", [NMEM, D]) if "mem" not in self.dram else None
                self.din(f"wkvK{l}", [KC * 128, KC * 128])
                self.din(f"wkvV{l}", [4 * 128, KC * 512])
                self.din(f"wq{l}", [KC * 128, KC * 128])
                self.din(f"wo{l}", [KC * 128, KC * 128])
            if f"ffn{l}" in stages:
                self.din(f"wgu{l}", [FC * 128, KC * 256])
                self.din(f"wd{l}", [KC * 128, FC * 128])
        if "fox_a" in stages:
            self.din("fq", [KC * 128, KC * 128])
            self.din("fk", [KC * 128, KC * 128])
            self.din("fv", [4 * 128, KC * 512])
            self.din("ffl", [128, KC * 16])
        if "fox_b" in stages:
            self.din("fo", [KC * 128, KC * 128])
        if "XT_in" in ext_in:
            self.din("XT_in", [D, TOK])
        scratch("XT", [D, TOK])
        if "final" in stages:
            self.dout("out", [TOK, D])
        self.xt_res = [RL(NB) for _ in range(KC)]
        self.out_res = Res()

        with ExitStack() as es:
            for nm in P.semnames:
                P.sems[nm] = es.enter_context(nc.semaphore(f"s_{nm}"))
            self.banks = [es.enter_context(nc.psum_tensor(f"bank{i}", [128, 512], F32)) for i in range(7)]
            self.bank_res = RL(7)
            self.bank_bf = es.enter_context(nc.psum_tensor("bankbf", [128, 1024], BF16))
            self.bank_bf_res = Res()
            self.bank_u_res = Res()
            self.params = self.sb(es, "params", [128, NPAR], F32)
            self.ident_f = self.sb(es, "ident_f", [128, 128], F32)
            self.ident_b = self.sb(es, "ident_b", [128, 128], BF16)
            self.triu_f = self.sb(es, "triu_f", [128, 128], F32)
            self.triu_b = self.sb(es, "triu_b", [128, 128], BF16)
            self.triu4 = self.sb(es, "triu4", [128, 4, 128], F32)
            self.ones_bf = self.sb(es, "ones_bf", [128, 128], BF16)
            self.ones_f = self.sb(es, "ones_f", [128, 128], F32)
            self.eps_col = self.sb(es, "eps_col", [128, 1], F32)
            self.sq = [self.sb(es, f"sq{i}", [128, TB], BF16) for i in range(4)]
            self.sq_res = RL(4)
            self.rstd = self.sb(es, "rstd", [128, TB], F32)
            self.rstd_res = Res()
            cres = Res()
            P.dma("sp", self.params[:, :], self.dram["params"][:, :], writes=(cres,))
            P.dma("sp", self.ident_f[:, :], self.dram["ident"][:, :], writes=(cres,))
            P.dma("pool", self.ident_b[:, :], self.dram["ident"][:, :], writes=(cres,))
            P.dma("sp", self.triu_f[:, :], self.dram["triu"][:, :], writes=(cres,))
            P.dma("pool", self.triu_b[:, :], self.dram["triu"][:, :], writes=(cres,))
            for i4 in range(4):
                P.dma("sp", self.triu4[:, i4, :], self.dram["triu"][:, :], writes=(cres,), nowaw=True)
            P.op("dve", lambda E: E.memset(self.ones_bf[:, :], 1.0), writes=(cres,))
            P.op("dve", lambda E: E.memset(self.ones_f[:, :], 1.0), writes=(cres,))
            P.op("dve", lambda E: E.memset(self.eps_col[:, :], EPS), writes=(cres,))
            if "XT_in" in ext_in:
                P.dma("sp", self.dram["XT"][:, :], self.dram["XT_in"][:, :],
                      writes=[r for row in self.xt_res for r in row])
            P.barrier()

            for st in stages:
                P.scope = st
                if st == "l0":
                    self.phase_l0()
                elif st.startswith("xa"):
                    self.phase_xattn(int(st[2:]))
                elif st.startswith("ffn"):
                    self.phase_ffn(int(st[3:]))
                elif st == "fox_a":
                    self.phase_fox_a(scratch)
                elif st == "fox_b":
                    self.phase_fox_b(scratch)
                elif st == "final":
                    self.phase_final()
                else:
                    raise ValueError(st)
            with nc.Block() as block:
                P.emit(block)
        return nc


def prep_inputs(inputs):
    common = {}
    par = np.zeros((128, NPAR), np.float32)

    def put(name, arr2d):
        o, n = PCOL[name]
        par[:, o:o + n] = arr2d
    put("ev_norm", colvec(inputs["ev_norm"][0]))
    put("xa_norm0", colvec(inputs["xa_norm"][0]))
    put("xa_norm1", colvec(inputs["xa_norm"][1]))
    put("ffn_norm0", colvec(inputs["ffn_norm"][0]))
    put("ffn_norm1", colvec(inputs["ffn_norm"][1]))
    put("od_norm", colvec(inputs["od_norm"][0]))
    put("final_norm", colvec(inputs["final_norm"]))
    put("mem_norm0", colvec(inputs["xa_mem_norm"][0]))
    put("mem_norm1", colvec(inputs["xa_mem_norm"][1]))
    put("pool_scale", colvec(inputs["ev_pool_scale"][0]))
    put("hg_norm", colvec(inputs["ev_hg_norm"][0]))
    for i in range(3):
        put(f"lb{i}", colvec(inputs["lb_table"][i]))
    put("b_f", np.tile(inputs["od_b_f"][0][None, :], (128, 1)))
    put("iota16", np.tile(np.arange(1, 17, dtype=np.float32)[None, :], (128, 1)))
    common["params"] = par
    common["ident"] = np.eye(128, dtype=np.float32)
    common["triu"] = np.triu(np.ones((128, 128), np.float32))
    W = inputs["ev_w_in"][0]
    common["w_u"] = panelize(W[:, 0:1024], 512)
    q, f, i_, g = (W[:, 1024 + k * 1024:1024 + (k + 1) * 1024].reshape(D, 8, 128) for k in range(4))
    common["w_h"] = panelize(np.stack([q, f, i_, g], axis=2).reshape(D, 8 * 512), 512)
    common["w_fi"] = panelize(np.stack([f, i_], axis=2).reshape(D, 8 * 256), 256)
    common["w_pool"] = np.ascontiguousarray(
        inputs["ev_w_pool"][0].reshape(4, 2, 128, 256).transpose(2, 0, 1, 3).reshape(128, 2048))
    wo = inputs["ev_w_out"][0]
    common["wout_a"] = panelize(wo[0:1024], 128)
    common["wout_b"] = panelize(wo[1024:2048], 128)
    for l in range(2):
        wkv = inputs["xa_wkv"][l]
        common[f"wkvK{l}"] = panelize(wkv[:, 0:D], 128)
        common[f"wkvV{l}"] = panelize(wkv[:, D:2 * D], 512)
        common[f"wq{l}"] = panelize(inputs["xa_wq"][l], 128)
        common[f"wo{l}"] = panelize(inputs["xa_wo"][l], 128)
        wg = inputs["ffn_w_gate"][l]
        wu = inputs["ffn_w_up"][l]
        wgu = np.concatenate([wg.reshape(D, FC, 1, 128), wu.reshape(D, FC, 1, 128)], axis=2).reshape(D, FC * 256)
        common[f"wgu{l}"] = panelize(wgu, 256)
        common[f"wd{l}"] = panelize(inputs["ffn_w_down"][l], 128)
    W1 = inputs["od_w_in"][0]
    common["fq"] = panelize(W1[:, 0:D], 128)
    common["fk"] = panelize(W1[:, D:2 * D], 128)
    common["fv"] = panelize(W1[:, 2 * D:3 * D], 512)
    common["ffl"] = panelize(W1[:, 3 * D:3 * D + 16], 16)
    common["fo"] = panelize(inputs["od_w_out"][0], 128)
    return common


def core_inputs(inputs, common, c):
    b, half = c // 2, c % 2
    m = dict(common)
    par = common["params"].copy()
    par[:, PCOL["tok0"][0]] = half * TOK
    par[:, PCOL["prevmask"][0]] = 0.0 if half == 1 else -30000.0
    m["params"] = par
    x = inputs["x"]
    m["x_own"] = np.ascontiguousarray(x[b, half * TOK:(half + 1) * TOK])
    m["x_prev"] = np.ascontiguousarray(x[b, 0:TOK]) if half == 1 else np.zeros((TOK, D), np.float32)
    m["mem"] = np.ascontiguousarray(inputs["mem"][b])
    return m


STAGES = ["l0", "xa0", "ffn0", "fox_a", "fox_b", "xa1", "ffn1", "final"]


def kernel(**inputs):
    inputs = {k: np.asarray(v) for k, v in inputs.items()}
    common = prep_inputs(inputs)
    maps = [core_inputs(inputs, common, c) for c in range(NCORES)]
    b = Builder(0, 0)
    nc = b.build(STAGES)
    names = set(b.dram.keys())
    in_maps = [{k: v for k, v in m.items() if k in names} for m in maps]
    res = run_bass_kernel_spmd(nc, in_maps, core_ids=list(range(NCORES))).results
    x = inputs["x"]
    out = np.zeros(x.shape, np.float32)
    for c in range(NCORES):
        bb, half = c // 2, c % 2
        out[bb, half * TOK:(half + 1) * TOK] = res[c]["out"]
    return out
```
